# Optimizing a Trainium2 kernel written in Bass

```python
import jax, jax.numpy as jnp
from jax import lax
import numpy as np

D_MODEL = 1024
BATCH = 16
SEQ = 256
DEPTH = 2
DEC_BATCH = 8
DEC_SEQ = 4096
PAST_LEN = 256

GRID_W = 64
EPS = 1e-6
A_WIDTH = D_MODEL // 2
A_HEADS = 4
A_DV = A_WIDTH // A_HEADS
A_KWIDTH = A_WIDTH // 2
A_DK = A_KWIDTH // A_HEADS
A_RANK = 16
A_TAU = 16.0
A_CHUNK = 64
B_WIDTH = D_MODEL // 4
B_KSIZE = 31
C_WIDTH = D_MODEL // 4
C_HEADS = 4
C_HEAD_DIM = C_WIDTH // C_HEADS
C_CHUNK = 128
MIX_WIDTH = A_WIDTH + B_WIDTH + C_WIDTH
D_FF = 4 * D_MODEL
N_MOD = 6
IN_SPLITS = (
    A_KWIDTH,
    2 * A_KWIDTH,
    2 * A_KWIDTH + A_WIDTH,
    2 * A_KWIDTH + 2 * A_WIDTH,
    2 * A_KWIDTH + 2 * A_WIDTH + 2 * A_RANK,
    2 * A_KWIDTH + 2 * A_WIDTH + 2 * A_RANK + 2 * B_WIDTH,
)
IN_WIDTH = 2 * A_KWIDTH + 2 * A_WIDTH + 2 * A_RANK + 2 * B_WIDTH + 2 * C_WIDTH

kernel_name = "hybrid_gla_conformer_gmlp_diffusion_step"


def rmsnorm(x, g):
    xf = x.astype(jnp.float32)
    y = xf * lax.rsqrt(jnp.mean(xf * xf, axis=-1, keepdims=True) + EPS)
    return (y * g.astype(jnp.float32)).astype(x.dtype)


def layernorm(x, g, b):
    xf = x.astype(jnp.float32)
    mu = jnp.mean(xf, axis=-1, keepdims=True)
    xc = xf - mu
    var = jnp.mean(xc * xc, axis=-1, keepdims=True)
    y = xc * lax.rsqrt(var + EPS) * g.astype(jnp.float32) + b.astype(jnp.float32)
    return y.astype(x.dtype)


def adaln(cond, w_mod, b_mod):
    mod = jax.nn.silu(cond) @ w_mod + b_mod
    mod = mod.reshape(mod.shape[:-1] + (N_MOD, D_MODEL))
    return tuple(mod[..., i, None, :] for i in range(N_MOD))


def gla_scan(q, k, v, log_a, s0):
    bsz, t = q.shape[:2]
    n = t // A_CHUNK

    def chunks(a):
        return a.reshape((bsz, n, A_CHUNK) + a.shape[2:])

    q, k, v, log_a = chunks(q), chunks(k), chunks(v), chunks(log_a)
    b = jnp.cumsum(log_a, axis=2)
    b_last = b[:, :, -1:]
    q_dec = q * jnp.exp(b)
    k_dec = k * jnp.exp(-b)
    k_end = k * jnp.exp(b_last - b)
    mask = jnp.tril(jnp.ones((A_CHUNK, A_CHUNK), dtype=bool))
    att = jnp.einsum("bnthd,bnshd->bnhts", q_dec, k_dec)
    att = jnp.where(mask, att, 0.0)
    o_intra = jnp.einsum("bnhts,bnshv->bnthv", att, v)
    ds = jnp.einsum("bnshd,bnshv->bnhdv", k_end, v)
    decay = jnp.exp(b_last[:, :, 0])

    def step(s, inp):
        d, dsn = inp
        return d[..., None] * s + dsn, s

    s_final, s_in = lax.scan(step, s0, (jnp.moveaxis(decay, 1, 0), jnp.moveaxis(ds, 1, 0)))
    s_in = jnp.moveaxis(s_in, 0, 1)
    o_inter = jnp.einsum("bnthd,bnhdv->bnthv", q_dec, s_in)
    o = (o_intra + o_inter).reshape(bsz, t, A_HEADS, A_DV)
    return o, s_final


def gla_group(q, k, v, g, lr, w_a_gate, b_a_gate, a_norm_g, s0_fwd, s0_bwd):
    bsz, t = q.shape[:2]
    f32 = jnp.float32
    qh = q.astype(f32).reshape(bsz, t, A_HEADS, A_DK) * (A_DK ** -0.5)
    kh = k.astype(f32).reshape(bsz, t, A_HEADS, A_DK)
    vh = v.astype(f32).reshape(bsz, t, A_HEADS, A_DV)
    logits = jnp.einsum("btzr,zrk->btzk", lr.reshape(bsz, t, 2, A_RANK), w_a_gate) + b_a_gate
    log_a = (jax.nn.log_sigmoid(logits.astype(f32)) / A_TAU).reshape(bsz, t, 2, A_HEADS, A_DK)
    o_f, s_f = gla_scan(qh, kh, vh, log_a[:, :, 0], s0_fwd)
    o_b, s_b = gla_scan(qh[:, ::-1], kh[:, ::-1], vh[:, ::-1], log_a[:, ::-1, 1], s0_bwd)
    o = rmsnorm(o_f + o_b[:, ::-1], a_norm_g).astype(g.dtype)
    o = o.reshape(bsz, t, A_WIDTH) * jax.nn.silu(g)
    return o, s_f, s_b


def conv_group(glu, w_dw, ln_g, ln_b, grid):
    bsz, t = glu.shape[:2]
    a, b = jnp.split(glu, 2, axis=-1)
    xc = a * jax.nn.sigmoid(b)
    if grid:
        rows = t // GRID_W
        xc = xc.reshape(bsz * rows, GRID_W, B_WIDTH)
    y = lax.conv_general_dilated(
        xc, w_dw, window_strides=(1,), padding=[(B_KSIZE // 2, B_KSIZE // 2)],
        dimension_numbers=("NWC", "WIO", "NWC"), feature_group_count=B_WIDTH)
    y = y.reshape(bsz, t, B_WIDTH)
    return jax.nn.silu(layernorm(y, ln_g, ln_b))


def gmlp_group(uv, ln_g, ln_b, w_s, b_s):
    bsz, t = uv.shape[:2]
    u, v = jnp.split(jax.nn.gelu(uv), 2, axis=-1)
    v = layernorm(v, ln_g, ln_b).reshape(bsz, t // C_CHUNK, C_CHUNK, C_HEADS, C_HEAD_DIM)
    sv = jnp.einsum("hpq,bnqhd->bnphd", w_s, v) + b_s.T[:, :, None]
    return u * sv.reshape(bsz, t, C_WIDTH)


def layer(x, mod, lp, s0_fwd, s0_bwd, grid):
    (norm1_g, w_in, w_a_gate, b_a_gate, a_norm_g, w_dw, b_ln_g, b_ln_b,
     c_ln_g, c_ln_b, w_s, b_s, w_out, norm2_g, w_ff1, w_ff2) = lp
    shift1, scale1, gate1, shift2, scale2, gate2 = mod
    h = rmsnorm(x, norm1_g) * (1 + scale1) + shift1
    z = h @ w_in
    q, k, v, g, lr, glu, uv = jnp.split(z, IN_SPLITS, axis=-1)
    o_a, s_f, s_b = gla_group(q, k, v, g, lr, w_a_gate, b_a_gate, a_norm_g, s0_fwd, s0_bwd)
    o_b = conv_group(glu, w_dw, b_ln_g, b_ln_b, grid)
    o_c = gmlp_group(uv, c_ln_g, c_ln_b, w_s, b_s)
    mix = jnp.concatenate([o_a, o_b.astype(o_a.dtype), o_c.astype(o_a.dtype)], axis=-1) @ w_out
    x = x + gate1 * mix
    h = rmsnorm(x, norm2_g) * (1 + scale2) + shift2
    x = x + gate2 * (jnp.square(jax.nn.relu(h @ w_ff1)) @ w_ff2)
    return x, s_f, s_b


def setup_inputs(seed: int = 0) -> dict:
    key = jax.random.key(seed)
    ks = jax.random.split(key, 32)
    f32 = jnp.float32
    L = DEPTH

    def nrm(k, shape, scale):
        return jax.random.normal(k, shape, f32) * scale

    return {
        "x_prompt": nrm(ks[0], (BATCH, SEQ, D_MODEL), 1.0),
        "x_sample": nrm(ks[1], (DEC_BATCH, DEC_SEQ, D_MODEL), 1.0),
        "c": nrm(ks[2], (DEC_BATCH, D_MODEL), 1.0),
        "state_gla": nrm(ks[3], (DEC_BATCH, DEPTH, 2, A_HEADS, A_DK, A_DV), 1.0),
        "c_ctx": nrm(ks[4], (D_MODEL,), 1.0),
        "w_mod": nrm(ks[5], (L, D_MODEL, N_MOD * D_MODEL), 0.5 * D_MODEL ** -0.5),
        "b_mod": nrm(ks[6], (L, N_MOD * D_MODEL), 0.05),
        "norm1_g": 1.0 + nrm(ks[7], (L, D_MODEL), 0.05),
        "w_in": nrm(ks[8], (L, D_MODEL, IN_WIDTH), D_MODEL ** -0.5),
        "w_a_gate": nrm(ks[9], (L, 2, A_RANK, A_KWIDTH), A_RANK ** -0.5),
        "b_a_gate": nrm(ks[10], (L, 2, A_KWIDTH), 0.5),
        "a_norm_g": 1.0 + nrm(ks[11], (L, A_DV), 0.05),
        "w_dw": nrm(ks[12], (L, B_KSIZE, 1, B_WIDTH), B_KSIZE ** -0.5),
        "b_ln_g": 1.0 + nrm(ks[13], (L, B_WIDTH), 0.05),
        "b_ln_b": nrm(ks[14], (L, B_WIDTH), 0.05),
        "c_ln_g": 1.0 + nrm(ks[15], (L, C_WIDTH), 0.05),
        "c_ln_b": nrm(ks[16], (L, C_WIDTH), 0.05),
        "w_s": nrm(ks[17], (L, C_HEADS, C_CHUNK, C_CHUNK), C_CHUNK ** -0.5),
        "b_s": 1.0 + nrm(ks[18], (L, C_HEADS, C_CHUNK), 0.1),
        "w_out": nrm(ks[19], (L, MIX_WIDTH, D_MODEL), MIX_WIDTH ** -0.5),
        "norm2_g": 1.0 + nrm(ks[20], (L, D_MODEL), 0.05),
        "w_ff1": nrm(ks[21], (L, D_MODEL, D_FF), D_MODEL ** -0.5),
        "w_ff2": nrm(ks[22], (L, D_FF, D_MODEL), D_FF ** -0.5),
        "final_g": 1.0 + nrm(ks[23], (D_MODEL,), 0.05),
    }


def reference(x_prompt, x_sample, c, state_gla, c_ctx, w_mod, b_mod, norm1_g, w_in,
              w_a_gate, b_a_gate, a_norm_g, w_dw, b_ln_g, b_ln_b, c_ln_g, c_ln_b,
              w_s, b_s, w_out, norm2_g, w_ff1, w_ff2, final_g):
    f32 = jnp.float32
    y_p = x_prompt
    y_s = x_sample
    s_zero = jnp.zeros((x_prompt.shape[0], A_HEADS, A_DK, A_DV), f32)
    ctx_states = []
    for l in range(DEPTH):
        lp = (norm1_g[l], w_in[l], w_a_gate[l], b_a_gate[l], a_norm_g[l], w_dw[l],
              b_ln_g[l], b_ln_b[l], c_ln_g[l], c_ln_b[l], w_s[l], b_s[l], w_out[l],
              norm2_g[l], w_ff1[l], w_ff2[l])
        mod_ctx = adaln(c_ctx, w_mod[l], b_mod[l])
        y_p, s_f, s_b = layer(y_p, mod_ctx, lp, s_zero, s_zero, False)
        ctx_states.append(jnp.stack([s_f, s_b], axis=1))
        mod_lat = adaln(c, w_mod[l], b_mod[l])
        y_s, _, _ = layer(y_s, mod_lat, lp,
                          state_gla[:, l, 0].astype(f32), state_gla[:, l, 1].astype(f32), True)
    new_state_gla = jnp.stack(ctx_states, axis=1).astype(state_gla.dtype)
    y_prompt = rmsnorm(y_p, final_g)
    y_sample = rmsnorm(y_s, final_g)
    return (y_prompt, y_sample, new_state_gla)
```

```python
import numpy as np
import concourse.bass as bass
import concourse.mybir as mybir

F32 = mybir.dt.float32
BF16 = mybir.dt.bfloat16
U8 = mybir.dt.uint8
I32 = mybir.dt.int32
AF = mybir.ActivationFunctionType
ALU = mybir.AluOpType
AX = mybir.AxisListType
DT_SIZE = {F32: 4, BF16: 2, U8: 1, I32: 4}
NSLOTS = 24


class T:
    def __init__(self, ap, space, plo, phi, lo, hi):
        self.ap = ap
        self.tile = self
        self.region = (space, plo, phi, lo, hi)

    def __getitem__(self, k):
        return V(self, self.ap[k])

    def v(self, ap):
        return V(self, ap)


class V:
    def __init__(self, tile, ap):
        self.tile = tile
        self.ap = ap

    def __getitem__(self, k):
        return V(self.tile, self.ap[k])


class Grid:
    def __init__(self, t, n):
        self.ap = t.ap
        space, plo, phi, lo, hi = t.region
        sz = (hi - lo) // n
        self.n = n
        self.subs = [T(t.ap[:, i], space, plo, phi, lo + i * sz, lo + (i + 1) * sz) for i in range(n)]
        self.tile = self.subs

    def __getitem__(self, key):
        k = key[1] if isinstance(key, tuple) and len(key) > 1 else slice(None)
        idx = [k] if isinstance(k, int) else list(range(*k.indices(self.n)))
        return V([self.subs[i] for i in idx], self.ap[key])


class Arena:
    def __init__(self, nc, name, nbytes, space):
        self.space = space
        self.nbytes = nbytes
        self.h = nc.alloc_sbuf_tensor(name, [128, nbytes], U8)
        self.ap = self.h.ap()
        self.off = 0
        self.peak = 0

    def mark(self):
        return self.off

    def release(self, m):
        self.off = m

    def alloc(self, shape, dtype, nparts=128, pbase=0):
        if isinstance(shape, int):
            shape = (shape,)
        n = int(np.prod(shape)) * DT_SIZE[dtype]
        off = (self.off + 31) // 32 * 32
        assert off + n <= self.nbytes, f"arena {self.space} overflow: {off + n} > {self.nbytes}"
        self.off = off + n
        self.peak = max(self.peak, self.off)
        v = self.ap[pbase:pbase + nparts, off:off + n].bitcast(dtype)
        if len(shape) > 1:
            names = [f"d{i}" for i in range(len(shape))]
            s = f"p ({' '.join(names)}) -> p {' '.join(names)}"
            v = v.rearrange(s, **{nm: int(sz) for nm, sz in zip(names, shape)})
        return T(v, self.space, pbase, pbase + nparts, off, off + n)


class Op:
    __slots__ = ("eng", "emit", "deps", "is_dma", "sigval", "signaler", "slot", "waits")

    def __init__(self, eng, emit, deps, is_dma):
        self.eng = eng
        self.emit = emit
        self.deps = deps
        self.is_dma = is_dma
        self.sigval = 0
        self.signaler = is_dma
        self.slot = None
        self.waits = None


class Prog:
    ENGS = ("pe", "act", "dve", "pool", "sp")

    def __init__(self, nc):
        self.nc = nc
        self.ops = []
        self.records = {}
        self.psum_h = nc.alloc_psum_tensor("psum_all", [128, 4096], F32)
        self.psum_ap = self.psum_h.ap()
        self.n_dram = 0
        self.dry = False

    def psum_bank(self, b, nbanks=1):
        ap = self.psum_ap[:, b * 512:(b + nbanks) * 512]
        return T(ap, "psum", 0, 128, b * 2048, (b + nbanks) * 2048)

    def dram(self, name, shape, dtype, kind="Internal"):
        h = self.nc.dram_tensor(name, list(shape), dtype, kind=kind)
        return T(h.ap(), "dram:" + name, 0, 1, 0, 1)

    def dram_sub(self, t, ap, lo, hi):
        sp = t.region[0]
        return T(ap, sp, 0, 1, lo, hi)

    def _deps_for(self, region, is_write, opid, deps):
        space, plo, phi, lo, hi = region
        recs = self.records.setdefault(space, [])
        keep = []
        eng = self.ops_eng
        for r in recs:
            overlap = r[0] < phi and plo < r[1] and r[2] < hi and lo < r[3]
            if not overlap:
                keep.append(r)
                continue
            conflict = is_write or r[5]
            if conflict:
                deps.add(r[4])
            contained = plo <= r[0] and r[1] <= phi and lo <= r[2] and r[3] <= hi
            if is_write and contained:
                continue
            if (not is_write) and (not r[5]) and contained and self.ops[r[4]].eng == eng \
                    and not self.ops[r[4]].is_dma and not self.cur_is_dma:
                continue
            keep.append(r)
        self.records[space] = keep

    def record(self, eng, emit, reads, writes, is_dma=False):
        if self.dry:
            return None
        opid = len(self.ops)
        deps = set()
        self.ops_eng = eng
        self.cur_is_dma = is_dma
        seen = set()
        for t in writes:
            self._deps_for(t.region, True, opid, deps)
        for t in reads:
            self._deps_for(t.region, False, opid, deps)
        op = Op(eng, emit, deps, is_dma)
        self.ops.append(op)
        for t in writes:
            sp, plo, phi, lo, hi = t.region
            self.records[sp].append([plo, phi, lo, hi, opid, True])
        for t in reads:
            key = id(t)
            if key in seen:
                continue
            seen.add(key)
            sp, plo, phi, lo, hi = t.region
            self.records[sp].append([plo, phi, lo, hi, opid, False])
        return op

    def op(self, eng, method, **kw):
        writes, reads, args = [], [], {}
        for k, v in kw.items():
            if isinstance(v, (T, V, Grid)):
                tl = v.tile if isinstance(v.tile, list) else [v.tile]
                (writes if k in ("out", "accum_out") else reads).extend(tl)
                args[k] = v.ap
            else:
                args[k] = v
        if method == "matmul" and kw.get("start") is False:
            ot = kw["out"].tile
            reads.extend(ot if isinstance(ot, list) else [ot])
        is_dma = method == "dma_start"
        return self.record(eng, lambda e: getattr(e, method)(**args), reads, writes, is_dma)

    def mm(self, out, lhsT, rhs, start=True, stop=True, **kw):
        return self.op("pe", "matmul", out=out, lhsT=lhsT, rhs=rhs, start=start, stop=stop, **kw)

    def transpose(self, out, in_, identity):
        return self.op("pe", "transpose", out=out, in_=in_, identity=identity)

    def dma(self, out, in_, eng="sp", **kw):
        return self.op(eng, "dma_start", out=out, in_=in_, **kw)

    def act(self, out, in_, func, eng="act", **kw):
        return self.op(eng, "activation", out=out, in_=in_, func=func, **kw)

    def finalize(self):
        nc = self.nc
        ops = self.ops
        for op in ops:
            for d in op.deps:
                dop = ops[d]
                if dop.eng == "pe" and op.eng == "pe" and not dop.is_dma and not op.is_dma:
                    continue
                dop.signaler = True
        sig = {e: 0 for e in self.ENGS}
        slotcnt = [0] * NSLOTS
        ndma = 0
        for op in ops:
            if op.is_dma:
                op.slot = ndma % NSLOTS
                ndma += 1
                slotcnt[op.slot] += 16
                op.sigval = slotcnt[op.slot]
            elif op.signaler:
                sig[op.eng] += 1
                op.sigval = sig[op.eng]
        seen = {e: {} for e in self.ENGS}
        for op in ops:
            need = {}
            for d in op.deps:
                dop = ops[d]
                if dop.eng == "pe" and op.eng == "pe" and not dop.is_dma and not op.is_dma:
                    continue
                key = ("dma", dop.slot) if dop.is_dma else dop.eng
                need[key] = max(need.get(key, 0), dop.sigval)
            if op.is_dma and op.sigval > 16:
                key = ("dma", op.slot)
                need[key] = max(need.get(key, 0), op.sigval - 16)
            w = []
            s = seen[op.eng]
            for key, val in need.items():
                if s.get(key, 0) < val:
                    s[key] = val
                    w.append((key, val))
            op.waits = w
        self.final_slot = slotcnt
        self.final_sig = sig
        self.sems = {e: nc.alloc_semaphore("sem_" + e) for e in self.ENGS}
        for i in range(NSLOTS):
            self.sems[("dma", i)] = nc.alloc_semaphore(f"sem_dma{i}")
        per = {e: [op for op in ops if op.eng == e] for e in self.ENGS}
        self.stats = {e: (len(per[e]), sum(len(o.waits) for o in per[e])) for e in self.ENGS}
        sems = self.sems

        def run(engine, name):
            for op in per[name]:
                for key, val in op.waits:
                    engine.wait_ge(sems[key], val)
                ins = op.emit(engine)
                if op.is_dma:
                    ins.then_inc(sems[("dma", op.slot)], 16)
                elif op.signaler:
                    ins.then_inc(sems[name], 1)
            if name == "sp":
                for i in range(NSLOTS):
                    if slotcnt[i] > 0:
                        engine.wait_ge(sems[("dma", i)], slotcnt[i])
                for e in ("pe", "act", "dve", "pool"):
                    if sig[e] > 0:
                        engine.wait_ge(sems[e], sig[e])

        with nc.Block() as block:
            @block.tensor
            def _(e):
                run(e, "pe")

            @block.scalar
            def _(e):
                run(e, "act")

            @block.vector
            def _(e):
                run(e, "dve")

            @block.gpsimd
            def _(e):
                run(e, "pool")

            @block.sync
            def _(e):
                run(e, "sp")

from concourse.bass_utils import run_bass_kernel_spmd

D = 1024
L = 2
NTOK = 4608
NB = 4
DFF = 4096
EPS = 1e-6
QO, KO, GO, VO, GAO, GBO, UO, VGO, LRO, NCOL = 0, 512, 1024, 1536, 2048, 2304, 2560, 2816, 3072, 3104
WI_SLABS = [(0, 512), (512, 1024), (1024, 1536), (1536, 2048), (2048, 2560), (2560, 3104)]
SLOT_COLS = 8 * 544
NSLOT = 3


def build(nlayers=L, dbg=None, groups=("ctx", "lat"), do_pre=True, do_main=True, nlat=8, stop=None, defer_cast=True):
    nc = bass.Bass("TRN2", target_bir_lowering=False)
    P = Prog(nc)

    def inp(name, shape, dt=F32):
        return P.dram(name, shape, dt, kind="ExternalInput")

    xin = inp("xin", [NTOK, D])
    cvec = inp("cvec", [128, 8, 2])
    sgla = inp("sgla", [L, 128, 4, 128])
    w_in_r = inp("w_in_r", [L, 128, 8, NCOL])
    w_out_r = inp("w_out_r", [L, 128, 8, D])
    w_ff1_r = inp("w_ff1_r", [L, 128, 8, DFF])
    w_ff2_r = inp("w_ff2_r", [L, 128, 32, D])
    w_mod_r = inp("w_mod_r", [L, 128, 8, 6 * D])
    b_mod = inp("b_mod", [L, 6 * D])
    wg = inp("wg", [L, 33, 512])
    n1g = inp("n1g", [L, 128, 8])
    n2g = inp("n2g", [L, 128, 8])
    ang = inp("ang", [128, L])
    wdw = inp("wdw", [L, 128, 2, 31])
    blng = inp("blng", [L, 128, 2])
    blnb = inp("blnb", [L, 128, 2])
    clng = inp("clng", [L, 256])
    clnb = inp("clnb", [L, 256])
    wst = inp("wst", [L, 128, 4, 128])
    bs = inp("bs", [L, 1, 512])
    fg = inp("fg", [D])
    y = P.dram("y", [NTOK, D], F32, kind="ExternalOutput")
    nst = P.dram("nst", [2, L, 2, 4, 64, 128], F32, kind="ExternalOutput")
    dbg_outs = {}

    wi_s = [[P.dram(f"wi{l}_{s}", [128, 8 * (c1 - c0)], BF16) for s, (c0, c1) in enumerate(WI_SLABS)] for l in range(L)]
    wo_s = [[P.dram(f"wo{l}_{s}", [128, 4 * D], BF16) for s in range(2)] for l in range(L)]
    f1_s = [[P.dram(f"f1{l}_{s}", [128, 8 * 512], BF16) for s in range(8)] for l in range(L)]
    f2_s = [[[P.dram(f"f2{l}_{nh}_{ks}", [128, 4 * 512], BF16) for ks in range(8)] for nh in range(2)] for l in range(L)]
    gates_s = [[[P.dram(f"gate{l}_{c}_{g}", [128, D], F32) for g in range(2)] for c in range(2)] for l in range(L)]
    xmid = P.dram("xmid", [NTOK, D], F32)
    sb_scr = P.dram("sb_scr", [32, 64, 512], BF16)
    xmid_t = [P.dram_sub(xmid, xmid.ap[t * 512:(t + 1) * 512, :], t, t + 1) for t in range(9)]
    y_t = [P.dram_sub(y, y.ap[t * 512:(t + 1) * 512, :], t, t + 1) for t in range(9)]
    sb_t = [P.dram_sub(sb_scr, sb_scr.ap[b], b, b + 1) for b in range(32)]

    A = Arena(nc, "arena", 211712, "sb")
    rr = [0]

    blo = [0]

    def bank(n=1):
        if blo[0]:
            assert n == 1
            b = blo[0] + rr[0] % (8 - blo[0])
            rr[0] = rr[0] + 1
        elif n == 2:
            b = ((rr[0] + 1) // 2 * 2) % 8
            rr[0] = b + 2
        else:
            b = rr[0] % 8
            rr[0] = b + 1
        return P.psum_bank(b, n)

    def pv(ps, **kw):
        names = list(kw.keys())
        if len(names) == 1:
            s = f"p ({names[0]} ww) -> p {names[0]} ww"
        else:
            s = f"p ({names[0]} {names[1]} ww) -> p {names[0]} {names[1]} ww"
        return V(ps.tile, ps.ap.rearrange(s, **kw))

    evc = [0]

    def copy(out, in_, scale=None):
        evc[0] += 1
        if evc[0] % 2 == 0:
            if scale is None:
                P.act(out, in_, AF.Copy)
            else:
                P.act(out, in_, AF.Copy, scale=float(scale))
        else:
            if scale is None:
                P.op("dve", "tensor_copy", out=out, in_=in_)
            else:
                P.op("dve", "tensor_scalar", out=out, in0=in_, scalar1=float(scale), scalar2=None, op0=ALU.mult)

    ident = A.alloc(128, BF16)
    ones_bf = A.alloc(128, BF16)
    ones32 = A.alloc(128, F32)
    triL = A.alloc(128, F32)
    triU = A.alloc(128, F32)
    negcol = A.alloc(1, F32)
    mask = A.alloc((8, 128), BF16)
    sel = A.alloc((2, 128), F32, nparts=2)
    onehot = A.alloc(2, F32, nparts=2)
    m0 = A.mark()
    tmpc = A.alloc(128, F32)
    P.record("pool", lambda e: e.memset(ones32.ap, 1.0), [], [ones32])
    P.record("pool", lambda e: e.memset(negcol.ap, -1.0 / 16), [], [negcol])
    P.op("dve", "tensor_copy", out=ones_bf, in_=ones32)
    P.record("pool", lambda e: e.affine_select(out=tmpc.ap, in_=ones32.ap, pattern=[[-1, 128]], compare_op=ALU.is_equal,
                                               fill=0.0, base=0, channel_multiplier=1), [ones32], [tmpc])
    P.op("dve", "tensor_copy", out=ident, in_=tmpc)
    tl1 = A.alloc(128, F32)
    tu1 = A.alloc(128, F32)
    P.record("pool", lambda e: e.affine_select(out=tl1.ap, in_=ones32.ap, pattern=[[1, 128]], compare_op=ALU.is_ge,
                                               fill=0.0, base=0, channel_multiplier=-1), [ones32], [tl1])
    P.record("pool", lambda e: e.affine_select(out=tu1.ap, in_=ones32.ap, pattern=[[-1, 128]], compare_op=ALU.is_ge,
                                               fill=0.0, base=0, channel_multiplier=1), [ones32], [tu1])
    P.op("dve", "tensor_scalar", out=triL, in0=tl1, scalar1=-1.0 / 16, scalar2=None, op0=ALU.mult)
    P.op("dve", "tensor_scalar", out=triU, in0=tu1, scalar1=-1.0 / 16, scalar2=None, op0=ALU.mult)
    for hd in range(8):
        P.op("dve", "tensor_copy", out=mask[:, hd, :], in_=(tl1 if hd < 4 else tu1))
    P.record("pool", lambda e: e.affine_select(out=sel.ap, in_=ones32.ap[0:2, :].unsqueeze(1).to_broadcast([2, 2, 128]),
                                               pattern=[[-1, 2], [0, 128]], compare_op=ALU.is_equal,
                                               fill=0.0, base=0, channel_multiplier=1), [ones32], [sel])
    P.record("pool", lambda e: e.affine_select(out=onehot.ap, in_=ones32.ap[0:2, 0:2], pattern=[[-1, 2]],
                                               compare_op=ALU.is_equal, fill=0.0, base=0, channel_multiplier=1),
             [ones32], [onehot])
    A.release(m0)

    wg_bf = A.alloc((L, 512), BF16, nparts=33)
    wst_bf = A.alloc((L, 4, 128), BF16)
    bs_hi = A.alloc((L, 512), BF16, nparts=1)
    bs_lo = A.alloc((L, 512), BF16, nparts=1)
    cG = A.alloc((L, 256), F32)
    cB = A.alloc((L, 256), F32)
    fgt = A.alloc(D, F32)
    modc = A.alloc((L, 2, 4, 8), F32)
    angc = A.alloc(L, F32)
    blg = A.alloc((L, 2), F32)
    blb = A.alloc((L, 2), F32)
    blgh = A.alloc((L, 2), F32)
    blbh = A.alloc((L, 2), F32)
    wdw_t = A.alloc((L, 2, 31), F32)
    n1g_t = A.alloc((L, 8), F32)
    n2g_t = A.alloc((L, 8), F32)
    cT = A.alloc((8, 2), F32)
    sc = A.alloc((8, 2), F32)
    m0 = A.mark()
    st_wg = A.alloc((L, 512), F32, nparts=33)
    st_ws = A.alloc((L, 4, 128), F32)
    st_bs = A.alloc((L, 512), F32, nparts=1)
    st_bs2 = A.alloc((L, 512), F32, nparts=1)
    P.dma(st_wg, V(wg, wg.ap.rearrange("l k n -> k l n")))
    P.dma(st_ws, V(wst, wst.ap.rearrange("l q h p -> q l h p")))
    P.dma(st_bs, V(bs, bs.ap.rearrange("l o n -> o l n")))
    P.dma(cG, V(clng, clng.ap.partition_broadcast(128)))
    P.dma(cB, V(clnb, clnb.ap.partition_broadcast(128)))
    P.dma(fgt, V(fg, fg.ap.partition_broadcast(128)))
    P.dma(angc, ang)
    P.dma(blg, V(blng, blng.ap.rearrange("l p c -> p l c")))
    P.dma(blb, V(blnb, blnb.ap.rearrange("l p c -> p l c")))
    P.dma(wdw_t, V(wdw, wdw.ap.rearrange("l p c j -> p l c j")))
    P.dma(n1g_t, V(n1g, n1g.ap.rearrange("l p k -> p l k")))
    P.dma(n2g_t, V(n2g, n2g.ap.rearrange("l p k -> p l k")))
    P.dma(cT, cvec)
    P.op("dve", "tensor_copy", out=wg_bf, in_=st_wg)
    P.op("dve", "tensor_copy", out=wst_bf, in_=st_ws)
    P.op("dve", "tensor_copy", out=bs_hi, in_=st_bs)
    P.op("dve", "tensor_tensor", out=st_bs2, in0=st_bs, in1=bs_hi, op=ALU.subtract)
    P.op("dve", "tensor_copy", out=bs_lo, in_=st_bs2)
    P.op("dve", "tensor_scalar", out=blgh, in0=blg, scalar1=0.5, scalar2=None, op0=ALU.mult)
    P.op("dve", "tensor_scalar", out=blbh, in0=blb, scalar1=0.5, scalar2=None, op0=ALU.mult)
    P.op("dve", "tensor_scalar", out=wdw_t, in0=wdw_t, scalar1=0.5, scalar2=None, op0=ALU.mult)
    thc = A.alloc((8, 2), F32)
    P.act(thc, cT, AF.Tanh, scale=0.5)
    P.op("dve", "scalar_tensor_tensor", out=sc, in0=thc, scalar=1.0, in1=cT, op0=ALU.add, op1=ALU.mult)
    P.op("dve", "tensor_scalar", out=sc, in0=sc, scalar1=0.5, scalar2=None, op0=ALU.mult)
    A.release(m0)

    m0 = A.mark()
    modrow = A.alloc(6 * D, F32, nparts=2)
    brow = A.alloc(6 * D, F32, nparts=2)
    wm = [A.alloc((8, 512), F32) for _ in range(2)]
    gst = [A.alloc(512, F32) for _ in range(2)]
    mcs = A.alloc(64, F32)
    s32 = [A.alloc(4096, F32) for _ in range(4)]
    s16 = [A.alloc(4096, BF16) for _ in range(4)]
    ci = [0]

    def cast_units(src_ap, src_t, dst, kc, ncols, scale=None):
        per = max(1, 4096 // ncols)
        return [(src_ap, src_t, dst, k0, min(kc, k0 + per), ncols, scale) for k0 in range(0, kc, per)]

    def cast_unit(u, st=None):
        src_ap, src_t, dst, k0, k1, ncols, scale = u
        st32, st16 = st if st is not None else (s32, s16)
        if True:
            n = (k1 - k0) * ncols
            a = st32[ci[0] % len(st32)]
            b = st16[ci[0] % len(st16)]
            ci[0] += 1
            P.dma(V(a, a.ap[:, 0:n].rearrange("p (k n) -> p k n", n=ncols)), V(src_t, src_ap[:, k0:k1, :]))
            e = ci[0] % 2
            if scale is None:
                if e == 0:
                    P.act(b[:, 0:n], a[:, 0:n], AF.Copy)
                elif e == 1:
                    P.op("dve", "tensor_copy", out=b[:, 0:n], in_=a[:, 0:n])
                else:
                    P.op("pool", "tensor_copy", out=b[:, 0:n], in_=a[:, 0:n])
            else:
                if e == 0:
                    P.act(b[:, 0:n], a[:, 0:n], AF.Copy, scale=float(scale))
                else:
                    P.op("dve" if e == 1 else "pool", "tensor_scalar", out=b[:, 0:n], in0=a[:, 0:n],
                         scalar1=float(scale), scalar2=None, op0=ALU.mult)
            P.dma(V(dst, dst.ap[:, k0 * ncols:k1 * ncols]), b[:, 0:n], eng=("act" if st is None else "sp"))

    def layer_cast_units(l):
        us = []
        for s, (c0, c1) in enumerate(WI_SLABS):
            us += cast_units(w_in_r.ap[l, :, :, c0:c1], w_in_r, wi_s[l][s], 8, c1 - c0, scale=(0.125 if s == 0 else None))
        for s in range(2):
            us += cast_units(w_out_r.ap[l, :, 4 * s:4 * s + 4, :], w_out_r, wo_s[l][s], 4, D, scale=0.5)
        for s in range(8):
            us += cast_units(w_ff1_r.ap[l, :, :, s * 512:(s + 1) * 512], w_ff1_r, f1_s[l][s], 8, 512)
        for nh in range(2):
            for ks in range(8):
                us += cast_units(w_ff2_r.ap[l, :, 4 * ks:4 * ks + 4, nh * 512:(nh + 1) * 512], w_ff2_r, f2_s[l][nh][ks], 4, 512)
        return us

    l0_units = layer_cast_units(0)
    for l in range(nlayers):
        P.dma(brow, V(b_mod, b_mod.ap[l].partition_broadcast(2)))
        for cc in range(12):
            w = wm[cc % 2]
            P.dma(w, V(w_mod_r, w_mod_r.ap[l, :, :, cc * 512:(cc + 1) * 512]), eng="act")
            ps = bank()
            for kc in range(8):
                P.mm(ps[0:2, :], sc[:, kc, :], w[:, kc, :], start=(kc == 0), stop=(kc == 7))
            P.op("dve", "tensor_tensor", out=modrow[:, cc * 512:(cc + 1) * 512], in0=ps[0:2, :],
                 in1=brow[:, cc * 512:(cc + 1) * 512], op=ALU.add)
            for _ in range(2):
                if l0_units:
                    cast_unit(l0_units.pop(0))
        i = 0
        for c in range(2):
            for g, vec in enumerate((2, 5)):
                for nh in range(2):
                    ps = bank()
                    P.mm(ps, sel[:, c, :], modrow[:, vec * D + nh * 512: vec * D + (nh + 1) * 512])
                    st = gst[i % 2]
                    i += 1
                    copy(st, ps)
                    P.dma(V(gates_s[l][c][g], gates_s[l][c][g].ap[:, nh * 512:(nh + 1) * 512]), st)
        ps = bank()
        for c in range(2):
            for vi, vec in enumerate((0, 1, 3, 4)):
                for kc in range(8):
                    j = (c * 4 + vi) * 8 + kc
                    P.mm(ps[:, j:j + 1], modrow[:, vec * D + kc * 128: vec * D + (kc + 1) * 128], onehot[:, c:c + 1])
        P.op("dve", "tensor_copy", out=mcs, in_=ps[:, 0:64])
        mv = V(mcs, mcs.ap.rearrange("p (c v k) -> p c v k", c=2, v=4))
        for c in range(2):
            P.op("dve", "scalar_tensor_tensor", out=modc[:, l, c, 0, :], in0=mv[:, c, 1, :], scalar=1.0,
                 in1=n1g_t[:, l, :], op0=ALU.add, op1=ALU.mult)
            P.op("dve", "tensor_copy", out=modc[:, l, c, 1, :], in_=mv[:, c, 0, :])
            P.op("dve", "scalar_tensor_tensor", out=modc[:, l, c, 2, :], in0=mv[:, c, 3, :], scalar=1.0,
                 in1=n2g_t[:, l, :], op0=ALU.add, op1=ALU.mult)
            P.op("dve", "tensor_copy", out=modc[:, l, c, 3, :], in_=mv[:, c, 2, :])

    while l0_units:
        cast_unit(l0_units.pop(0))
    if not defer_cast:
        for l in range(1, nlayers):
            for u in layer_cast_units(l):
                cast_unit(u)
    A.release(m0)

    gate_t = [A.alloc(D, F32) for _ in range(2)]
    Scat = A.alloc((NB, 4, 128), BF16)
    S32 = A.alloc((4, 128), F32)
    xtb = [Grid(A.alloc((NB, D), F32), NB) for _ in range(2)]
    xt = xtb[0]
    ssn = A.alloc(NB, F32)
    rsn = A.alloc(NB, F32)
    ssn2 = A.alloc(NB, F32)
    rsn2 = A.alloc(NB, F32)
    junk = A.alloc(D, BF16)
    xnb = [A.alloc(D, BF16) for _ in range(3)]
    hT = Grid(A.alloc((8, 512), BF16), 8)
    mixT = Grid(A.alloc((6, 512), BF16), 6)
    mixC = A.alloc((2, 512), BF16)
    tmp32 = [A.alloc(512, F32) for _ in range(2)]
    slots = [A.alloc(SLOT_COLS, BF16) for _ in range(NSLOT)]
    lrT = A.alloc(512, BF16, nparts=33)
    xcp = A.alloc((2, 752), BF16)
    ov = A.mark()
    gv = A.alloc((NB, 256), F32)
    gsq = A.alloc((NB, 256), F32)
    e32 = A.alloc(512, F32)
    sp = [A.alloc(512, F32) for _ in range(2)]
    Ek = A.alloc(512, F32)
    Epos = A.alloc((4, 128), F32)
    Eneg = A.alloc((4, 128), F32)
    qcat = A.alloc((4, 512), BF16)
    kcat = A.alloc((4, 512), BF16)
    kdt = Grid(A.alloc((NB, 512), BF16), NB)
    vtok = Grid(A.alloc((NB, 512), BF16), NB)
    sg = A.alloc((4, 512), BF16)
    dec = A.alloc((NB, 4), F32)
    u_g = A.alloc((2, 512), BF16)
    vst = A.alloc((6, NB), F32)
    vn = A.alloc((NB, 256), BF16)
    vtmp = A.alloc(256, F32)
    Abf = [A.alloc((8, 128), BF16) for _ in range(2)]
    sqb = A.alloc(512, BF16)
    lnvb = [A.alloc(512, F32) for _ in range(2)]
    on32 = A.alloc(512, F32)
    th32 = [A.alloc(512, F32) for _ in range(2)]
    y32 = A.alloc((2, 512), F32)
    ysq = A.alloc((2, 512), F32)
    m32 = A.alloc(512, F32)
    msq = A.alloc(512, F32)
    var = A.alloc(512, F32)
    yc = A.alloc(512, F32)
    z32 = A.alloc(512, F32)
    stS = A.alloc((4, 128), F32)
    ov_end = A.mark()
    A.release(ov)
    diag = A.alloc((2, 31, 128), BF16)
    assert A.off <= ov + 20 * 1024
    A.release(ov)
    hid = A.alloc((32, 512), BF16)
    rbf = [A.alloc(512, BF16) for _ in range(2)]
    cst = ([A.alloc(4096, F32) for _ in range(2)], [A.alloc(4096, BF16) for _ in range(1)])
    print('overlay: mixer', ov_end - ov, 'ffn', A.off - ov)
    assert A.off <= ov_end, (A.off - ov, ov_end - ov)
    A.off = max(A.off, ov_end)
    print("SBUF arena peak bytes/partition:", A.peak, "end", A.off)

    P.record("pool", lambda e: e.memset(lrT.ap[32:33, :], 1.0), [], [lrT])

    wq = []
    wstate = {"n": 0}

    wsched = []
    wstate["issued"] = 0
    wstate["freed"] = set()
    wstate["cur"] = -1
    widx = {}

    def wpump():
        while (wstate["issued"] < len(wsched) and wstate["issued"] <= wstate["cur"] + 2
               and (wstate["issued"] < NSLOT or (wstate["issued"] - NSLOT) in wstate["freed"])):
            i = wstate["issued"]
            s_, n_ = wsched[i]
            P.dma(slots[i % NSLOT][:, 0:n_], s_)
            wstate["issued"] += 1

    def wload(name, src, ncols_total):
        k = wstate["n"]
        wstate["n"] += 1
        widx[id(slots[k % NSLOT])] = k
        if P.dry:
            wsched.append((src, ncols_total))
            return slots[k % NSLOT]
        wstate["cur"] = k
        wpump()
        assert wstate["issued"] > k, ("weight slot not freed in time", name, k)
        return slots[k % NSLOT]

    def wfree(slot):
        if P.dry:
            return
        wstate["freed"].add(widx[id(slot)])
        wpump()

    def dbgout(name, t, shape, dtype=F32, l=0, ti=None):
        if dbg and (not P.dry) and name in dbg and l == 0 and ti == dbg.get("tile", 1) and name not in dbg_outs:
            o = P.dram("dbg_" + name, list(shape), dtype, kind="ExternalOutput")
            P.dma(o, t)
            dbg_outs[name] = o

    def norm_sq(xt_, ss, b):
        P.act(junk, xt_[:, b, :], AF.Square, accum_out=ss[:, b:b + 1])

    def norm_stats(xt_, ss, rs):
        P.record("pool", lambda e: e.memset(ss.ap, 0.0), [], [ss])
        for b in range(NB):
            norm_sq(xt_, ss, b)
        P.act(rs, ss, AF.Ln, scale=1.0 / D, bias=EPS)
        P.act(rs, rs, AF.Exp, scale=-0.5)

    def norm_xn(xt_, rs, b, nbuf=2):
        P.op("dve", "tensor_scalar", out=xnb[b % nbuf], in0=xt_[:, b, :], scalar1=rs[:, b:b + 1], scalar2=None, op0=ALU.mult)

    def norm_block(l, cond, which, xt_, rs, b, ahead=True):
        gi, si = (0, 1) if which == 1 else (2, 3)
        xn = xnb[b % 2] if ahead else xnb[b % 3]
        if ahead and b == 0:
            norm_xn(xt_, rs, 0)
            norm_xn(xt_, rs, 1)
        psb = bank()
        pst = T(psb.ap.bitcast(BF16), *psb.region)
        for kc in range(8):
            P.transpose(pst[:, kc * 128:(kc + 1) * 128], xn[:, kc * 128:(kc + 1) * 128], ident)
        if ahead and b + 2 < NB:
            norm_xn(xt_, rs, b + 2)
        for kc in range(8):
            o = hT[:, kc, b * 128:(b + 1) * 128]
            i = pst[:, kc * 128:(kc + 1) * 128]
            if b % 2 == 0:
                P.op("dve", "tensor_scalar", out=o, in0=i, scalar1=modc[:, l, cond, gi, kc:kc + 1],
                     scalar2=modc[:, l, cond, si, kc:kc + 1], op0=ALU.mult, op1=ALU.add)
            else:
                P.act(o, i, AF.Identity, scale=modc[:, l, cond, gi, kc:kc + 1], bias=modc[:, l, cond, si, kc:kc + 1])

    def norm_to_hT(l, cond, which):
        norm_stats(xt, ssn, rsn)
        for b in range(NB):
            norm_block(l, cond, which, xt, rsn, b)

    def projA(w, c0, ncols, nchunk, mrows=128):
        wv = V(w, w.ap[:, 0:8 * ncols].rearrange("p (k n) -> p k n", n=ncols))
        for ch in range(nchunk):
            ps = bank()
            for kc in range(8):
                P.mm(ps[0:mrows, :], wv[:, kc, c0 + ch * 128: c0 + ch * 128 + mrows], hT[:, kc, :],
                     start=(kc == 0), stop=(kc == 7))
            yield ch, ps

    def projB(w, c0, ncols, n, b):
        wv = V(w, w.ap[:, 0:8 * ncols].rearrange("p (k n) -> p k n", n=ncols))
        ps = bank()
        for kc in range(8):
            P.mm(ps[:, 0:n], hT[:, kc, b * 128:(b + 1) * 128], wv[:, kc, c0:c0 + n], start=(kc == 0), stop=(kc == 7))
        return ps

    def gate_prep(l, b, w_k, need_feat, extra=None):
        t0, t1 = b * 128, (b + 1) * 128
        ps = bank()
        P.mm(ps, lrT[:, t0:t1], wg_bf[:, l, :])
        pk = projB(w_k, 0, 512, 512, b)
        ex = extra() if extra is not None else None
        s = sp[b % 2]
        P.act(e32, ps, AF.Exp, scale=-1.0)
        P.act(s, e32, AF.Ln, bias=1.0)
        pb = bank()
        P.mm(pb[:, 0:256], triL, s[:, 0:256])
        P.mm(pb[:, 256:512], triU, s[:, 256:512])
        if need_feat:
            pf = bank()
            pfv = pv(pf, h=4)
            for h in range(4):
                P.mm(pfv[0:64, h, :], s[:, h * 64:(h + 1) * 64], triL)
                P.mm(pfv[64:128, h, :], s[:, 256 + h * 64:256 + (h + 1) * 64], triU, tile_position=(0, 64))
        else:
            pd = ps
            for h in range(4):
                P.mm(pd[64:128, h:h + 1], s[:, 256 + h * 64:256 + (h + 1) * 64], negcol, tile_position=(0, 64))
        P.act(Ek, pb, AF.Exp, scale=-1.0)
        if need_feat:
            P.act(Epos, pfv, AF.Exp)
            P.act(Eneg, pfv, AF.Exp, scale=-1.0)
            P.act(V(dec, dec.ap[0:64, b, :].unsqueeze(2)), pfv[0:64, :, 127:128], AF.Exp)
            P.act(V(dec, dec.ap[64:128, b, :].unsqueeze(2)), pfv[64:128, :, 0:1], AF.Exp)
        else:
            P.act(dec[64:128, b, :], pd[64:128, 0:4], AF.Exp)
        P.op("dve", "tensor_tensor",
             out=V(kdt[:, b, :].tile, kdt.ap[:, b, :].rearrange("p (h z d) -> p h z d", h=4, z=2)),
             in0=V(pk.tile, pk.ap.rearrange("p (h z d) -> p h z d", h=4, z=2)),
             in1=V(Ek, Ek.ap.rearrange("p (z h d) -> p h z d", z=2, h=4)), op=ALU.mult)
        return ex

    def state_update(b, rows, dst):
        r0, r1 = rows
        pd = bank()
        pdv = pv(pd, h=4)
        for h in range(4):
            P.mm(pdv[:, h, :], kdt[:, b, h * 128:(h + 1) * 128], vtok[:, b, h * 128:(h + 1) * 128])
        P.op("dve", "tensor_tensor", out=S32[r0:r1], in0=S32[r0:r1], in1=pdv[r0:r1], op=ALU.add)
        P.op("dve", "tensor_tensor", out=S32[r0:r1], in0=S32[r0:r1],
             in1=V(dec, dec.ap[r0:r1, b, :].unsqueeze(2).to_broadcast([r1 - r0, 4, 128])), op=ALU.mult)
        if dst is not None:
            P.act(dst, S32[r0:r1], AF.Copy)

    def load_x(l, t, xt_):
        src = xin if l == 0 else xmid_t[t]
        sap = (xin.ap if l == 0 else xmid.ap)[t * 512:(t + 1) * 512, :].rearrange("(b p) d -> p b d", p=128)
        P.dma(xt_, V(src, sap))

    def prepass(l, cond, t, seq_first_blocks, seq_last_blocks, gb0, group, nxt, dcast):
        w5 = wload("wi5", wi_s[l][5], 8 * 544)
        for ch, ps in projA(w5, 512, 544, 1, mrows=32):
            copy(lrT[0:32, :], ps[0:32, :])
        wfree(w5)
        w1 = wload("wi1", wi_s[l][1], 8 * 512)
        w3 = wload("wi3", wi_s[l][3], 8 * 512)
        def chain_step(b):
            gb = gb0 + b
            if gb in seq_last_blocks:
                if group == "ctx":
                    P.record("pool", lambda e: e.memset(S32.ap[64:128], 0.0), [], [S32])
                else:
                    P.dma(S32[64:128], V(sgla, sgla.ap[l, 64:128]))
            sbf = Abf[b % 2]
            sbv = V(sbf, sbf.ap.rearrange("p a b -> p (a b)")[:, 0:512].rearrange("p (h v) -> p h v", h=4))
            P.act(sbv[64:128], S32[64:128], AF.Copy)
            P.dma(V(sb_t[gb], sb_t[gb].ap.rearrange("d (h v) -> d h v", h=4)), sbv[64:128])
            state_update(b, (64, 128), None)
            if gb in seq_first_blocks and group == "ctx":
                si = seq_first_blocks.index(gb)
                P.dma(V(nst, nst.ap[si, l, 1].rearrange("h d v -> d h v")), S32[64:128])

        order = list(reversed(range(NB)))
        for i, b in enumerate(order):
            def v_proj(b=b):
                pvv = projB(w3, 0, 512, 512, b)
                copy(vtok[:, b, :], pvv)
            gate_prep(l, b, w1, False, extra=v_proj)
            if i == 0:
                nxt.load()
            if i == 2:
                nxt.stats()
            if i >= 1:
                chain_step(order[i - 1])
        wfree(w1)
        wfree(w3)
        nxt.finish()
        chain_step(order[-1])

    def mainpass(l, cond, t, seq_first_blocks, seq_last_blocks, gb0, group, last_layer, nxt, dcast):
        nseg, seglen = (2, 256) if group == "ctx" else (8, 64)
        pad = seglen + 30
        w5 = wload("wi5", wi_s[l][5], 8 * 544)
        for ch, ps in projA(w5, 0, 544, 2):
            P.act(u_g[:, ch, :], ps, AF.Gelu_apprx_tanh)
        for ch, ps in projA(w5, 512, 544, 1, mrows=32):
            copy(lrT[0:32, :], ps[0:32, :])
        for b in range(NB):
            ps = projB(w5, 256, 544, 256, b)
            P.act(gv[:, b, :], ps[:, 0:256], AF.Gelu_apprx_tanh)
        wfree(w5)
        if stop == "slab5":
            return
        P.op("pool", "tensor_tensor", out=gsq, in0=gv, in1=gv, op=ALU.mult)
        P.op("dve", "tensor_reduce", out=vst[:, 0, :], in_=gv, axis=AX.X, op=ALU.add)
        P.op("dve", "tensor_reduce", out=vst[:, 1, :], in_=gsq, axis=AX.X, op=ALU.add)
        P.op("dve", "tensor_scalar", out=vst[:, 2, :], in0=vst[:, 0, :], scalar1=1.0 / 256, scalar2=None, op0=ALU.mult)
        P.op("dve", "tensor_tensor", out=vst[:, 3, :], in0=vst[:, 2, :], in1=vst[:, 2, :], op=ALU.mult)
        P.op("dve", "scalar_tensor_tensor", out=vst[:, 4, :], in0=vst[:, 1, :], scalar=1.0 / 256, in1=vst[:, 3, :],
             op0=ALU.mult, op1=ALU.subtract)
        P.act(vst[:, 5, :], vst[:, 4, :], AF.Ln, bias=EPS)
        P.act(vst[:, 5, :], vst[:, 5, :], AF.Exp, scale=-0.5)
        for b in range(NB):
            P.op("dve", "scalar_tensor_tensor", out=vtmp, in0=gv[:, b, :], scalar=vst[:, 2, b:b + 1], in1=cG[:, l, :],
                 op0=ALU.subtract, op1=ALU.mult)
            P.op("dve", "scalar_tensor_tensor", out=vn[:, b, :], in0=vtmp, scalar=vst[:, 5, b:b + 1], in1=cB[:, l, :],
                 op0=ALU.mult, op1=ALU.add)
        if stop == "gmlpstats":
            return
        w1 = wload("wi1", wi_s[l][1], 8 * 512)
        w0 = wload("wi0", wi_s[l][0], 8 * 512)
        w1v = V(w1, w1.ap[:, 0:4096].rearrange("p (k n) -> p k n", n=512))
        w0v = V(w0, w0.ap[:, 0:4096].rearrange("p (k n) -> p k n", n=512))
        for b in range(NB):
            t0, t1 = b * 128, (b + 1) * 128

            def qk_proj(t0=t0, t1=t1):
                res = []
                for wv_ in (w0v, w1v):
                    ps = bank()
                    psv = pv(ps, h=4)
                    for h in range(4):
                        for kc in range(8):
                            P.mm(psv[:, h, :], wv_[:, kc, h * 128:(h + 1) * 128], hT[:, kc, t0:t1],
                                 start=(kc == 0), stop=(kc == 7))
                    res.append(psv)
                return res

            psq, psk = gate_prep(l, b, w1, True, extra=qk_proj)
            P.op("dve", "tensor_tensor", out=qcat[:, :, t0:t1], in0=psq, in1=Epos, op=ALU.mult)
            P.op("dve", "tensor_tensor", out=kcat[:, :, t0:t1], in0=psk, in1=Eneg, op=ALU.mult)
        wfree(w1)
        wfree(w0)
        nxt.load()
        w3 = wload("wi3", wi_s[l][3], 8 * 512)
        for b in range(NB):
            pvv = projB(w3, 0, 512, 512, b)
            copy(vtok[:, b, :], pvv)
        wfree(w3)
        for c in range(2):
            P.op("dve", "tensor_tensor", out=diag[:, c, :, :],
                 in0=V(ident, ident.ap.unsqueeze(1).to_broadcast([128, 31, 128])),
                 in1=V(wdw_t, wdw_t.ap[:, l, c, :].unsqueeze(2).to_broadcast([128, 31, 128])), op=ALU.mult)
        if stop == "glaprep":
            return
        w2 = wload("wi2", wi_s[l][2], 8 * 512)
        for ch, ps in projA(w2, 0, 512, 4):
            th = th32[ch % 2]
            P.act(th, ps, AF.Tanh, scale=0.5)
            P.op("dve", "scalar_tensor_tensor", out=sg[:, ch, :], in0=th, scalar=1.0, in1=ps, op0=ALU.add, op1=ALU.mult)
        wfree(w2)
        if stop == "g":
            return
        w4 = wload("wi4", wi_s[l][4], 8 * 512)
        xcv = V(xcp, xcp.ap[:, :, 0:nseg * pad].rearrange("p c (s w) -> p c s w", w=pad))
        for c in range(2):
            wv4 = V(w4, w4.ap[:, 0:4096].rearrange("p (k n) -> p k n", n=512))
            pa = bank()
            pbk = bank()
            for kc in range(8):
                P.mm(pa, wv4[:, kc, c * 128:(c + 1) * 128], hT[:, kc, :], start=(kc == 0), stop=(kc == 7))
            for kc in range(8):
                P.mm(pbk, wv4[:, kc, 256 + c * 128:256 + (c + 1) * 128], hT[:, kc, :], start=(kc == 0), stop=(kc == 7))
            th = th32[c % 2]
            P.act(th, pbk, AF.Tanh, scale=0.5)
            P.op("dve", "scalar_tensor_tensor", out=xcv[:, c, :, 15:15 + seglen],
                 in0=V(th, th.ap.rearrange("p (s w) -> p s w", w=seglen)), scalar=1.0,
                 in1=V(pa.tile, pa.ap.rearrange("p (s w) -> p s w", w=seglen)), op0=ALU.add, op1=ALU.mult)
        if stop == "conv_glu":
            return
        wfree(w4)
        for b in range(NB):
            P.dma(Scat[64:128, b], V(sb_t[gb0 + b], sb_t[gb0 + b].ap.rearrange("d (h v) -> d h v", h=4)))
        if stop == "gc_load":
            return
        def att_stage(b):
            t0, t1 = b * 128, (b + 1) * 128
            pa2 = bank(2)
            pav = V(pa2.tile, pa2.ap.rearrange("p (a z) -> p a z", a=8))
            for h in range(4):
                for z in range(2):
                    P.mm(pav[:, 4 * z + h, :], kcat[z * 64:(z + 1) * 64, h, t0:t1], qcat[z * 64:(z + 1) * 64, h, t0:t1])
            Ab = Abf[b % 2]
            P.op("dve", "tensor_tensor", out=Ab, in0=pav, in1=mask, op=ALU.mult)

        def norm_tail(b, po):
            t0, t1 = b * 128, (b + 1) * 128
            P.op("dve", "tensor_tensor", out=on32, in0=po, in1=lnvb[b % 2], op=ALU.mult)
            P.op("dve", "scalar_tensor_tensor", out=mixT[:, 0:4, t0:t1],
                 in0=V(on32, on32.ap.rearrange("p (h t) -> p h t", h=4)), scalar=angc[:, l:l + 1],
                 in1=sg[:, :, t0:t1], op0=ALU.mult, op1=ALU.mult)

        att_stage(0)
        pend = None
        for b in range(NB):
            gb = gb0 + b
            t0, t1 = b * 128, (b + 1) * 128
            if gb in seq_first_blocks:
                if group == "ctx":
                    P.record("pool", lambda e: e.memset(S32.ap[0:64], 0.0), [], [S32])
                else:
                    P.dma(S32[0:64], V(sgla, sgla.ap[l, 0:64]))
            if b == 0 or gb in seq_first_blocks:
                P.act(Scat[0:64, b], S32[0:64], AF.Copy)
            state_update(b, (0, 64), Scat[0:64, b + 1] if b + 1 < NB else None)
            if gb in seq_last_blocks and group == "ctx":
                si = seq_last_blocks.index(gb)
                P.dma(V(nst, nst.ap[si, l, 0].rearrange("h d v -> d h v")), S32[0:64])
            if b + 1 < NB:
                att_stage(b + 1)
            if pend is not None:
                norm_tail(*pend)
            Ab = Abf[b % 2]
            po = bank()
            pov = pv(po, h=4)
            for h in range(4):
                P.mm(pov[:, h, :], vtok[:, b, h * 128:(h + 1) * 128], Ab[:, h, :], start=True, stop=False)
                P.mm(pov[:, h, :], vtok[:, b, h * 128:(h + 1) * 128], Ab[:, 4 + h, :], start=False, stop=False)
                P.mm(pov[:, h, :], Scat[:, b, h, :], qcat[:, h, t0:t1], start=False, stop=True)
            P.act(sqb, po, AF.Square)
            pss = bank()
            P.mm(pss, ones_bf, sqb)
            P.act(lnvb[b % 2], pss, AF.Ln, scale=1.0 / 128, bias=EPS)
            P.act(lnvb[b % 2], lnvb[b % 2], AF.Exp, scale=-0.5)
            pend = (b, po)
        norm_tail(*pend)
        if stop == "glacore":
            return
        flat = nseg * pad - 30
        n2 = flat - 512
        for c in range(2):
            py = bank(2)
            for j in range(31):
                P.mm(py[:, 0:512], diag[:, c, j, :], xcp[:, c, j:j + 512], start=(j == 0), stop=(j == 30))
            for j in range(31):
                P.mm(py[:, 512:512 + n2], diag[:, c, j, :], xcp[:, c, 512 + j:512 + j + n2], start=(j == 0), stop=(j == 30))
            pyv = V(py.tile, py.ap[:, 0:nseg * pad].rearrange("p (s w) -> p s w", w=pad))[:, :, 0:seglen]
            P.act(V(y32, y32.ap[:, c, :].rearrange("p (s w) -> p s w", w=seglen)), pyv, AF.Copy)
            P.act(V(ysq, ysq.ap[:, c, :].rearrange("p (s w) -> p s w", w=seglen)), pyv, AF.Square)
        p1 = bank()
        p2 = bank()
        for c in range(2):
            P.mm(p1, ones32, y32[:, c, :], start=(c == 0), stop=(c == 1))
        for c in range(2):
            P.mm(p2, ones32, ysq[:, c, :], start=(c == 0), stop=(c == 1))
        P.op("dve", "tensor_scalar", out=m32, in0=p1, scalar1=1.0 / 256, scalar2=None, op0=ALU.mult)
        P.op("pool", "tensor_tensor", out=msq, in0=m32, in1=m32, op=ALU.mult)
        P.op("dve", "scalar_tensor_tensor", out=var, in0=p2, scalar=1.0 / 256, in1=msq, op0=ALU.mult, op1=ALU.subtract)
        P.act(var, var, AF.Ln, bias=EPS)
        P.act(var, var, AF.Exp, scale=-0.5)
        for c in range(2):
            P.op("pool", "tensor_tensor", out=yc, in0=y32[:, c, :], in1=m32, op=ALU.subtract)
            P.op("dve", "tensor_tensor", out=yc, in0=yc, in1=var, op=ALU.mult)
            P.op("dve", "tensor_scalar", out=z32, in0=yc, scalar1=blg[:, l, c:c + 1], scalar2=blb[:, l, c:c + 1],
                 op0=ALU.mult, op1=ALU.add)
            th = th32[c % 2]
            P.act(th, z32, AF.Tanh, scale=0.5)
            P.op("dve", "scalar_tensor_tensor", out=mixT[:, 4 + c, :], in0=th, scalar=1.0, in1=z32, op0=ALU.add, op1=ALU.mult)
        if stop == "conv":
            return
        for b in range(NB):
            t0, t1 = b * 128, (b + 1) * 128
            pg = bank()
            pgv = pv(pg, a=2)
            for h in range(4):
                o = pgv[(h % 2) * 64:(h % 2) * 64 + 64, h // 2, 0:128]
                tp = {"tile_position": (0, 64)} if h % 2 == 1 else {}
                P.mm(o, vn[:, b, h * 64:(h + 1) * 64], wst_bf[:, l, h, :], start=True, stop=False, **tp)
                P.mm(o, ones_bf[0:1, 0:64], bs_hi[:, l, h * 128:(h + 1) * 128], start=False, stop=False, **tp)
                P.mm(o, ones_bf[0:1, 0:64], bs_lo[:, l, h * 128:(h + 1) * 128], start=False, stop=True, **tp)
            P.op("dve", "scalar_tensor_tensor", out=mixC[:, :, t0:t1], in0=pgv[:, :, 0:128], scalar=2.0,
                 in1=u_g[:, :, t0:t1], op0=ALU.mult, op1=ALU.mult)
        if stop == "gmlp":
            return
        wo = [wload("wo0", wo_s[l][0], 4096), wload("wo1", wo_s[l][1], 4096)]
        P.record("pool", lambda e: e.memset(ssn.ap, 0.0), [], [ssn])
        kk = [0]

        def wout_block(b):
            for nh in range(2):
                ps = bank()
                for kc in range(8):
                    wv_ = V(wo[kc // 4], wo[kc // 4].ap[:, 0:4096].rearrange("p (k n) -> p k n", n=D))
                    mx = mixT[:, kc, b * 128:(b + 1) * 128] if kc < 6 else mixC[:, kc - 6, b * 128:(b + 1) * 128]
                    P.mm(ps, mx, wv_[:, kc % 4, nh * 512:(nh + 1) * 512], start=(kc == 0), stop=(kc == 7))
                tt = tmp32[kk[0] % 2]
                kk[0] += 1
                P.op("dve", "tensor_tensor", out=tt, in0=ps, in1=gate_t[0][:, nh * 512:(nh + 1) * 512], op=ALU.mult)
                P.op("dve", "tensor_tensor", out=xt[:, b, nh * 512:(nh + 1) * 512],
                     in0=xt[:, b, nh * 512:(nh + 1) * 512], in1=tt, op=ALU.add)
            norm_sq(xt, ssn, b)
            P.act(rsn[:, b:b + 1], ssn[:, b:b + 1], AF.Ln, scale=1.0 / D, bias=EPS)
            P.act(rsn[:, b:b + 1], rsn[:, b:b + 1], AF.Exp, scale=-0.5)
            norm_xn(xt, rsn, b, nbuf=3)

        wout_block(0)
        wout_block(1)
        for b in range(NB):
            if b + 2 < NB:
                wout_block(b + 2)
            if b + 2 == NB - 1:
                wfree(wo[0])
                wfree(wo[1])
            norm_block(l, cond, 2, xt, rsn, b, ahead=False)
        if stop == "wout":
            return
        dbgout("x1", xt, [128, 4, 1024], F32, l, t)
        nxt.stats()
        for s in range(8):
            dcast(1 if s % 2 == 0 else 0)
            wf = wload("f1", f1_s[l][s], 4096)
            for ch, ps in projA(wf, 0, 512, 4):
                r = rbf[ch % 2]
                P.act(r, ps, AF.Relu)
                P.op("pool" if ch % 2 == 0 else "dve", "tensor_tensor", out=hid[:, s * 4 + ch, :], in0=r, in1=r, op=ALU.mult)
            wfree(wf)
        dcast(-1)
        blo[0] = 4
        for nh in range(2):
            accs = [P.psum_bank(b) for b in range(4)]
            for ks in range(8):
                if nh == 0 and ks % 2 == 1:
                    nxt.block(ks // 2)
                wf = wload("f2", f2_s[l][nh][ks], 2048)
                wv_ = V(wf, wf.ap[:, 0:2048].rearrange("p (k n) -> p k n", n=512))
                for b in range(NB):
                    for k4 in range(4):
                        kc = ks * 4 + k4
                        P.mm(accs[b], hid[:, kc, b * 128:(b + 1) * 128], wv_[:, k4, :], start=(kc == 0), stop=(kc == 31))
                wfree(wf)
            for b in range(NB):
                tt = tmp32[b % 2]
                P.op("dve", "tensor_tensor", out=tt, in0=accs[b], in1=gate_t[1][:, nh * 512:(nh + 1) * 512], op=ALU.mult)
                P.op("pool", "tensor_tensor", out=xt[:, b, nh * 512:(nh + 1) * 512], in0=xt[:, b, nh * 512:(nh + 1) * 512],
                     in1=tt, op=ALU.add)
        blo[0] = 0
        rr[0] = 4
        if stop == "ffn":
            return
        if not last_layer:
            P.dma(V(xmid_t[t], xmid.ap[t * 512:(t + 1) * 512, :].rearrange("(b p) d -> p b d", p=128)), xt)
        else:
            P.record("pool", lambda e: e.memset(ssn.ap, 0.0), [], [ssn])
            for b in range(NB):
                P.act(junk, xt[:, b, :], AF.Square, accum_out=ssn[:, b:b + 1])
            P.act(rsn, ssn, AF.Ln, scale=1.0 / D, bias=EPS)
            P.act(rsn, rsn, AF.Exp, scale=-0.5)
            for b in range(NB):
                P.op("dve", "scalar_tensor_tensor", out=xt[:, b, :], in0=xt[:, b, :], scalar=rsn[:, b:b + 1], in1=fgt,
                     op0=ALU.mult, op1=ALU.mult)
            P.dma(V(y_t[t], y.ap[t * 512:(t + 1) * 512, :].rearrange("(b p) d -> p b d", p=128)), xt)

    def emit_layers():
        nonlocal xt
        jobs = []
        for l in range(nlayers):
            last = (l == nlayers - 1)
            for group in groups:
                cond = 0 if group == "ctx" else 1
                if group == "ctx":
                    tiles = [0]
                    firsts, lasts = [0, 2], [1, 3]
                else:
                    tiles = list(range(1, 1 + nlat))
                    firsts, lasts = [0], [4 * nlat - 1]
                if do_pre:
                    for ti in reversed(range(len(tiles))):
                        jobs.append(("pre", l, cond, tiles[ti], firsts, lasts, ti * 4, group, last))
                if do_main:
                    for ti in range(len(tiles)):
                        jobs.append(("main", l, cond, tiles[ti], firsts, lasts, ti * 4, group, last))

        class Nxt:
            def __init__(self, j):
                self.j = j
                self.ok = j < len(jobs)
                self.st = 0
                self.nb = 0
                if self.ok:
                    self.l, self.cond, self.t = jobs[j][1:4]
                    self.xt = xtb[j % 2]

            def load(self):
                if self.ok and self.st == 0:
                    load_x(self.l, self.t, self.xt)
                    self.st = 1

            def stats(self):
                self.load()
                if self.ok and self.st == 1:
                    norm_stats(self.xt, ssn2, rsn2)
                    self.st = 2

            def block(self, b):
                self.stats()
                if self.ok and self.nb == b:
                    norm_block(self.l, self.cond, 1, self.xt, rsn2, b)
                    self.nb = b + 1

            def finish(self):
                for b in range(NB):
                    self.block(b)

        deferred = []
        if defer_cast and nlayers > 1:
            for l in range(1, nlayers):
                deferred += layer_cast_units(l)
        main_l0 = [j for j, jb in enumerate(jobs) if jb[0] == "main" and jb[1] == 0]

        P.record("pool", lambda e: e.memset(xcp.ap, 0.0), [], [xcp])
        Nxt(0).finish()
        cur_key = None
        for j, (kind, l, cond, t, firsts, lasts, gb0, group, last) in enumerate(jobs):
            if (l, group) != cur_key:
                cur_key = (l, group)
                P.dma(gate_t[0], gates_s[l][cond][0])
                P.dma(gate_t[1], gates_s[l][cond][1])
                P.record("pool", lambda e: e.memset(xcp.ap, 0.0), [], [xcp])
            if l >= 1:
                assert not deferred
            xt = xtb[j % 2]
            nxt = Nxt(j + 1)
            is_last_l0 = bool(main_l0) and j == main_l0[-1]

            def dcast(n, is_last_l0=is_last_l0, l=l):
                if l != 0:
                    return
                if n < 0:
                    n = len(deferred) if is_last_l0 else 0
                for _ in range(n):
                    if deferred:
                        cast_unit(deferred.pop(0), cst)

            if kind == "pre":
                prepass(l, cond, t, firsts, lasts, gb0, group, nxt, dcast)
            else:
                mainpass(l, cond, t, firsts, lasts, gb0, group, last, nxt, dcast)
            nxt.finish()

    P.dry = True
    emit_layers()
    P.dry = False
    rr[0] = 0
    evc[0] = 0
    wstate["n"] = 0
    wstate["issued"] = 0
    wstate["freed"] = set()
    wstate["cur"] = -1
    emit_layers()
    P.finalize()
    print("ops per engine (n, waits):", P.stats)
    return nc, dbg_outs


_CACHE = {}


def _prep_inputs(inp):
    f = np.float32
    g = {k: np.asarray(v) for k, v in inp.items()}
    w_in = g["w_in"]
    qi = np.arange(0, 256)
    ki = np.arange(256, 512)
    vi = np.arange(512, 1024)
    gi = np.arange(1024, 1536)
    lri = np.arange(1536, 1568)
    gai = np.arange(1568, 1824)
    gbi = np.arange(1824, 2080)
    ui = np.arange(2080, 2336)
    vgi = np.arange(2336, 2592)
    qdup = np.concatenate([np.concatenate([qi[h * 64:(h + 1) * 64]] * 2) for h in range(4)])
    kdup = np.concatenate([np.concatenate([ki[h * 64:(h + 1) * 64]] * 2) for h in range(4)])
    idx = np.concatenate([qdup, kdup, gi, vi, gai, gbi, ui, vgi, lri])
    assert idx.size == NCOL

    def kchunk(w):
        Lw, K, N = w.shape
        return np.ascontiguousarray(w.reshape(Lw, K // 128, 128, N).transpose(0, 2, 1, 3))

    shared = {
        "w_in_r": kchunk(w_in[:, :, idx]),
        "w_out_r": kchunk(g["w_out"]),
        "w_ff1_r": kchunk(g["w_ff1"]),
        "w_ff2_r": kchunk(g["w_ff2"]),
        "w_mod_r": kchunk(g["w_mod"]),
        "b_mod": np.ascontiguousarray(g["b_mod"]),
    }
    wag = g["w_a_gate"]
    bag = g["b_a_gate"]
    wgm = np.zeros((L, 33, 2, 4, 64), f)
    for z in range(2):
        wgm[:, z * 16:(z + 1) * 16, z, :, :] = wag[:, z].reshape(L, 16, 4, 64)
        wgm[:, 32, z, :, :] = bag[:, z].reshape(L, 4, 64)
    shared["wg"] = wgm.reshape(L, 33, 512)
    shared["n1g"] = np.ascontiguousarray(g["norm1_g"].reshape(L, 8, 128).transpose(0, 2, 1))
    shared["n2g"] = np.ascontiguousarray(g["norm2_g"].reshape(L, 8, 128).transpose(0, 2, 1))
    shared["ang"] = np.ascontiguousarray(g["a_norm_g"].T)
    shared["wdw"] = np.ascontiguousarray(g["w_dw"].reshape(L, 31, 2, 128).transpose(0, 3, 2, 1))
    shared["blng"] = np.ascontiguousarray(g["b_ln_g"].reshape(L, 2, 128).transpose(0, 2, 1))
    shared["blnb"] = np.ascontiguousarray(g["b_ln_b"].reshape(L, 2, 128).transpose(0, 2, 1))
    shared["clng"] = np.ascontiguousarray(g["c_ln_g"])
    shared["clnb"] = np.ascontiguousarray(g["c_ln_b"])
    shared["wst"] = np.ascontiguousarray(g["w_s"].transpose(0, 3, 1, 2))
    shared["bs"] = np.ascontiguousarray(g["b_s"].reshape(L, 1, 512))
    shared["fg"] = np.ascontiguousarray(g["final_g"])
    shared = {k: np.ascontiguousarray(v.astype(f)) for k, v in shared.items()}
    maps = []
    for i in range(8):
        m = dict(shared)
        m["xin"] = np.ascontiguousarray(np.concatenate(
            [g["x_prompt"][2 * i], g["x_prompt"][2 * i + 1], g["x_sample"][i]], axis=0).astype(f))
        cv = np.stack([g["c_ctx"], g["c"][i]], axis=-1).astype(f)
        m["cvec"] = np.ascontiguousarray(cv.reshape(8, 128, 2).transpose(1, 0, 2))
        sg_ = g["state_gla"][i]
        m["sgla"] = np.ascontiguousarray(sg_.transpose(0, 1, 3, 2, 4).reshape(L, 128, 4, 128).astype(f))
        maps.append(m)
    return maps


def kernel(**inputs):
    if "nc" not in _CACHE:
        _CACHE["nc"] = build()[0]
    nc = _CACHE["nc"]
    maps = _prep_inputs(inputs)
    res = run_bass_kernel_spmd(nc, maps, core_ids=list(range(8)))
    yp = np.zeros((16, 256, D), np.float32)
    ys = np.zeros((8, 4096, D), np.float32)
    ns = np.zeros((16, L, 2, 4, 64, 128), np.float32)
    for i in range(8):
        r = res.results[i]
        yy = np.asarray(r["y"])
        yp[2 * i] = yy[0:256]
        yp[2 * i + 1] = yy[256:512]
        ys[i] = yy[512:]
        n_ = np.asarray(r["nst"])
        ns[2 * i] = n_[0]
        ns[2 * i + 1] = n_[1]
    return yp, ys, ns
```

```python
import numpy as np
import concourse.bass as bass
import concourse.mybir as mybir

F32 = mybir.dt.float32
BF16 = mybir.dt.bfloat16
U8 = mybir.dt.uint8
I32 = mybir.dt.int32
AF = mybir.ActivationFunctionType
ALU = mybir.AluOpType
AX = mybir.AxisListType
DT_SIZE = {F32: 4, BF16: 2, U8: 1, I32: 4}
NSLOTS = 24


class T:
    def __init__(self, ap, space, plo, phi, lo, hi):
        self.ap = ap
        self.tile = self
        self.region = (space, plo, phi, lo, hi)

    def __getitem__(self, k):
        return V(self, self.ap[k])

    def v(self, ap):
        return V(self, ap)


class V:
    def __init__(self, tile, ap):
        self.tile = tile
        self.ap = ap

    def __getitem__(self, k):
        return V(self.tile, self.ap[k])


class Grid:
    def __init__(self, t, n):
        self.ap = t.ap
        space, plo, phi, lo, hi = t.region
        sz = (hi - lo) // n
        self.n = n
        self.subs = [T(t.ap[:, i], space, plo, phi, lo + i * sz, lo + (i + 1) * sz) for i in range(n)]
        self.tile = self.subs

    def __getitem__(self, key):
        k = key[1] if isinstance(key, tuple) and len(key) > 1 else slice(None)
        idx = [k] if isinstance(k, int) else list(range(*k.indices(self.n)))
        return V([self.subs[i] for i in idx], self.ap[key])


class Arena:
    def __init__(self, nc, name, nbytes, space):
        self.space = space
        self.nbytes = nbytes
        self.h = nc.alloc_sbuf_tensor(name, [128, nbytes], U8)
        self.ap = self.h.ap()
        self.off = 0
        self.peak = 0

    def mark(self):
        return self.off

    def release(self, m):
        self.off = m

    def alloc(self, shape, dtype, nparts=128, pbase=0):
        if isinstance(shape, int):
            shape = (shape,)
        n = int(np.prod(shape)) * DT_SIZE[dtype]
        off = (self.off + 31) // 32 * 32
        assert off + n <= self.nbytes, f"arena {self.space} overflow: {off + n} > {self.nbytes}"
        self.off = off + n
        self.peak = max(self.peak, self.off)
        v = self.ap[pbase:pbase + nparts, off:off + n].bitcast(dtype)
        if len(shape) > 1:
            names = [f"d{i}" for i in range(len(shape))]
            s = f"p ({' '.join(names)}) -> p {' '.join(names)}"
            v = v.rearrange(s, **{nm: int(sz) for nm, sz in zip(names, shape)})
        return T(v, self.space, pbase, pbase + nparts, off, off + n)


class Op:
    __slots__ = ("eng", "emit", "deps", "is_dma", "sigval", "signaler", "slot", "waits")

    def __init__(self, eng, emit, deps, is_dma):
        self.eng = eng
        self.emit = emit
        self.deps = deps
        self.is_dma = is_dma
        self.sigval = 0
        self.signaler = is_dma
        self.slot = None
        self.waits = None


class Prog:
    ENGS = ("pe", "act", "dve", "pool", "sp")

    def __init__(self, nc):
        self.nc = nc
        self.ops = []
        self.records = {}
        self.psum_h = nc.alloc_psum_tensor("psum_all", [128, 4096], F32)
        self.psum_ap = self.psum_h.ap()
        self.n_dram = 0
        self.dry = False

    def psum_bank(self, b, nbanks=1):
        ap = self.psum_ap[:, b * 512:(b + nbanks) * 512]
        return T(ap, "psum", 0, 128, b * 2048, (b + nbanks) * 2048)

    def dram(self, name, shape, dtype, kind="Internal"):
        h = self.nc.dram_tensor(name, list(shape), dtype, kind=kind)
        return T(h.ap(), "dram:" + name, 0, 1, 0, 1)

    def dram_sub(self, t, ap, lo, hi):
        sp = t.region[0]
        return T(ap, sp, 0, 1, lo, hi)

    def _deps_for(self, region, is_write, opid, deps):
        space, plo, phi, lo, hi = region
        recs = self.records.setdefault(space, [])
        keep = []
        eng = self.ops_eng
        for r in recs:
            overlap = r[0] < phi and plo < r[1] and r[2] < hi and lo < r[3]
            if not overlap:
                keep.append(r)
                continue
            conflict = is_write or r[5]
            if conflict:
                deps.add(r[4])
            contained = plo <= r[0] and r[1] <= phi and lo <= r[2] and r[3] <= hi
            if is_write and contained:
                continue
            if (not is_write) and (not r[5]) and contained and self.ops[r[4]].eng == eng \
                    and not self.ops[r[4]].is_dma and not self.cur_is_dma:
                continue
            keep.append(r)
        self.records[space] = keep

    def record(self, eng, emit, reads, writes, is_dma=False):
        if self.dry:
            return None
        opid = len(self.ops)
        deps = set()
        self.ops_eng = eng
        self.cur_is_dma = is_dma
        seen = set()
        for t in writes:
            self._deps_for(t.region, True, opid, deps)
        for t in reads:
            self._deps_for(t.region, False, opid, deps)
        op = Op(eng, emit, deps, is_dma)
        self.ops.append(op)
        for t in writes:
            sp, plo, phi, lo, hi = t.region
            self.records[sp].append([plo, phi, lo, hi, opid, True])
        for t in reads:
            key = id(t)
            if key in seen:
                continue
            seen.add(key)
            sp, plo, phi, lo, hi = t.region
            self.records[sp].append([plo, phi, lo, hi, opid, False])
        return op

    def op(self, eng, method, **kw):
        writes, reads, args = [], [], {}
        for k, v in kw.items():
            if isinstance(v, (T, V, Grid)):
                tl = v.tile if isinstance(v.tile, list) else [v.tile]
                (writes if k in ("out", "accum_out") else reads).extend(tl)
                args[k] = v.ap
            else:
                args[k] = v
        if method == "matmul" and kw.get("start") is False:
            ot = kw["out"].tile
            reads.extend(ot if isinstance(ot, list) else [ot])
        is_dma = method == "dma_start"
        return self.record(eng, lambda e: getattr(e, method)(**args), reads, writes, is_dma)

    def mm(self, out, lhsT, rhs, start=True, stop=True, **kw):
        return self.op("pe", "matmul", out=out, lhsT=lhsT, rhs=rhs, start=start, stop=stop, **kw)

    def transpose(self, out, in_, identity):
        return self.op("pe", "transpose", out=out, in_=in_, identity=identity)

    def dma(self, out, in_, eng="sp", **kw):
        return self.op(eng, "dma_start", out=out, in_=in_, **kw)

    def act(self, out, in_, func, eng="act", **kw):
        return self.op(eng, "activation", out=out, in_=in_, func=func, **kw)

    def finalize(self):
        nc = self.nc
        ops = self.ops
        for op in ops:
            for d in op.deps:
                dop = ops[d]
                if dop.eng == "pe" and op.eng == "pe" and not dop.is_dma and not op.is_dma:
                    continue
                dop.signaler = True
        sig = {e: 0 for e in self.ENGS}
        slotcnt = [0] * NSLOTS
        ndma = 0
        nsw = 0
        for op in ops:
            if op.is_dma and op.eng == "pool":
                op.slot = ("sw", nsw)
                nsw += 1
                op.sigval = 16
            elif op.is_dma:
                op.slot = ndma % NSLOTS
                ndma += 1
                slotcnt[op.slot] += 16
                op.sigval = slotcnt[op.slot]
            elif op.signaler:
                sig[op.eng] += 1
                op.sigval = sig[op.eng]
        seen = {e: {} for e in self.ENGS}
        for op in ops:
            need = {}
            for d in op.deps:
                dop = ops[d]
                if dop.eng == "pe" and op.eng == "pe" and not dop.is_dma and not op.is_dma:
                    continue
                key = ("dma", dop.slot) if dop.is_dma else dop.eng
                need[key] = max(need.get(key, 0), dop.sigval)
            if op.is_dma and op.sigval > 16:
                key = ("dma", op.slot)
                need[key] = max(need.get(key, 0), op.sigval - 16)
            w = []
            s = seen[op.eng]
            for key, val in need.items():
                if s.get(key, 0) < val:
                    s[key] = val
                    w.append((key, val))
            op.waits = w
        self.final_slot = slotcnt
        self.final_sig = sig
        self.sems = {e: nc.alloc_semaphore("sem_" + e) for e in self.ENGS}
        for i in range(NSLOTS):
            self.sems[("dma", i)] = nc.alloc_semaphore(f"sem_dma{i}")
        for i in range(nsw):
            self.sems[("dma", ("sw", i))] = nc.alloc_semaphore(f"sem_swdma{i}")
        per = {e: [op for op in ops if op.eng == e] for e in self.ENGS}
        self.stats = {e: (len(per[e]), sum(len(o.waits) for o in per[e])) for e in self.ENGS}
        sems = self.sems

        def run(engine, name):
            for op in per[name]:
                for key, val in op.waits:
                    engine.wait_ge(sems[key], val)
                ins = op.emit(engine)
                if op.is_dma:
                    ins.then_inc(sems[("dma", op.slot)], 16)
                elif op.signaler:
                    ins.then_inc(sems[name], 1)
            if name == "sp":
                for i in range(NSLOTS):
                    if slotcnt[i] > 0:
                        engine.wait_ge(sems[("dma", i)], slotcnt[i])
                for i in range(nsw):
                    engine.wait_ge(sems[("dma", ("sw", i))], 16)
                for e in ("pe", "act", "dve", "pool"):
                    if sig[e] > 0:
                        engine.wait_ge(sems[e], sig[e])

        with nc.Block() as block:
            @block.tensor
            def _(e):
                run(e, "pe")

            @block.scalar
            def _(e):
                run(e, "act")

            @block.vector
            def _(e):
                run(e, "dve")

            @block.gpsimd
            def _(e):
                run(e, "pool")

            @block.sync
            def _(e):
                run(e, "sp")

from concourse.bass_utils import run_bass_kernel_spmd

D = 1024
L = 2
NTOK = 4608
NB = 4
DFF = 4096
EPS = 1e-6
QO, KO, GO, VO, GAO, GBO, UO, VGO, LRO, NCOL = 0, 512, 1024, 1536, 2048, 2304, 2560, 2816, 3072, 3104
WI_SLABS = [(0, 512), (512, 1024), (1024, 1536), (1536, 2048), (2048, 2560), (2560, 3104)]
SLOT_COLS = 8 * 544
NSLOT = 3


def build(nlayers=L, dbg=None, groups=("ctx", "lat"), do_pre=True, do_main=True, nlat=8, stop=None, defer_cast=True):
    nc = bass.Bass("TRN2", target_bir_lowering=False)
    P = Prog(nc)

    def inp(name, shape, dt=F32):
        return P.dram(name, shape, dt, kind="ExternalInput")

    xin = inp("xin", [NTOK, D])
    cvec = inp("cvec", [128, 8, 2])
    sgla = inp("sgla", [L, 128, 4, 128])
    w_in_r = inp("w_in_r", [L, 128, 8, NCOL])
    w_out_r = inp("w_out_r", [L, 128, 8, D])
    w_ff1_r = inp("w_ff1_r", [L, 128, 8, DFF])
    w_ff2_r = inp("w_ff2_r", [L, 128, 32, D])
    w_mod_r = inp("w_mod_r", [L, 128, 8, 6 * D])
    b_mod = inp("b_mod", [L, 6 * D])
    wg = inp("wg", [L, 33, 512])
    n1g = inp("n1g", [L, 128, 8])
    n2g = inp("n2g", [L, 128, 8])
    ang = inp("ang", [128, L])
    wdw = inp("wdw", [L, 128, 2, 31])
    blng = inp("blng", [L, 128, 2])
    blnb = inp("blnb", [L, 128, 2])
    clng = inp("clng", [L, 256])
    clnb = inp("clnb", [L, 256])
    wst = inp("wst", [L, 128, 4, 128])
    bs = inp("bs", [L, 1, 512])
    fg = inp("fg", [D])
    y = P.dram("y", [NTOK, D], F32, kind="ExternalOutput")
    nst = P.dram("nst", [2, L, 2, 4, 64, 128], F32, kind="ExternalOutput")
    dbg_outs = {}

    wi_s = [[P.dram(f"wi{l}_{s}", [128, 8 * (c1 - c0)], BF16) for s, (c0, c1) in enumerate(WI_SLABS)] for l in range(L)]
    wo_s = [[P.dram(f"wo{l}_{s}", [128, 4 * D], BF16) for s in range(2)] for l in range(L)]
    f1_s = [[P.dram(f"f1{l}_{s}", [128, 8 * 512], BF16) for s in range(8)] for l in range(L)]
    f2_s = [[[P.dram(f"f2{l}_{nh}_{ks}", [128, 4 * 512], BF16) for ks in range(8)] for nh in range(2)] for l in range(L)]
    gates_s = [[[P.dram(f"gate{l}_{c}_{g}", [128, D], F32) for g in range(2)] for c in range(2)] for l in range(L)]
    xmid = P.dram("xmid", [NTOK, D], F32)
    sb_scr = P.dram("sb_scr", [32, 64, 512], BF16)
    xmid_t = [P.dram_sub(xmid, xmid.ap[t * 512:(t + 1) * 512, :], t, t + 1) for t in range(9)]
    y_t = [P.dram_sub(y, y.ap[t * 512:(t + 1) * 512, :], t, t + 1) for t in range(9)]
    sb_t = [P.dram_sub(sb_scr, sb_scr.ap[b], b, b + 1) for b in range(32)]

    A = Arena(nc, "arena", 211712, "sb")
    rr = [0]

    blo = [0]

    def bank(n=1):
        if blo[0]:
            assert n == 1
            b = blo[0] + rr[0] % (8 - blo[0])
            rr[0] = rr[0] + 1
        elif n == 2:
            b = ((rr[0] + 1) // 2 * 2) % 8
            rr[0] = b + 2
        else:
            b = rr[0] % 8
            rr[0] = b + 1
        return P.psum_bank(b, n)

    def pv(ps, **kw):
        names = list(kw.keys())
        if len(names) == 1:
            s = f"p ({names[0]} ww) -> p {names[0]} ww"
        else:
            s = f"p ({names[0]} {names[1]} ww) -> p {names[0]} {names[1]} ww"
        return V(ps.tile, ps.ap.rearrange(s, **kw))

    evc = [0]

    def copy(out, in_, scale=None):
        evc[0] += 1
        if evc[0] % 2 == 0:
            if scale is None:
                P.act(out, in_, AF.Copy)
            else:
                P.act(out, in_, AF.Copy, scale=float(scale))
        else:
            if scale is None:
                P.op("dve", "tensor_copy", out=out, in_=in_)
            else:
                P.op("dve", "tensor_scalar", out=out, in0=in_, scalar1=float(scale), scalar2=None, op0=ALU.mult)

    ident = A.alloc(128, BF16)
    ones_bf = A.alloc(128, BF16)
    ones32 = A.alloc(128, F32)
    triL = A.alloc(128, F32)
    triU = A.alloc(128, F32)
    negcol = A.alloc(1, F32)
    mask = A.alloc((8, 128), BF16)
    sel = A.alloc((2, 128), F32, nparts=2)
    onehot = A.alloc(2, F32, nparts=2)
    m0 = A.mark()
    tmpc = A.alloc(128, F32)
    P.record("pool", lambda e: e.memset(ones32.ap, 1.0), [], [ones32])
    P.record("pool", lambda e: e.memset(negcol.ap, -1.0 / 16), [], [negcol])
    P.op("dve", "tensor_copy", out=ones_bf, in_=ones32)
    P.record("pool", lambda e: e.affine_select(out=tmpc.ap, in_=ones32.ap, pattern=[[-1, 128]], compare_op=ALU.is_equal,
                                               fill=0.0, base=0, channel_multiplier=1), [ones32], [tmpc])
    P.op("dve", "tensor_copy", out=ident, in_=tmpc)
    tl1 = A.alloc(128, F32)
    tu1 = A.alloc(128, F32)
    P.record("pool", lambda e: e.affine_select(out=tl1.ap, in_=ones32.ap, pattern=[[1, 128]], compare_op=ALU.is_ge,
                                               fill=0.0, base=0, channel_multiplier=-1), [ones32], [tl1])
    P.record("pool", lambda e: e.affine_select(out=tu1.ap, in_=ones32.ap, pattern=[[-1, 128]], compare_op=ALU.is_ge,
                                               fill=0.0, base=0, channel_multiplier=1), [ones32], [tu1])
    P.op("dve", "tensor_scalar", out=triL, in0=tl1, scalar1=-1.0 / 16, scalar2=None, op0=ALU.mult)
    P.op("dve", "tensor_scalar", out=triU, in0=tu1, scalar1=-1.0 / 16, scalar2=None, op0=ALU.mult)
    for hd in range(8):
        P.op("dve", "tensor_copy", out=mask[:, hd, :], in_=(tl1 if hd < 4 else tu1))
    P.record("pool", lambda e: e.affine_select(out=sel.ap, in_=ones32.ap[0:2, :].unsqueeze(1).to_broadcast([2, 2, 128]),
                                               pattern=[[-1, 2], [0, 128]], compare_op=ALU.is_equal,
                                               fill=0.0, base=0, channel_multiplier=1), [ones32], [sel])
    P.record("pool", lambda e: e.affine_select(out=onehot.ap, in_=ones32.ap[0:2, 0:2], pattern=[[-1, 2]],
                                               compare_op=ALU.is_equal, fill=0.0, base=0, channel_multiplier=1),
             [ones32], [onehot])
    A.release(m0)

    wg_bf = A.alloc((L, 512), BF16, nparts=33)
    wst_bf = A.alloc((L, 4, 128), BF16)
    bs_hi = A.alloc((L, 512), BF16, nparts=1)
    bs_lo = A.alloc((L, 512), BF16, nparts=1)
    cG = A.alloc((L, 256), F32)
    cB = A.alloc((L, 256), F32)
    fgt = A.alloc(D, F32)
    modc = A.alloc((L, 2, 4, 8), F32)
    angc = A.alloc(L, F32)
    blg = A.alloc((L, 2), F32)
    blb = A.alloc((L, 2), F32)
    blgh = A.alloc((L, 2), F32)
    blbh = A.alloc((L, 2), F32)
    wdw_t = A.alloc((L, 2, 31), F32)
    n1g_t = A.alloc((L, 8), F32)
    n2g_t = A.alloc((L, 8), F32)
    cT = A.alloc((8, 2), F32)
    sc = A.alloc((8, 2), F32)
    m0 = A.mark()
    st_wg = A.alloc((L, 512), F32, nparts=33)
    st_ws = A.alloc((L, 4, 128), F32)
    st_bs = A.alloc((L, 512), F32, nparts=1)
    st_bs2 = A.alloc((L, 512), F32, nparts=1)
    P.dma(st_wg, V(wg, wg.ap.rearrange("l k n -> k l n")))
    P.dma(st_ws, V(wst, wst.ap.rearrange("l q h p -> q l h p")))
    P.dma(st_bs, V(bs, bs.ap.rearrange("l o n -> o l n")))
    P.dma(cG, V(clng, clng.ap.partition_broadcast(128)))
    P.dma(cB, V(clnb, clnb.ap.partition_broadcast(128)))
    P.dma(fgt, V(fg, fg.ap.partition_broadcast(128)))
    P.dma(angc, ang)
    P.dma(blg, V(blng, blng.ap.rearrange("l p c -> p l c")))
    P.dma(blb, V(blnb, blnb.ap.rearrange("l p c -> p l c")))
    P.dma(wdw_t, V(wdw, wdw.ap.rearrange("l p c j -> p l c j")))
    P.dma(n1g_t, V(n1g, n1g.ap.rearrange("l p k -> p l k")))
    P.dma(n2g_t, V(n2g, n2g.ap.rearrange("l p k -> p l k")))
    P.dma(cT, cvec)
    P.op("dve", "tensor_copy", out=wg_bf, in_=st_wg)
    P.op("dve", "tensor_copy", out=wst_bf, in_=st_ws)
    P.op("dve", "tensor_copy", out=bs_hi, in_=st_bs)
    P.op("dve", "tensor_tensor", out=st_bs2, in0=st_bs, in1=bs_hi, op=ALU.subtract)
    P.op("dve", "tensor_copy", out=bs_lo, in_=st_bs2)
    P.op("dve", "tensor_scalar", out=blgh, in0=blg, scalar1=0.5, scalar2=None, op0=ALU.mult)
    P.op("dve", "tensor_scalar", out=blbh, in0=blb, scalar1=0.5, scalar2=None, op0=ALU.mult)
    P.op("dve", "tensor_scalar", out=wdw_t, in0=wdw_t, scalar1=0.5, scalar2=None, op0=ALU.mult)
    thc = A.alloc((8, 2), F32)
    P.act(thc, cT, AF.Tanh, scale=0.5)
    P.op("dve", "scalar_tensor_tensor", out=sc, in0=thc, scalar=1.0, in1=cT, op0=ALU.add, op1=ALU.mult)
    P.op("dve", "tensor_scalar", out=sc, in0=sc, scalar1=0.5, scalar2=None, op0=ALU.mult)
    A.release(m0)

    m0 = A.mark()
    modrow = A.alloc(6 * D, F32, nparts=2)
    brow = A.alloc(6 * D, F32, nparts=2)
    wm = [A.alloc((8, 512), F32) for _ in range(2)]
    gst = [A.alloc(512, F32) for _ in range(2)]
    mcs = A.alloc(64, F32)
    s32 = [A.alloc(4096, F32) for _ in range(4)]
    s16 = [A.alloc(4096, BF16) for _ in range(4)]
    ci = [0]

    def cast_units(src_ap, src_t, dst, kc, ncols, scale=None):
        per = max(1, 4096 // ncols)
        return [(src_ap, src_t, dst, k0, min(kc, k0 + per), ncols, scale) for k0 in range(0, kc, per)]

    def cast_unit(u, st=None):
        src_ap, src_t, dst, k0, k1, ncols, scale = u
        st32, st16 = st if st is not None else (s32, s16)
        if True:
            n = (k1 - k0) * ncols
            a = st32[ci[0] % len(st32)]
            b = st16[ci[0] % len(st16)]
            ci[0] += 1
            P.dma(V(a, a.ap[:, 0:n].rearrange("p (k n) -> p k n", n=ncols)), V(src_t, src_ap[:, k0:k1, :]))
            e = ci[0] % 2
            if scale is None:
                if e == 0:
                    P.act(b[:, 0:n], a[:, 0:n], AF.Copy)
                elif e == 1:
                    P.op("dve", "tensor_copy", out=b[:, 0:n], in_=a[:, 0:n])
                else:
                    P.op("pool", "tensor_copy", out=b[:, 0:n], in_=a[:, 0:n])
            else:
                if e == 0:
                    P.act(b[:, 0:n], a[:, 0:n], AF.Copy, scale=float(scale))
                else:
                    P.op("dve" if e == 1 else "pool", "tensor_scalar", out=b[:, 0:n], in0=a[:, 0:n],
                         scalar1=float(scale), scalar2=None, op0=ALU.mult)
            P.dma(V(dst, dst.ap[:, k0 * ncols:k1 * ncols]), b[:, 0:n], eng=("pool" if st is None else "sp"))

    def layer_cast_units(l):
        us = []
        for s, (c0, c1) in enumerate(WI_SLABS):
            us += cast_units(w_in_r.ap[l, :, :, c0:c1], w_in_r, wi_s[l][s], 8, c1 - c0, scale=(0.125 if s == 0 else None))
        for s in range(2):
            us += cast_units(w_out_r.ap[l, :, 4 * s:4 * s + 4, :], w_out_r, wo_s[l][s], 4, D, scale=0.5)
        for s in range(8):
            us += cast_units(w_ff1_r.ap[l, :, :, s * 512:(s + 1) * 512], w_ff1_r, f1_s[l][s], 8, 512)
        for nh in range(2):
            for ks in range(8):
                us += cast_units(w_ff2_r.ap[l, :, 4 * ks:4 * ks + 4, nh * 512:(nh + 1) * 512], w_ff2_r, f2_s[l][nh][ks], 4, 512)
        return us

    l0_units = layer_cast_units(0)
    for l in range(nlayers):
        P.dma(brow, V(b_mod, b_mod.ap[l].partition_broadcast(2)))
        for cc in range(12):
            w = wm[cc % 2]
            P.dma(w, V(w_mod_r, w_mod_r.ap[l, :, :, cc * 512:(cc + 1) * 512]), eng="act")
            ps = bank()
            for kc in range(8):
                P.mm(ps[0:2, :], sc[:, kc, :], w[:, kc, :], start=(kc == 0), stop=(kc == 7))
            P.op("dve", "tensor_tensor", out=modrow[:, cc * 512:(cc + 1) * 512], in0=ps[0:2, :],
                 in1=brow[:, cc * 512:(cc + 1) * 512], op=ALU.add)
            for _ in range(2):
                if l0_units:
                    cast_unit(l0_units.pop(0))
        i = 0
        for c in range(2):
            for g, vec in enumerate((2, 5)):
                for nh in range(2):
                    ps = bank()
                    P.mm(ps, sel[:, c, :], modrow[:, vec * D + nh * 512: vec * D + (nh + 1) * 512])
                    st = gst[i % 2]
                    i += 1
                    copy(st, ps)
                    P.dma(V(gates_s[l][c][g], gates_s[l][c][g].ap[:, nh * 512:(nh + 1) * 512]), st)
        ps = bank()
        for c in range(2):
            for vi, vec in enumerate((0, 1, 3, 4)):
                for kc in range(8):
                    j = (c * 4 + vi) * 8 + kc
                    P.mm(ps[:, j:j + 1], modrow[:, vec * D + kc * 128: vec * D + (kc + 1) * 128], onehot[:, c:c + 1])
        P.op("dve", "tensor_copy", out=mcs, in_=ps[:, 0:64])
        mv = V(mcs, mcs.ap.rearrange("p (c v k) -> p c v k", c=2, v=4))
        for c in range(2):
            P.op("dve", "scalar_tensor_tensor", out=modc[:, l, c, 0, :], in0=mv[:, c, 1, :], scalar=1.0,
                 in1=n1g_t[:, l, :], op0=ALU.add, op1=ALU.mult)
            P.op("dve", "tensor_copy", out=modc[:, l, c, 1, :], in_=mv[:, c, 0, :])
            P.op("dve", "scalar_tensor_tensor", out=modc[:, l, c, 2, :], in0=mv[:, c, 3, :], scalar=1.0,
                 in1=n2g_t[:, l, :], op0=ALU.add, op1=ALU.mult)
            P.op("dve", "tensor_copy", out=modc[:, l, c, 3, :], in_=mv[:, c, 2, :])

    while l0_units:
        cast_unit(l0_units.pop(0))
    if not defer_cast:
        for l in range(1, nlayers):
            for u in layer_cast_units(l):
                cast_unit(u)
    A.release(m0)

    gate_t = [A.alloc(D, F32) for _ in range(2)]
    Scat = A.alloc((NB, 4, 128), BF16)
    S32 = A.alloc((4, 128), F32)
    xtb = [Grid(A.alloc((NB, D), F32), NB) for _ in range(2)]
    xt = xtb[0]
    ssn = A.alloc(NB, F32)
    rsn = A.alloc(NB, F32)
    ssn2 = A.alloc(NB, F32)
    rsn2 = A.alloc(NB, F32)
    junk = A.alloc(D, BF16)
    xnb = [A.alloc(D, BF16) for _ in range(3)]
    hT = Grid(A.alloc((8, 512), BF16), 8)
    mixT = Grid(A.alloc((6, 512), BF16), 6)
    mixC = A.alloc((2, 512), BF16)
    tmp32 = [A.alloc(512, F32) for _ in range(2)]
    slots = [A.alloc(SLOT_COLS, BF16) for _ in range(NSLOT)]
    lrT = A.alloc(512, BF16, nparts=33)
    xcp = A.alloc((2, 752), BF16)
    ov = A.mark()
    gv = A.alloc((NB, 256), F32)
    gsq = A.alloc((NB, 256), F32)
    e32 = A.alloc(512, F32)
    sp = [A.alloc(512, F32) for _ in range(2)]
    Ek = A.alloc(512, F32)
    Epos = A.alloc((4, 128), F32)
    Eneg = A.alloc((4, 128), F32)
    qcat = A.alloc((4, 512), BF16)
    kcat = A.alloc((4, 512), BF16)
    kdt = Grid(A.alloc((NB, 512), BF16), NB)
    vtok = Grid(A.alloc((NB, 512), BF16), NB)
    sg = A.alloc((4, 512), BF16)
    dec = A.alloc((NB, 4), F32)
    u_g = A.alloc((2, 512), BF16)
    vst = A.alloc((6, NB), F32)
    vn = A.alloc((NB, 256), BF16)
    vtmp = A.alloc(256, F32)
    Abf = [A.alloc((8, 128), BF16) for _ in range(2)]
    sqb = A.alloc(512, BF16)
    lnvb = [A.alloc(512, F32) for _ in range(2)]
    on32 = A.alloc(512, F32)
    th32 = [A.alloc(512, F32) for _ in range(2)]
    y32 = A.alloc((2, 512), F32)
    ysq = A.alloc((2, 512), F32)
    m32 = A.alloc(512, F32)
    msq = A.alloc(512, F32)
    var = A.alloc(512, F32)
    yc = A.alloc(512, F32)
    z32 = A.alloc(512, F32)
    stS = A.alloc((4, 128), F32)
    ov_end = A.mark()
    A.release(ov)
    diag = A.alloc((2, 31, 128), BF16)
    assert A.off <= ov + 20 * 1024
    A.release(ov)
    hid = A.alloc((32, 512), BF16)
    rbf = [A.alloc(512, BF16) for _ in range(2)]
    cst = ([A.alloc(4096, F32) for _ in range(2)], [A.alloc(4096, BF16) for _ in range(1)])
    print('overlay: mixer', ov_end - ov, 'ffn', A.off - ov)
    assert A.off <= ov_end, (A.off - ov, ov_end - ov)
    A.off = max(A.off, ov_end)
    print("SBUF arena peak bytes/partition:", A.peak, "end", A.off)

    P.record("pool", lambda e: e.memset(lrT.ap[32:33, :], 1.0), [], [lrT])

    wq = []
    wstate = {"n": 0}

    wsched = []
    wstate["issued"] = 0
    wstate["freed"] = set()
    wstate["cur"] = -1
    widx = {}

    def wpump():
        while (wstate["issued"] < len(wsched) and wstate["issued"] <= wstate["cur"] + 2
               and (wstate["issued"] < NSLOT or (wstate["issued"] - NSLOT) in wstate["freed"])):
            i = wstate["issued"]
            s_, n_ = wsched[i]
            P.dma(slots[i % NSLOT][:, 0:n_], s_)
            wstate["issued"] += 1

    def wload(name, src, ncols_total):
        k = wstate["n"]
        wstate["n"] += 1
        widx[id(slots[k % NSLOT])] = k
        if P.dry:
            wsched.append((src, ncols_total))
            return slots[k % NSLOT]
        wstate["cur"] = k
        wpump()
        assert wstate["issued"] > k, ("weight slot not freed in time", name, k)
        return slots[k % NSLOT]

    def wfree(slot):
        if P.dry:
            return
        wstate["freed"].add(widx[id(slot)])
        wpump()

    def dbgout(name, t, shape, dtype=F32, l=0, ti=None):
        if dbg and (not P.dry) and name in dbg and l == 0 and ti == dbg.get("tile", 1) and name not in dbg_outs:
            o = P.dram("dbg_" + name, list(shape), dtype, kind="ExternalOutput")
            P.dma(o, t)
            dbg_outs[name] = o

    def norm_sq(xt_, ss, b):
        P.act(junk, xt_[:, b, :], AF.Square, accum_out=ss[:, b:b + 1])

    def norm_stats(xt_, ss, rs):
        P.record("pool", lambda e: e.memset(ss.ap, 0.0), [], [ss])
        for b in range(NB):
            norm_sq(xt_, ss, b)
        P.act(rs, ss, AF.Ln, scale=1.0 / D, bias=EPS)
        P.act(rs, rs, AF.Exp, scale=-0.5)

    def norm_xn(xt_, rs, b, nbuf=2):
        P.op("dve", "tensor_scalar", out=xnb[b % nbuf], in0=xt_[:, b, :], scalar1=rs[:, b:b + 1], scalar2=None, op0=ALU.mult)

    def norm_block(l, cond, which, xt_, rs, b, ahead=True):
        gi, si = (0, 1) if which == 1 else (2, 3)
        xn = xnb[b % 2] if ahead else xnb[b % 3]
        if ahead and b == 0:
            norm_xn(xt_, rs, 0)
            norm_xn(xt_, rs, 1)
        psb = bank()
        pst = T(psb.ap.bitcast(BF16), *psb.region)
        for kc in range(8):
            P.transpose(pst[:, kc * 128:(kc + 1) * 128], xn[:, kc * 128:(kc + 1) * 128], ident)
        if ahead and b + 2 < NB:
            norm_xn(xt_, rs, b + 2)
        for kc in range(8):
            o = hT[:, kc, b * 128:(b + 1) * 128]
            i = pst[:, kc * 128:(kc + 1) * 128]
            if b % 2 == 0:
                P.op("dve", "tensor_scalar", out=o, in0=i, scalar1=modc[:, l, cond, gi, kc:kc + 1],
                     scalar2=modc[:, l, cond, si, kc:kc + 1], op0=ALU.mult, op1=ALU.add)
            else:
                P.act(o, i, AF.Identity, scale=modc[:, l, cond, gi, kc:kc + 1], bias=modc[:, l, cond, si, kc:kc + 1])

    def norm_to_hT(l, cond, which):
        norm_stats(xt, ssn, rsn)
        for b in range(NB):
            norm_block(l, cond, which, xt, rsn, b)

    def projA(w, c0, ncols, nchunk, mrows=128):
        wv = V(w, w.ap[:, 0:8 * ncols].rearrange("p (k n) -> p k n", n=ncols))
        for ch in range(nchunk):
            ps = bank()
            for kc in range(8):
                P.mm(ps[0:mrows, :], wv[:, kc, c0 + ch * 128: c0 + ch * 128 + mrows], hT[:, kc, :],
                     start=(kc == 0), stop=(kc == 7))
            yield ch, ps

    def projB(w, c0, ncols, n, b):
        wv = V(w, w.ap[:, 0:8 * ncols].rearrange("p (k n) -> p k n", n=ncols))
        ps = bank()
        for kc in range(8):
            P.mm(ps[:, 0:n], hT[:, kc, b * 128:(b + 1) * 128], wv[:, kc, c0:c0 + n], start=(kc == 0), stop=(kc == 7))
        return ps

    def gate_prep(l, b, w_k, need_feat, extra=None):
        t0, t1 = b * 128, (b + 1) * 128
        ps = bank()
        P.mm(ps, lrT[:, t0:t1], wg_bf[:, l, :])
        pk = projB(w_k, 0, 512, 512, b)
        ex = extra() if extra is not None else None
        s = sp[b % 2]
        P.act(e32, ps, AF.Exp, scale=-1.0)
        P.act(s, e32, AF.Ln, bias=1.0)
        pb = bank()
        P.mm(pb[:, 0:256], triL, s[:, 0:256])
        P.mm(pb[:, 256:512], triU, s[:, 256:512])
        if need_feat:
            pf = bank()
            pfv = pv(pf, h=4)
            for h in range(4):
                P.mm(pfv[0:64, h, :], s[:, h * 64:(h + 1) * 64], triL)
                P.mm(pfv[64:128, h, :], s[:, 256 + h * 64:256 + (h + 1) * 64], triU, tile_position=(0, 64))
        else:
            pd = ps
            for h in range(4):
                P.mm(pd[64:128, h:h + 1], s[:, 256 + h * 64:256 + (h + 1) * 64], negcol, tile_position=(0, 64))
        P.act(Ek, pb, AF.Exp, scale=-1.0)
        if need_feat:
            P.act(Epos, pfv, AF.Exp)
            P.act(Eneg, pfv, AF.Exp, scale=-1.0)
            P.act(V(dec, dec.ap[0:64, b, :].unsqueeze(2)), pfv[0:64, :, 127:128], AF.Exp)
            P.act(V(dec, dec.ap[64:128, b, :].unsqueeze(2)), pfv[64:128, :, 0:1], AF.Exp)
        else:
            P.act(dec[64:128, b, :], pd[64:128, 0:4], AF.Exp)
        P.op("dve", "tensor_tensor",
             out=V(kdt[:, b, :].tile, kdt.ap[:, b, :].rearrange("p (h z d) -> p h z d", h=4, z=2)),
             in0=V(pk.tile, pk.ap.rearrange("p (h z d) -> p h z d", h=4, z=2)),
             in1=V(Ek, Ek.ap.rearrange("p (z h d) -> p h z d", z=2, h=4)), op=ALU.mult)
        return ex

    def state_update(b, rows, dst):
        r0, r1 = rows
        pd = bank()
        pdv = pv(pd, h=4)
        for h in range(4):
            P.mm(pdv[:, h, :], kdt[:, b, h * 128:(h + 1) * 128], vtok[:, b, h * 128:(h + 1) * 128])
        P.op("dve", "tensor_tensor", out=S32[r0:r1], in0=S32[r0:r1], in1=pdv[r0:r1], op=ALU.add)
        P.op("dve", "tensor_tensor", out=S32[r0:r1], in0=S32[r0:r1],
             in1=V(dec, dec.ap[r0:r1, b, :].unsqueeze(2).to_broadcast([r1 - r0, 4, 128])), op=ALU.mult)
        if dst is not None:
            P.act(dst, S32[r0:r1], AF.Copy)

    def load_x(l, t, xt_):
        src = xin if l == 0 else xmid_t[t]
        sap = (xin.ap if l == 0 else xmid.ap)[t * 512:(t + 1) * 512, :].rearrange("(b p) d -> p b d", p=128)
        P.dma(xt_, V(src, sap))

    def prepass(l, cond, t, seq_first_blocks, seq_last_blocks, gb0, group, nxt, dcast):
        w5 = wload("wi5", wi_s[l][5], 8 * 544)
        for ch, ps in projA(w5, 512, 544, 1, mrows=32):
            copy(lrT[0:32, :], ps[0:32, :])
        wfree(w5)
        w1 = wload("wi1", wi_s[l][1], 8 * 512)
        w3 = wload("wi3", wi_s[l][3], 8 * 512)
        def chain_step(b):
            gb = gb0 + b
            if gb in seq_last_blocks:
                if group == "ctx":
                    P.record("pool", lambda e: e.memset(S32.ap[64:128], 0.0), [], [S32])
                else:
                    P.dma(S32[64:128], V(sgla, sgla.ap[l, 64:128]))
            sbf = Abf[b % 2]
            sbv = V(sbf, sbf.ap.rearrange("p a b -> p (a b)")[:, 0:512].rearrange("p (h v) -> p h v", h=4))
            P.act(sbv[64:128], S32[64:128], AF.Copy)
            P.dma(V(sb_t[gb], sb_t[gb].ap.rearrange("d (h v) -> d h v", h=4)), sbv[64:128])
            state_update(b, (64, 128), None)
            if gb in seq_first_blocks and group == "ctx":
                si = seq_first_blocks.index(gb)
                P.dma(V(nst, nst.ap[si, l, 1].rearrange("h d v -> d h v")), S32[64:128])

        order = list(reversed(range(NB)))
        for i, b in enumerate(order):
            def v_proj(b=b):
                pvv = projB(w3, 0, 512, 512, b)
                copy(vtok[:, b, :], pvv)
            gate_prep(l, b, w1, False, extra=v_proj)
            if i == 0:
                nxt.load()
            if i == 2:
                nxt.stats()
            if i >= 1:
                chain_step(order[i - 1])
        wfree(w1)
        wfree(w3)
        nxt.finish()
        chain_step(order[-1])

    def mainpass(l, cond, t, seq_first_blocks, seq_last_blocks, gb0, group, last_layer, nxt, dcast):
        nseg, seglen = (2, 256) if group == "ctx" else (8, 64)
        pad = seglen + 30
        w5 = wload("wi5", wi_s[l][5], 8 * 544)
        for ch, ps in projA(w5, 0, 544, 2):
            P.act(u_g[:, ch, :], ps, AF.Gelu_apprx_tanh)
        for ch, ps in projA(w5, 512, 544, 1, mrows=32):
            copy(lrT[0:32, :], ps[0:32, :])
        for b in range(NB):
            ps = projB(w5, 256, 544, 256, b)
            P.act(gv[:, b, :], ps[:, 0:256], AF.Gelu_apprx_tanh)
        wfree(w5)
        if stop == "slab5":
            return
        P.op("pool", "tensor_tensor", out=gsq, in0=gv, in1=gv, op=ALU.mult)
        P.op("dve", "tensor_reduce", out=vst[:, 0, :], in_=gv, axis=AX.X, op=ALU.add)
        P.op("dve", "tensor_reduce", out=vst[:, 1, :], in_=gsq, axis=AX.X, op=ALU.add)
        P.op("dve", "tensor_scalar", out=vst[:, 2, :], in0=vst[:, 0, :], scalar1=1.0 / 256, scalar2=None, op0=ALU.mult)
        P.op("dve", "tensor_tensor", out=vst[:, 3, :], in0=vst[:, 2, :], in1=vst[:, 2, :], op=ALU.mult)
        P.op("dve", "scalar_tensor_tensor", out=vst[:, 4, :], in0=vst[:, 1, :], scalar=1.0 / 256, in1=vst[:, 3, :],
             op0=ALU.mult, op1=ALU.subtract)
        P.act(vst[:, 5, :], vst[:, 4, :], AF.Ln, bias=EPS)
        P.act(vst[:, 5, :], vst[:, 5, :], AF.Exp, scale=-0.5)
        for b in range(NB):
            P.op("dve", "scalar_tensor_tensor", out=vtmp, in0=gv[:, b, :], scalar=vst[:, 2, b:b + 1], in1=cG[:, l, :],
                 op0=ALU.subtract, op1=ALU.mult)
            P.op("dve", "scalar_tensor_tensor", out=vn[:, b, :], in0=vtmp, scalar=vst[:, 5, b:b + 1], in1=cB[:, l, :],
                 op0=ALU.mult, op1=ALU.add)
        if stop == "gmlpstats":
            return
        w1 = wload("wi1", wi_s[l][1], 8 * 512)
        w0 = wload("wi0", wi_s[l][0], 8 * 512)
        w1v = V(w1, w1.ap[:, 0:4096].rearrange("p (k n) -> p k n", n=512))
        w0v = V(w0, w0.ap[:, 0:4096].rearrange("p (k n) -> p k n", n=512))
        for b in range(NB):
            t0, t1 = b * 128, (b + 1) * 128

            def qk_proj(t0=t0, t1=t1):
                res = []
                for wv_ in (w0v, w1v):
                    ps = bank()
                    psv = pv(ps, h=4)
                    for h in range(4):
                        for kc in range(8):
                            P.mm(psv[:, h, :], wv_[:, kc, h * 128:(h + 1) * 128], hT[:, kc, t0:t1],
                                 start=(kc == 0), stop=(kc == 7))
                    res.append(psv)
                return res

            psq, psk = gate_prep(l, b, w1, True, extra=qk_proj)
            P.op("dve", "tensor_tensor", out=qcat[:, :, t0:t1], in0=psq, in1=Epos, op=ALU.mult)
            P.op("dve", "tensor_tensor", out=kcat[:, :, t0:t1], in0=psk, in1=Eneg, op=ALU.mult)
        wfree(w1)
        wfree(w0)
        nxt.load()
        w3 = wload("wi3", wi_s[l][3], 8 * 512)
        for b in range(NB):
            pvv = projB(w3, 0, 512, 512, b)
            copy(vtok[:, b, :], pvv)
        wfree(w3)
        for c in range(2):
            P.op("dve", "tensor_tensor", out=diag[:, c, :, :],
                 in0=V(ident, ident.ap.unsqueeze(1).to_broadcast([128, 31, 128])),
                 in1=V(wdw_t, wdw_t.ap[:, l, c, :].unsqueeze(2).to_broadcast([128, 31, 128])), op=ALU.mult)
        if stop == "glaprep":
            return
        w2 = wload("wi2", wi_s[l][2], 8 * 512)
        for ch, ps in projA(w2, 0, 512, 4):
            th = th32[ch % 2]
            P.act(th, ps, AF.Tanh, scale=0.5)
            P.op("dve", "scalar_tensor_tensor", out=sg[:, ch, :], in0=th, scalar=1.0, in1=ps, op0=ALU.add, op1=ALU.mult)
        wfree(w2)
        if stop == "g":
            return
        w4 = wload("wi4", wi_s[l][4], 8 * 512)
        xcv = V(xcp, xcp.ap[:, :, 0:nseg * pad].rearrange("p c (s w) -> p c s w", w=pad))
        for c in range(2):
            wv4 = V(w4, w4.ap[:, 0:4096].rearrange("p (k n) -> p k n", n=512))
            pa = bank()
            pbk = bank()
            for kc in range(8):
                P.mm(pa, wv4[:, kc, c * 128:(c + 1) * 128], hT[:, kc, :], start=(kc == 0), stop=(kc == 7))
            for kc in range(8):
                P.mm(pbk, wv4[:, kc, 256 + c * 128:256 + (c + 1) * 128], hT[:, kc, :], start=(kc == 0), stop=(kc == 7))
            th = th32[c % 2]
            P.act(th, pbk, AF.Tanh, scale=0.5)
            P.op("dve", "scalar_tensor_tensor", out=xcv[:, c, :, 15:15 + seglen],
                 in0=V(th, th.ap.rearrange("p (s w) -> p s w", w=seglen)), scalar=1.0,
                 in1=V(pa.tile, pa.ap.rearrange("p (s w) -> p s w", w=seglen)), op0=ALU.add, op1=ALU.mult)
        if stop == "conv_glu":
            return
        wfree(w4)
        for b in range(NB):
            P.dma(Scat[64:128, b], V(sb_t[gb0 + b], sb_t[gb0 + b].ap.rearrange("d (h v) -> d h v", h=4)))
        if stop == "gc_load":
            return
        def att_stage(b):
            t0, t1 = b * 128, (b + 1) * 128
            pa2 = bank(2)
            pav = V(pa2.tile, pa2.ap.rearrange("p (a z) -> p a z", a=8))
            for h in range(4):
                for z in range(2):
                    P.mm(pav[:, 4 * z + h, :], kcat[z * 64:(z + 1) * 64, h, t0:t1], qcat[z * 64:(z + 1) * 64, h, t0:t1])
            Ab = Abf[b % 2]
            P.op("dve", "tensor_tensor", out=Ab, in0=pav, in1=mask, op=ALU.mult)

        def norm_tail(b, po):
            t0, t1 = b * 128, (b + 1) * 128
            P.op("dve", "tensor_tensor", out=on32, in0=po, in1=lnvb[b % 2], op=ALU.mult)
            P.op("dve", "scalar_tensor_tensor", out=mixT[:, 0:4, t0:t1],
                 in0=V(on32, on32.ap.rearrange("p (h t) -> p h t", h=4)), scalar=angc[:, l:l + 1],
                 in1=sg[:, :, t0:t1], op0=ALU.mult, op1=ALU.mult)

        att_stage(0)
        pend = None
        for b in range(NB):
            gb = gb0 + b
            t0, t1 = b * 128, (b + 1) * 128
            if gb in seq_first_blocks:
                if group == "ctx":
                    P.record("pool", lambda e: e.memset(S32.ap[0:64], 0.0), [], [S32])
                else:
                    P.dma(S32[0:64], V(sgla, sgla.ap[l, 0:64]))
            if b == 0 or gb in seq_first_blocks:
                P.act(Scat[0:64, b], S32[0:64], AF.Copy)
            state_update(b, (0, 64), Scat[0:64, b + 1] if b + 1 < NB else None)
            if gb in seq_last_blocks and group == "ctx":
                si = seq_last_blocks.index(gb)
                P.dma(V(nst, nst.ap[si, l, 0].rearrange("h d v -> d h v")), S32[0:64])
            if b + 1 < NB:
                att_stage(b + 1)
            if pend is not None:
                norm_tail(*pend)
            Ab = Abf[b % 2]
            po = bank()
            pov = pv(po, h=4)
            for h in range(4):
                P.mm(pov[:, h, :], vtok[:, b, h * 128:(h + 1) * 128], Ab[:, h, :], start=True, stop=False)
                P.mm(pov[:, h, :], vtok[:, b, h * 128:(h + 1) * 128], Ab[:, 4 + h, :], start=False, stop=False)
                P.mm(pov[:, h, :], Scat[:, b, h, :], qcat[:, h, t0:t1], start=False, stop=True)
            P.act(sqb, po, AF.Square)
            pss = bank()
            P.mm(pss, ones_bf, sqb)
            P.act(lnvb[b % 2], pss, AF.Ln, scale=1.0 / 128, bias=EPS)
            P.act(lnvb[b % 2], lnvb[b % 2], AF.Exp, scale=-0.5)
            pend = (b, po)
        norm_tail(*pend)
        if stop == "glacore":
            return
        flat = nseg * pad - 30
        n2 = flat - 512
        for c in range(2):
            py = bank(2)
            for j in range(31):
                P.mm(py[:, 0:512], diag[:, c, j, :], xcp[:, c, j:j + 512], start=(j == 0), stop=(j == 30))
            for j in range(31):
                P.mm(py[:, 512:512 + n2], diag[:, c, j, :], xcp[:, c, 512 + j:512 + j + n2], start=(j == 0), stop=(j == 30))
            pyv = V(py.tile, py.ap[:, 0:nseg * pad].rearrange("p (s w) -> p s w", w=pad))[:, :, 0:seglen]
            P.act(V(y32, y32.ap[:, c, :].rearrange("p (s w) -> p s w", w=seglen)), pyv, AF.Copy)
            P.act(V(ysq, ysq.ap[:, c, :].rearrange("p (s w) -> p s w", w=seglen)), pyv, AF.Square)
        p1 = bank()
        p2 = bank()
        for c in range(2):
            P.mm(p1, ones32, y32[:, c, :], start=(c == 0), stop=(c == 1))
        for c in range(2):
            P.mm(p2, ones32, ysq[:, c, :], start=(c == 0), stop=(c == 1))
        P.op("dve", "tensor_scalar", out=m32, in0=p1, scalar1=1.0 / 256, scalar2=None, op0=ALU.mult)
        P.op("pool", "tensor_tensor", out=msq, in0=m32, in1=m32, op=ALU.mult)
        P.op("dve", "scalar_tensor_tensor", out=var, in0=p2, scalar=1.0 / 256, in1=msq, op0=ALU.mult, op1=ALU.subtract)
        P.act(var, var, AF.Ln, bias=EPS)
        P.act(var, var, AF.Exp, scale=-0.5)
        for c in range(2):
            P.op("pool", "tensor_tensor", out=yc, in0=y32[:, c, :], in1=m32, op=ALU.subtract)
            P.op("dve", "tensor_tensor", out=yc, in0=yc, in1=var, op=ALU.mult)
            P.op("dve", "tensor_scalar", out=z32, in0=yc, scalar1=blg[:, l, c:c + 1], scalar2=blb[:, l, c:c + 1],
                 op0=ALU.mult, op1=ALU.add)
            th = th32[c % 2]
            P.act(th, z32, AF.Tanh, scale=0.5)
            P.op("dve", "scalar_tensor_tensor", out=mixT[:, 4 + c, :], in0=th, scalar=1.0, in1=z32, op0=ALU.add, op1=ALU.mult)
        if stop == "conv":
            return
        for b in range(NB):
            t0, t1 = b * 128, (b + 1) * 128
            pg = bank()
            pgv = pv(pg, a=2)
            for h in range(4):
                o = pgv[(h % 2) * 64:(h % 2) * 64 + 64, h // 2, 0:128]
                tp = {"tile_position": (0, 64)} if h % 2 == 1 else {}
                P.mm(o, vn[:, b, h * 64:(h + 1) * 64], wst_bf[:, l, h, :], start=True, stop=False, **tp)
                P.mm(o, ones_bf[0:1, 0:64], bs_hi[:, l, h * 128:(h + 1) * 128], start=False, stop=False, **tp)
                P.mm(o, ones_bf[0:1, 0:64], bs_lo[:, l, h * 128:(h + 1) * 128], start=False, stop=True, **tp)
            P.op("dve", "scalar_tensor_tensor", out=mixC[:, :, t0:t1], in0=pgv[:, :, 0:128], scalar=2.0,
                 in1=u_g[:, :, t0:t1], op0=ALU.mult, op1=ALU.mult)
        if stop == "gmlp":
            return
        wo = [wload("wo0", wo_s[l][0], 4096), wload("wo1", wo_s[l][1], 4096)]
        P.record("pool", lambda e: e.memset(ssn.ap, 0.0), [], [ssn])
        kk = [0]

        def wout_block(b):
            for nh in range(2):
                ps = bank()
                for kc in range(8):
                    wv_ = V(wo[kc // 4], wo[kc // 4].ap[:, 0:4096].rearrange("p (k n) -> p k n", n=D))
                    mx = mixT[:, kc, b * 128:(b + 1) * 128] if kc < 6 else mixC[:, kc - 6, b * 128:(b + 1) * 128]
                    P.mm(ps, mx, wv_[:, kc % 4, nh * 512:(nh + 1) * 512], start=(kc == 0), stop=(kc == 7))
                tt = tmp32[kk[0] % 2]
                kk[0] += 1
                P.op("dve", "tensor_tensor", out=tt, in0=ps, in1=gate_t[0][:, nh * 512:(nh + 1) * 512], op=ALU.mult)
                P.op("dve", "tensor_tensor", out=xt[:, b, nh * 512:(nh + 1) * 512],
                     in0=xt[:, b, nh * 512:(nh + 1) * 512], in1=tt, op=ALU.add)
            norm_sq(xt, ssn, b)
            P.act(rsn[:, b:b + 1], ssn[:, b:b + 1], AF.Ln, scale=1.0 / D, bias=EPS)
            P.act(rsn[:, b:b + 1], rsn[:, b:b + 1], AF.Exp, scale=-0.5)
            norm_xn(xt, rsn, b, nbuf=3)

        wout_block(0)
        wout_block(1)
        for b in range(NB):
            if b + 2 < NB:
                wout_block(b + 2)
            if b + 2 == NB - 1:
                wfree(wo[0])
                wfree(wo[1])
            norm_block(l, cond, 2, xt, rsn, b, ahead=False)
        if stop == "wout":
            return
        dbgout("x1", xt, [128, 4, 1024], F32, l, t)
        nxt.stats()
        for s in range(8):
            dcast(1 if s % 2 == 0 else 0)
            wf = wload("f1", f1_s[l][s], 4096)
            for ch, ps in projA(wf, 0, 512, 4):
                r = rbf[ch % 2]
                P.act(r, ps, AF.Relu)
                P.op("pool" if ch % 2 == 0 else "dve", "tensor_tensor", out=hid[:, s * 4 + ch, :], in0=r, in1=r, op=ALU.mult)
            wfree(wf)
        dcast(-1)
        blo[0] = 4
        for nh in range(2):
            accs = [P.psum_bank(b) for b in range(4)]
            for ks in range(8):
                if nh == 0 and ks % 2 == 1:
                    nxt.block(ks // 2)
                wf = wload("f2", f2_s[l][nh][ks], 2048)
                wv_ = V(wf, wf.ap[:, 0:2048].rearrange("p (k n) -> p k n", n=512))
                for b in range(NB):
                    for k4 in range(4):
                        kc = ks * 4 + k4
                        P.mm(accs[b], hid[:, kc, b * 128:(b + 1) * 128], wv_[:, k4, :], start=(kc == 0), stop=(kc == 31))
                wfree(wf)
            for b in range(NB):
                tt = tmp32[b % 2]
                P.op("dve", "tensor_tensor", out=tt, in0=accs[b], in1=gate_t[1][:, nh * 512:(nh + 1) * 512], op=ALU.mult)
                P.op("pool", "tensor_tensor", out=xt[:, b, nh * 512:(nh + 1) * 512], in0=xt[:, b, nh * 512:(nh + 1) * 512],
                     in1=tt, op=ALU.add)
        blo[0] = 0
        rr[0] = 4
        if stop == "ffn":
            return
        if not last_layer:
            P.dma(V(xmid_t[t], xmid.ap[t * 512:(t + 1) * 512, :].rearrange("(b p) d -> p b d", p=128)), xt)
        else:
            P.record("pool", lambda e: e.memset(ssn.ap, 0.0), [], [ssn])
            for b in range(NB):
                P.act(junk, xt[:, b, :], AF.Square, accum_out=ssn[:, b:b + 1])
            P.act(rsn, ssn, AF.Ln, scale=1.0 / D, bias=EPS)
            P.act(rsn, rsn, AF.Exp, scale=-0.5)
            for b in range(NB):
                P.op("dve", "scalar_tensor_tensor", out=xt[:, b, :], in0=xt[:, b, :], scalar=rsn[:, b:b + 1], in1=fgt,
                     op0=ALU.mult, op1=ALU.mult)
            P.dma(V(y_t[t], y.ap[t * 512:(t + 1) * 512, :].rearrange("(b p) d -> p b d", p=128)), xt)

    def emit_layers():
        nonlocal xt
        jobs = []
        for l in range(nlayers):
            last = (l == nlayers - 1)
            for group in groups:
                cond = 0 if group == "ctx" else 1
                if group == "ctx":
                    tiles = [0]
                    firsts, lasts = [0, 2], [1, 3]
                else:
                    tiles = list(range(1, 1 + nlat))
                    firsts, lasts = [0], [4 * nlat - 1]
                if do_pre:
                    for ti in reversed(range(len(tiles))):
                        jobs.append(("pre", l, cond, tiles[ti], firsts, lasts, ti * 4, group, last))
                if do_main:
                    for ti in range(len(tiles)):
                        jobs.append(("main", l, cond, tiles[ti], firsts, lasts, ti * 4, group, last))

        class Nxt:
            def __init__(self, j):
                self.j = j
                self.ok = j < len(jobs)
                self.st = 0
                self.nb = 0
                if self.ok:
                    self.l, self.cond, self.t = jobs[j][1:4]
                    self.xt = xtb[j % 2]

            def load(self):
                if self.ok and self.st == 0:
                    load_x(self.l, self.t, self.xt)
                    self.st = 1

            def stats(self):
                self.load()
                if self.ok and self.st == 1:
                    norm_stats(self.xt, ssn2, rsn2)
                    self.st = 2

            def block(self, b):
                self.stats()
                if self.ok and self.nb == b:
                    norm_block(self.l, self.cond, 1, self.xt, rsn2, b)
                    self.nb = b + 1

            def finish(self):
                for b in range(NB):
                    self.block(b)

        deferred = []
        if defer_cast and nlayers > 1:
            for l in range(1, nlayers):
                deferred += layer_cast_units(l)
        main_l0 = [j for j, jb in enumerate(jobs) if jb[0] == "main" and jb[1] == 0]

        P.record("pool", lambda e: e.memset(xcp.ap, 0.0), [], [xcp])
        Nxt(0).finish()
        cur_key = None
        for j, (kind, l, cond, t, firsts, lasts, gb0, group, last) in enumerate(jobs):
            if (l, group) != cur_key:
                cur_key = (l, group)
                P.dma(gate_t[0], gates_s[l][cond][0])
                P.dma(gate_t[1], gates_s[l][cond][1])
                P.record("pool", lambda e: e.memset(xcp.ap, 0.0), [], [xcp])
            if l >= 1:
                assert not deferred
            xt = xtb[j % 2]
            nxt = Nxt(j + 1)
            is_last_l0 = bool(main_l0) and j == main_l0[-1]

            def dcast(n, is_last_l0=is_last_l0, l=l):
                if l != 0:
                    return
                if n < 0:
                    n = len(deferred) if is_last_l0 else 0
                for _ in range(n):
                    if deferred:
                        cast_unit(deferred.pop(0), cst)

            if kind == "pre":
                prepass(l, cond, t, firsts, lasts, gb0, group, nxt, dcast)
            else:
                mainpass(l, cond, t, firsts, lasts, gb0, group, last, nxt, dcast)
            nxt.finish()

    P.dry = True
    emit_layers()
    P.dry = False
    rr[0] = 0
    evc[0] = 0
    wstate["n"] = 0
    wstate["issued"] = 0
    wstate["freed"] = set()
    wstate["cur"] = -1
    emit_layers()
    P.finalize()
    print("ops per engine (n, waits):", P.stats)
    return nc, dbg_outs


_CACHE = {}


def _prep_inputs(inp):
    f = np.float32
    g = {k: np.asarray(v) for k, v in inp.items()}
    w_in = g["w_in"]
    qi = np.arange(0, 256)
    ki = np.arange(256, 512)
    vi = np.arange(512, 1024)
    gi = np.arange(1024, 1536)
    lri = np.arange(1536, 1568)
    gai = np.arange(1568, 1824)
    gbi = np.arange(1824, 2080)
    ui = np.arange(2080, 2336)
    vgi = np.arange(2336, 2592)
    qdup = np.concatenate([np.concatenate([qi[h * 64:(h + 1) * 64]] * 2) for h in range(4)])
    kdup = np.concatenate([np.concatenate([ki[h * 64:(h + 1) * 64]] * 2) for h in range(4)])
    idx = np.concatenate([qdup, kdup, gi, vi, gai, gbi, ui, vgi, lri])
    assert idx.size == NCOL

    def kchunk(w):
        Lw, K, N = w.shape
        return np.ascontiguousarray(w.reshape(Lw, K // 128, 128, N).transpose(0, 2, 1, 3))

    shared = {
        "w_in_r": kchunk(w_in[:, :, idx]),
        "w_out_r": kchunk(g["w_out"]),
        "w_ff1_r": kchunk(g["w_ff1"]),
        "w_ff2_r": kchunk(g["w_ff2"]),
        "w_mod_r": kchunk(g["w_mod"]),
        "b_mod": np.ascontiguousarray(g["b_mod"]),
    }
    wag = g["w_a_gate"]
    bag = g["b_a_gate"]
    wgm = np.zeros((L, 33, 2, 4, 64), f)
    for z in range(2):
        wgm[:, z * 16:(z + 1) * 16, z, :, :] = wag[:, z].reshape(L, 16, 4, 64)
        wgm[:, 32, z, :, :] = bag[:, z].reshape(L, 4, 64)
    shared["wg"] = wgm.reshape(L, 33, 512)
    shared["n1g"] = np.ascontiguousarray(g["norm1_g"].reshape(L, 8, 128).transpose(0, 2, 1))
    shared["n2g"] = np.ascontiguousarray(g["norm2_g"].reshape(L, 8, 128).transpose(0, 2, 1))
    shared["ang"] = np.ascontiguousarray(g["a_norm_g"].T)
    shared["wdw"] = np.ascontiguousarray(g["w_dw"].reshape(L, 31, 2, 128).transpose(0, 3, 2, 1))
    shared["blng"] = np.ascontiguousarray(g["b_ln_g"].reshape(L, 2, 128).transpose(0, 2, 1))
    shared["blnb"] = np.ascontiguousarray(g["b_ln_b"].reshape(L, 2, 128).transpose(0, 2, 1))
    shared["clng"] = np.ascontiguousarray(g["c_ln_g"])
    shared["clnb"] = np.ascontiguousarray(g["c_ln_b"])
    shared["wst"] = np.ascontiguousarray(g["w_s"].transpose(0, 3, 1, 2))
    shared["bs"] = np.ascontiguousarray(g["b_s"].reshape(L, 1, 512))
    shared["fg"] = np.ascontiguousarray(g["final_g"])
    shared = {k: np.ascontiguousarray(v.astype(f)) for k, v in shared.items()}
    maps = []
    for i in range(8):
        m = dict(shared)
        m["xin"] = np.ascontiguousarray(np.concatenate(
            [g["x_prompt"][2 * i], g["x_prompt"][2 * i + 1], g["x_sample"][i]], axis=0).astype(f))
        cv = np.stack([g["c_ctx"], g["c"][i]], axis=-1).astype(f)
        m["cvec"] = np.ascontiguousarray(cv.reshape(8, 128, 2).transpose(1, 0, 2))
        sg_ = g["state_gla"][i]
        m["sgla"] = np.ascontiguousarray(sg_.transpose(0, 1, 3, 2, 4).reshape(L, 128, 4, 128).astype(f))
        maps.append(m)
    return maps


def kernel(**inputs):
    if "nc" not in _CACHE:
        _CACHE["nc"] = build()[0]
    nc = _CACHE["nc"]
    maps = _prep_inputs(inputs)
    res = run_bass_kernel_spmd(nc, maps, core_ids=list(range(8)))
    yp = np.zeros((16, 256, D), np.float32)
    ys = np.zeros((8, 4096, D), np.float32)
    ns = np.zeros((16, L, 2, 4, 64, 128), np.float32)
    for i in range(8):
        r = res.results[i]
        yy = np.asarray(r["y"])
        yp[2 * i] = yy[0:256]
        yp[2 * i + 1] = yy[256:512]
        ys[i] = yy[512:]
        n_ = np.asarray(r["nst"])
        ns[2 * i] = n_[0]
        ns[2 * i + 1] = n_[1]
    return yp, ys, ns
```

```python
import numpy as np
import concourse.bass as bass
import concourse.mybir as mybir

F32 = mybir.dt.float32
BF16 = mybir.dt.bfloat16
U8 = mybir.dt.uint8
I32 = mybir.dt.int32
AF = mybir.ActivationFunctionType
ALU = mybir.AluOpType
AX = mybir.AxisListType
DT_SIZE = {F32: 4, BF16: 2, U8: 1, I32: 4}
NSLOTS = 24


class T:
    def __init__(self, ap, space, plo, phi, lo, hi):
        self.ap = ap
        self.tile = self
        self.region = (space, plo, phi, lo, hi)

    def __getitem__(self, k):
        return V(self, self.ap[k])

    def v(self, ap):
        return V(self, ap)


class V:
    def __init__(self, tile, ap):
        self.tile = tile
        self.ap = ap

    def __getitem__(self, k):
        return V(self.tile, self.ap[k])


class Grid:
    def __init__(self, t, n):
        self.ap = t.ap
        space, plo, phi, lo, hi = t.region
        sz = (hi - lo) // n
        self.n = n
        self.subs = [T(t.ap[:, i], space, plo, phi, lo + i * sz, lo + (i + 1) * sz) for i in range(n)]
        self.tile = self.subs

    def __getitem__(self, key):
        k = key[1] if isinstance(key, tuple) and len(key) > 1 else slice(None)
        idx = [k] if isinstance(k, int) else list(range(*k.indices(self.n)))
        return V([self.subs[i] for i in idx], self.ap[key])


class Arena:
    def __init__(self, nc, name, nbytes, space):
        self.space = space
        self.nbytes = nbytes
        self.h = nc.alloc_sbuf_tensor(name, [128, nbytes], U8)
        self.ap = self.h.ap()
        self.off = 0
        self.peak = 0

    def mark(self):
        return self.off

    def release(self, m):
        self.off = m

    def alloc(self, shape, dtype, nparts=128, pbase=0):
        if isinstance(shape, int):
            shape = (shape,)
        n = int(np.prod(shape)) * DT_SIZE[dtype]
        off = (self.off + 31) // 32 * 32
        assert off + n <= self.nbytes, f"arena {self.space} overflow: {off + n} > {self.nbytes}"
        self.off = off + n
        self.peak = max(self.peak, self.off)
        v = self.ap[pbase:pbase + nparts, off:off + n].bitcast(dtype)
        if len(shape) > 1:
            names = [f"d{i}" for i in range(len(shape))]
            s = f"p ({' '.join(names)}) -> p {' '.join(names)}"
            v = v.rearrange(s, **{nm: int(sz) for nm, sz in zip(names, shape)})
        return T(v, self.space, pbase, pbase + nparts, off, off + n)


class Op:
    __slots__ = ("eng", "emit", "deps", "is_dma", "sigval", "signaler", "slot", "waits")

    def __init__(self, eng, emit, deps, is_dma):
        self.eng = eng
        self.emit = emit
        self.deps = deps
        self.is_dma = is_dma
        self.sigval = 0
        self.signaler = is_dma
        self.slot = None
        self.waits = None


class Prog:
    ENGS = ("pe", "act", "dve", "pool", "sp")

    def __init__(self, nc):
        self.nc = nc
        self.ops = []
        self.records = {}
        self.psum_h = nc.alloc_psum_tensor("psum_all", [128, 4096], F32)
        self.psum_ap = self.psum_h.ap()
        self.n_dram = 0
        self.dry = False

    def psum_bank(self, b, nbanks=1):
        ap = self.psum_ap[:, b * 512:(b + nbanks) * 512]
        return T(ap, "psum", 0, 128, b * 2048, (b + nbanks) * 2048)

    def dram(self, name, shape, dtype, kind="Internal"):
        h = self.nc.dram_tensor(name, list(shape), dtype, kind=kind)
        return T(h.ap(), "dram:" + name, 0, 1, 0, 1)

    def dram_sub(self, t, ap, lo, hi):
        sp = t.region[0]
        return T(ap, sp, 0, 1, lo, hi)

    def _deps_for(self, region, is_write, opid, deps):
        space, plo, phi, lo, hi = region
        recs = self.records.setdefault(space, [])
        keep = []
        eng = self.ops_eng
        for r in recs:
            overlap = r[0] < phi and plo < r[1] and r[2] < hi and lo < r[3]
            if not overlap:
                keep.append(r)
                continue
            conflict = is_write or r[5]
            if conflict:
                deps.add(r[4])
            contained = plo <= r[0] and r[1] <= phi and lo <= r[2] and r[3] <= hi
            if is_write and contained:
                continue
            if (not is_write) and (not r[5]) and contained and self.ops[r[4]].eng == eng \
                    and not self.ops[r[4]].is_dma and not self.cur_is_dma:
                continue
            keep.append(r)
        self.records[space] = keep

    def record(self, eng, emit, reads, writes, is_dma=False):
        if self.dry:
            return None
        opid = len(self.ops)
        deps = set()
        self.ops_eng = eng
        self.cur_is_dma = is_dma
        seen = set()
        for t in writes:
            self._deps_for(t.region, True, opid, deps)
        for t in reads:
            self._deps_for(t.region, False, opid, deps)
        op = Op(eng, emit, deps, is_dma)
        self.ops.append(op)
        for t in writes:
            sp, plo, phi, lo, hi = t.region
            self.records[sp].append([plo, phi, lo, hi, opid, True])
        for t in reads:
            key = id(t)
            if key in seen:
                continue
            seen.add(key)
            sp, plo, phi, lo, hi = t.region
            self.records[sp].append([plo, phi, lo, hi, opid, False])
        return op

    def op(self, eng, method, **kw):
        writes, reads, args = [], [], {}
        for k, v in kw.items():
            if isinstance(v, (T, V, Grid)):
                tl = v.tile if isinstance(v.tile, list) else [v.tile]
                (writes if k in ("out", "accum_out") else reads).extend(tl)
                args[k] = v.ap
            else:
                args[k] = v
        if method == "matmul" and kw.get("start") is False:
            ot = kw["out"].tile
            reads.extend(ot if isinstance(ot, list) else [ot])
        is_dma = method == "dma_start"
        return self.record(eng, lambda e: getattr(e, method)(**args), reads, writes, is_dma)

    def mm(self, out, lhsT, rhs, start=True, stop=True, **kw):
        return self.op("pe", "matmul", out=out, lhsT=lhsT, rhs=rhs, start=start, stop=stop, **kw)

    def transpose(self, out, in_, identity):
        return self.op("pe", "transpose", out=out, in_=in_, identity=identity)

    def dma(self, out, in_, eng="sp", **kw):
        return self.op(eng, "dma_start", out=out, in_=in_, **kw)

    def act(self, out, in_, func, eng="act", **kw):
        return self.op(eng, "activation", out=out, in_=in_, func=func, **kw)

    def finalize(self):
        nc = self.nc
        ops = self.ops
        for op in ops:
            for d in op.deps:
                dop = ops[d]
                if dop.eng == "pe" and op.eng == "pe" and not dop.is_dma and not op.is_dma:
                    continue
                dop.signaler = True
        sig = {e: 0 for e in self.ENGS}
        slotcnt = [0] * NSLOTS
        ndma = 0
        nsw = 0
        for op in ops:
            if op.is_dma and op.eng == "pool":
                op.slot = ("sw", nsw)
                nsw += 1
                op.sigval = 16
            elif op.is_dma:
                op.slot = ndma % NSLOTS
                ndma += 1
                slotcnt[op.slot] += 16
                op.sigval = slotcnt[op.slot]
            elif op.signaler:
                sig[op.eng] += 1
                op.sigval = sig[op.eng]
        seen = {e: {} for e in self.ENGS}
        for op in ops:
            need = {}
            for d in op.deps:
                dop = ops[d]
                if dop.eng == "pe" and op.eng == "pe" and not dop.is_dma and not op.is_dma:
                    continue
                key = ("dma", dop.slot) if dop.is_dma else dop.eng
                need[key] = max(need.get(key, 0), dop.sigval)
            if op.is_dma and op.sigval > 16:
                key = ("dma", op.slot)
                need[key] = max(need.get(key, 0), op.sigval - 16)
            w = []
            s = seen[op.eng]
            for key, val in need.items():
                if s.get(key, 0) < val:
                    s[key] = val
                    w.append((key, val))
            op.waits = w
        self.final_slot = slotcnt
        self.final_sig = sig
        self.sems = {e: nc.alloc_semaphore("sem_" + e) for e in self.ENGS}
        for i in range(NSLOTS):
            self.sems[("dma", i)] = nc.alloc_semaphore(f"sem_dma{i}")
        for i in range(nsw):
            self.sems[("dma", ("sw", i))] = nc.alloc_semaphore(f"sem_swdma{i}")
        per = {e: [op for op in ops if op.eng == e] for e in self.ENGS}
        self.stats = {e: (len(per[e]), sum(len(o.waits) for o in per[e])) for e in self.ENGS}
        sems = self.sems

        def run(engine, name):
            for op in per[name]:
                for key, val in op.waits:
                    engine.wait_ge(sems[key], val)
                ins = op.emit(engine)
                if op.is_dma:
                    ins.then_inc(sems[("dma", op.slot)], 16)
                elif op.signaler:
                    ins.then_inc(sems[name], 1)
            if name == "sp":
                for i in range(NSLOTS):
                    if slotcnt[i] > 0:
                        engine.wait_ge(sems[("dma", i)], slotcnt[i])
                for i in range(nsw):
                    engine.wait_ge(sems[("dma", ("sw", i))], 16)
                for e in ("pe", "act", "dve", "pool"):
                    if sig[e] > 0:
                        engine.wait_ge(sems[e], sig[e])

        with nc.Block() as block:
            @block.tensor
            def _(e):
                run(e, "pe")

            @block.scalar
            def _(e):
                run(e, "act")

            @block.vector
            def _(e):
                run(e, "dve")

            @block.gpsimd
            def _(e):
                run(e, "pool")

            @block.sync
            def _(e):
                run(e, "sp")

from concourse.bass_utils import run_bass_kernel_spmd

D = 1024
L = 2
NTOK = 4608
NB = 4
DFF = 4096
EPS = 1e-6
QO, KO, GO, VO, GAO, GBO, UO, VGO, LRO, NCOL = 0, 512, 1024, 1536, 2048, 2304, 2560, 2816, 3072, 3104
WI_SLABS = [(0, 512), (512, 1024), (1024, 1536), (1536, 2048), (2048, 2560), (2560, 3104)]
SLOT_COLS = 8 * 544
NSLOT = 3


def build(nlayers=L, dbg=None, groups=("ctx", "lat"), do_pre=True, do_main=True, nlat=8, stop=None, defer_cast=True):
    nc = bass.Bass("TRN2", target_bir_lowering=False)
    P = Prog(nc)

    def inp(name, shape, dt=F32):
        return P.dram(name, shape, dt, kind="ExternalInput")

    xin = inp("xin", [NTOK, D])
    cvec = inp("cvec", [128, 8, 2])
    sgla = inp("sgla", [L, 128, 4, 128])
    w_in_r = inp("w_in_r", [L, 128, 8, NCOL])
    w_out_r = inp("w_out_r", [L, 128, 8, D])
    w_ff1_r = inp("w_ff1_r", [L, 128, 8, DFF])
    w_ff2_r = inp("w_ff2_r", [L, 128, 32, D])
    w_mod_r = inp("w_mod_r", [L, 128, 8, 6 * D])
    b_mod = inp("b_mod", [L, 6 * D])
    wg = inp("wg", [L, 33, 512])
    n1g = inp("n1g", [L, 128, 8])
    n2g = inp("n2g", [L, 128, 8])
    ang = inp("ang", [128, L])
    wdw = inp("wdw", [L, 128, 2, 31])
    blng = inp("blng", [L, 128, 2])
    blnb = inp("blnb", [L, 128, 2])
    clng = inp("clng", [L, 256])
    clnb = inp("clnb", [L, 256])
    wst = inp("wst", [L, 128, 4, 128])
    bs = inp("bs", [L, 1, 512])
    fg = inp("fg", [D])
    y = P.dram("y", [NTOK, D], F32, kind="ExternalOutput")
    nst = P.dram("nst", [2, L, 2, 4, 64, 128], F32, kind="ExternalOutput")
    dbg_outs = {}

    wi_s = [[P.dram(f"wi{l}_{s}", [128, 8 * (c1 - c0)], BF16) for s, (c0, c1) in enumerate(WI_SLABS)] for l in range(L)]
    wo_s = [[P.dram(f"wo{l}_{s}", [128, 4 * D], BF16) for s in range(2)] for l in range(L)]
    f1_s = [[P.dram(f"f1{l}_{s}", [128, 8 * 512], BF16) for s in range(8)] for l in range(L)]
    f2_s = [[[P.dram(f"f2{l}_{nh}_{ks}", [128, 4 * 512], BF16) for ks in range(8)] for nh in range(2)] for l in range(L)]
    gates_s = [[[P.dram(f"gate{l}_{c}_{g}", [128, D], F32) for g in range(2)] for c in range(2)] for l in range(L)]
    xmid = P.dram("xmid", [NTOK, D], F32)
    sb_scr = P.dram("sb_scr", [32, 64, 512], BF16)
    xmid_t = [P.dram_sub(xmid, xmid.ap[t * 512:(t + 1) * 512, :], t, t + 1) for t in range(9)]
    y_t = [P.dram_sub(y, y.ap[t * 512:(t + 1) * 512, :], t, t + 1) for t in range(9)]
    sb_t = [P.dram_sub(sb_scr, sb_scr.ap[b], b, b + 1) for b in range(32)]

    A = Arena(nc, "arena", 211712, "sb")
    rr = [0]

    blo = [0]

    def bank(n=1):
        if blo[0]:
            assert n == 1
            b = blo[0] + rr[0] % (8 - blo[0])
            rr[0] = rr[0] + 1
        elif n == 2:
            b = ((rr[0] + 1) // 2 * 2) % 8
            rr[0] = b + 2
        else:
            b = rr[0] % 8
            rr[0] = b + 1
        return P.psum_bank(b, n)

    def pv(ps, **kw):
        names = list(kw.keys())
        if len(names) == 1:
            s = f"p ({names[0]} ww) -> p {names[0]} ww"
        else:
            s = f"p ({names[0]} {names[1]} ww) -> p {names[0]} {names[1]} ww"
        return V(ps.tile, ps.ap.rearrange(s, **kw))

    evc = [0]

    def copy(out, in_, scale=None):
        evc[0] += 1
        if evc[0] % 2 == 0:
            if scale is None:
                P.act(out, in_, AF.Copy)
            else:
                P.act(out, in_, AF.Copy, scale=float(scale))
        else:
            if scale is None:
                P.op("dve", "tensor_copy", out=out, in_=in_)
            else:
                P.op("dve", "tensor_scalar", out=out, in0=in_, scalar1=float(scale), scalar2=None, op0=ALU.mult)

    ident = A.alloc(128, BF16)
    ones_bf = A.alloc(128, BF16)
    ones32 = A.alloc(128, F32)
    triL = A.alloc(128, F32)
    triU = A.alloc(128, F32)
    negcol = A.alloc(1, F32)
    mask = A.alloc((8, 128), BF16)
    sel = A.alloc((2, 128), F32, nparts=2)
    onehot = A.alloc(2, F32, nparts=2)
    m0 = A.mark()
    tmpc = A.alloc(128, F32)
    P.record("pool", lambda e: e.memset(ones32.ap, 1.0), [], [ones32])
    P.record("pool", lambda e: e.memset(negcol.ap, -1.0 / 16), [], [negcol])
    P.op("dve", "tensor_copy", out=ones_bf, in_=ones32)
    P.record("pool", lambda e: e.affine_select(out=tmpc.ap, in_=ones32.ap, pattern=[[-1, 128]], compare_op=ALU.is_equal,
                                               fill=0.0, base=0, channel_multiplier=1), [ones32], [tmpc])
    P.op("dve", "tensor_copy", out=ident, in_=tmpc)
    tl1 = A.alloc(128, F32)
    tu1 = A.alloc(128, F32)
    P.record("pool", lambda e: e.affine_select(out=tl1.ap, in_=ones32.ap, pattern=[[1, 128]], compare_op=ALU.is_ge,
                                               fill=0.0, base=0, channel_multiplier=-1), [ones32], [tl1])
    P.record("pool", lambda e: e.affine_select(out=tu1.ap, in_=ones32.ap, pattern=[[-1, 128]], compare_op=ALU.is_ge,
                                               fill=0.0, base=0, channel_multiplier=1), [ones32], [tu1])
    P.op("dve", "tensor_scalar", out=triL, in0=tl1, scalar1=-1.0 / 16, scalar2=None, op0=ALU.mult)
    P.op("dve", "tensor_scalar", out=triU, in0=tu1, scalar1=-1.0 / 16, scalar2=None, op0=ALU.mult)
    for hd in range(8):
        P.op("dve", "tensor_copy", out=mask[:, hd, :], in_=(tl1 if hd < 4 else tu1))
    P.record("pool", lambda e: e.affine_select(out=sel.ap, in_=ones32.ap[0:2, :].unsqueeze(1).to_broadcast([2, 2, 128]),
                                               pattern=[[-1, 2], [0, 128]], compare_op=ALU.is_equal,
                                               fill=0.0, base=0, channel_multiplier=1), [ones32], [sel])
    P.record("pool", lambda e: e.affine_select(out=onehot.ap, in_=ones32.ap[0:2, 0:2], pattern=[[-1, 2]],
                                               compare_op=ALU.is_equal, fill=0.0, base=0, channel_multiplier=1),
             [ones32], [onehot])
    A.release(m0)

    wg_bf = A.alloc((L, 512), BF16, nparts=33)
    wst_bf = A.alloc((L, 4, 128), BF16)
    bs_hi = A.alloc((L, 512), BF16, nparts=1)
    bs_lo = A.alloc((L, 512), BF16, nparts=1)
    cG = A.alloc((L, 256), F32)
    cB = A.alloc((L, 256), F32)
    fgt = A.alloc(D, F32)
    modc = A.alloc((L, 2, 4, 8), F32)
    angc = A.alloc(L, F32)
    blg = A.alloc((L, 2), F32)
    blb = A.alloc((L, 2), F32)
    blgh = A.alloc((L, 2), F32)
    blbh = A.alloc((L, 2), F32)
    wdw_t = A.alloc((L, 2, 31), F32)
    n1g_t = A.alloc((L, 8), F32)
    n2g_t = A.alloc((L, 8), F32)
    cT = A.alloc((8, 2), F32)
    sc = A.alloc((8, 2), F32)
    m0 = A.mark()
    st_wg = A.alloc((L, 512), F32, nparts=33)
    st_ws = A.alloc((L, 4, 128), F32)
    st_bs = A.alloc((L, 512), F32, nparts=1)
    st_bs2 = A.alloc((L, 512), F32, nparts=1)
    P.dma(st_wg, V(wg, wg.ap.rearrange("l k n -> k l n")))
    P.dma(st_ws, V(wst, wst.ap.rearrange("l q h p -> q l h p")))
    P.dma(st_bs, V(bs, bs.ap.rearrange("l o n -> o l n")))
    P.dma(cG, V(clng, clng.ap.partition_broadcast(128)))
    P.dma(cB, V(clnb, clnb.ap.partition_broadcast(128)))
    P.dma(fgt, V(fg, fg.ap.partition_broadcast(128)))
    P.dma(angc, ang)
    P.dma(blg, V(blng, blng.ap.rearrange("l p c -> p l c")))
    P.dma(blb, V(blnb, blnb.ap.rearrange("l p c -> p l c")))
    P.dma(wdw_t, V(wdw, wdw.ap.rearrange("l p c j -> p l c j")))
    P.dma(n1g_t, V(n1g, n1g.ap.rearrange("l p k -> p l k")))
    P.dma(n2g_t, V(n2g, n2g.ap.rearrange("l p k -> p l k")))
    P.dma(cT, cvec)
    P.op("dve", "tensor_copy", out=wg_bf, in_=st_wg)
    P.op("dve", "tensor_copy", out=wst_bf, in_=st_ws)
    P.op("dve", "tensor_copy", out=bs_hi, in_=st_bs)
    P.op("dve", "tensor_tensor", out=st_bs2, in0=st_bs, in1=bs_hi, op=ALU.subtract)
    P.op("dve", "tensor_copy", out=bs_lo, in_=st_bs2)
    P.op("dve", "tensor_scalar", out=blgh, in0=blg, scalar1=0.5, scalar2=None, op0=ALU.mult)
    P.op("dve", "tensor_scalar", out=blbh, in0=blb, scalar1=0.5, scalar2=None, op0=ALU.mult)
    P.op("dve", "tensor_scalar", out=wdw_t, in0=wdw_t, scalar1=0.5, scalar2=None, op0=ALU.mult)
    thc = A.alloc((8, 2), F32)
    P.act(thc, cT, AF.Tanh, scale=0.5)
    P.op("dve", "scalar_tensor_tensor", out=sc, in0=thc, scalar=1.0, in1=cT, op0=ALU.add, op1=ALU.mult)
    P.op("dve", "tensor_scalar", out=sc, in0=sc, scalar1=0.5, scalar2=None, op0=ALU.mult)
    A.release(m0)

    m0 = A.mark()
    modrow = A.alloc(6 * D, F32, nparts=2)
    brow = A.alloc(6 * D, F32, nparts=2)
    wm = [A.alloc((8, 512), F32) for _ in range(2)]
    gst = [A.alloc(512, F32) for _ in range(2)]
    mcs = A.alloc(64, F32)
    s32 = [A.alloc(4096, F32) for _ in range(4)]
    s16 = [A.alloc(4096, BF16) for _ in range(4)]
    ci = [0]

    def cast_units(src_ap, src_t, dst, kc, ncols, scale=None):
        per = max(1, 4096 // ncols)
        return [(src_ap, src_t, dst, k0, min(kc, k0 + per), ncols, scale) for k0 in range(0, kc, per)]

    def cast_unit(u, st=None):
        src_ap, src_t, dst, k0, k1, ncols, scale = u
        st32, st16 = st if st is not None else (s32, s16)
        if True:
            n = (k1 - k0) * ncols
            a = st32[ci[0] % len(st32)]
            b = st16[ci[0] % len(st16)]
            ci[0] += 1
            P.dma(V(a, a.ap[:, 0:n].rearrange("p (k n) -> p k n", n=ncols)), V(src_t, src_ap[:, k0:k1, :]))
            e = ci[0] % 2
            if scale is None:
                if e == 0:
                    P.act(b[:, 0:n], a[:, 0:n], AF.Copy)
                elif e == 1:
                    P.op("dve", "tensor_copy", out=b[:, 0:n], in_=a[:, 0:n])
                else:
                    P.op("pool", "tensor_copy", out=b[:, 0:n], in_=a[:, 0:n])
            else:
                if e == 0:
                    P.act(b[:, 0:n], a[:, 0:n], AF.Copy, scale=float(scale))
                else:
                    P.op("dve" if e == 1 else "pool", "tensor_scalar", out=b[:, 0:n], in0=a[:, 0:n],
                         scalar1=float(scale), scalar2=None, op0=ALU.mult)
            P.dma(V(dst, dst.ap[:, k0 * ncols:k1 * ncols]), b[:, 0:n], eng="pool")

    def layer_cast_units(l):
        us = []
        for s, (c0, c1) in enumerate(WI_SLABS):
            us += cast_units(w_in_r.ap[l, :, :, c0:c1], w_in_r, wi_s[l][s], 8, c1 - c0, scale=(0.125 if s == 0 else None))
        for s in range(2):
            us += cast_units(w_out_r.ap[l, :, 4 * s:4 * s + 4, :], w_out_r, wo_s[l][s], 4, D, scale=0.5)
        for s in range(8):
            us += cast_units(w_ff1_r.ap[l, :, :, s * 512:(s + 1) * 512], w_ff1_r, f1_s[l][s], 8, 512)
        for nh in range(2):
            for ks in range(8):
                us += cast_units(w_ff2_r.ap[l, :, 4 * ks:4 * ks + 4, nh * 512:(nh + 1) * 512], w_ff2_r, f2_s[l][nh][ks], 4, 512)
        return us

    l0_units = layer_cast_units(0)
    for l in range(nlayers):
        P.dma(brow, V(b_mod, b_mod.ap[l].partition_broadcast(2)))
        for cc in range(12):
            w = wm[cc % 2]
            P.dma(w, V(w_mod_r, w_mod_r.ap[l, :, :, cc * 512:(cc + 1) * 512]), eng="act")
            ps = bank()
            for kc in range(8):
                P.mm(ps[0:2, :], sc[:, kc, :], w[:, kc, :], start=(kc == 0), stop=(kc == 7))
            P.op("dve", "tensor_tensor", out=modrow[:, cc * 512:(cc + 1) * 512], in0=ps[0:2, :],
                 in1=brow[:, cc * 512:(cc + 1) * 512], op=ALU.add)
            for _ in range(2):
                if l0_units:
                    cast_unit(l0_units.pop(0))
        i = 0
        for c in range(2):
            for g, vec in enumerate((2, 5)):
                for nh in range(2):
                    ps = bank()
                    P.mm(ps, sel[:, c, :], modrow[:, vec * D + nh * 512: vec * D + (nh + 1) * 512])
                    st = gst[i % 2]
                    i += 1
                    copy(st, ps)
                    P.dma(V(gates_s[l][c][g], gates_s[l][c][g].ap[:, nh * 512:(nh + 1) * 512]), st)
        ps = bank()
        for c in range(2):
            for vi, vec in enumerate((0, 1, 3, 4)):
                for kc in range(8):
                    j = (c * 4 + vi) * 8 + kc
                    P.mm(ps[:, j:j + 1], modrow[:, vec * D + kc * 128: vec * D + (kc + 1) * 128], onehot[:, c:c + 1])
        P.op("dve", "tensor_copy", out=mcs, in_=ps[:, 0:64])
        mv = V(mcs, mcs.ap.rearrange("p (c v k) -> p c v k", c=2, v=4))
        for c in range(2):
            P.op("dve", "scalar_tensor_tensor", out=modc[:, l, c, 0, :], in0=mv[:, c, 1, :], scalar=1.0,
                 in1=n1g_t[:, l, :], op0=ALU.add, op1=ALU.mult)
            P.op("dve", "tensor_copy", out=modc[:, l, c, 1, :], in_=mv[:, c, 0, :])
            P.op("dve", "scalar_tensor_tensor", out=modc[:, l, c, 2, :], in0=mv[:, c, 3, :], scalar=1.0,
                 in1=n2g_t[:, l, :], op0=ALU.add, op1=ALU.mult)
            P.op("dve", "tensor_copy", out=modc[:, l, c, 3, :], in_=mv[:, c, 2, :])

    while l0_units:
        cast_unit(l0_units.pop(0))
    if not defer_cast:
        for l in range(1, nlayers):
            for u in layer_cast_units(l):
                cast_unit(u)
    A.release(m0)

    gate_t = [A.alloc(D, F32) for _ in range(2)]
    Scat = A.alloc((NB, 4, 128), BF16)
    S32 = A.alloc((4, 128), F32)
    xtb = [Grid(A.alloc((NB, D), F32), NB) for _ in range(2)]
    xt = xtb[0]
    ssn = A.alloc(NB, F32)
    rsn = A.alloc(NB, F32)
    ssn2 = A.alloc(NB, F32)
    rsn2 = A.alloc(NB, F32)
    junk = A.alloc(D, BF16)
    xnb = [A.alloc(D, BF16) for _ in range(3)]
    hT = Grid(A.alloc((8, 512), BF16), 8)
    mixT = Grid(A.alloc((6, 512), BF16), 6)
    mixC = A.alloc((2, 512), BF16)
    tmp32 = [A.alloc(512, F32) for _ in range(2)]
    slots = [A.alloc(SLOT_COLS, BF16) for _ in range(NSLOT)]
    lrT = A.alloc(512, BF16, nparts=33)
    xcp = A.alloc((2, 752), BF16)
    ov = A.mark()
    gv = A.alloc((NB, 256), F32)
    gsq = A.alloc((NB, 256), F32)
    e32 = A.alloc(512, F32)
    sp = [A.alloc(512, F32) for _ in range(2)]
    Ek = A.alloc(512, F32)
    Epos = A.alloc((4, 128), F32)
    Eneg = A.alloc((4, 128), F32)
    qcat = A.alloc((4, 512), BF16)
    kcat = A.alloc((4, 512), BF16)
    kdt = Grid(A.alloc((NB, 512), BF16), NB)
    vtok = Grid(A.alloc((NB, 512), BF16), NB)
    sg = A.alloc((4, 512), BF16)
    dec = A.alloc((NB, 4), F32)
    u_g = A.alloc((2, 512), BF16)
    vst = A.alloc((6, NB), F32)
    vn = A.alloc((NB, 256), BF16)
    vtmp = A.alloc(256, F32)
    Abf = [A.alloc((8, 128), BF16) for _ in range(2)]
    sqb = A.alloc(512, BF16)
    lnvb = [A.alloc(512, F32) for _ in range(2)]
    on32 = A.alloc(512, F32)
    th32 = [A.alloc(512, F32) for _ in range(2)]
    y32 = A.alloc((2, 512), F32)
    ysq = A.alloc((2, 512), F32)
    m32 = A.alloc(512, F32)
    msq = A.alloc(512, F32)
    var = A.alloc(512, F32)
    yc = A.alloc(512, F32)
    z32 = A.alloc(512, F32)
    stS = A.alloc((4, 128), F32)
    ov_end = A.mark()
    A.release(ov)
    diag = A.alloc((2, 31, 128), BF16)
    assert A.off <= ov + 20 * 1024
    A.release(ov)
    hid = A.alloc((32, 512), BF16)
    rbf = [A.alloc(512, BF16) for _ in range(2)]
    cst = ([A.alloc(4096, F32) for _ in range(2)], [A.alloc(4096, BF16) for _ in range(1)])
    print('overlay: mixer', ov_end - ov, 'ffn', A.off - ov)
    assert A.off <= ov_end, (A.off - ov, ov_end - ov)
    A.off = max(A.off, ov_end)
    print("SBUF arena peak bytes/partition:", A.peak, "end", A.off)

    P.record("pool", lambda e: e.memset(lrT.ap[32:33, :], 1.0), [], [lrT])

    wq = []
    wstate = {"n": 0}

    wsched = []
    wstate["issued"] = 0
    wstate["freed"] = set()
    wstate["cur"] = -1
    widx = {}

    def wpump():
        while (wstate["issued"] < len(wsched) and wstate["issued"] <= wstate["cur"] + 2
               and (wstate["issued"] < NSLOT or (wstate["issued"] - NSLOT) in wstate["freed"])):
            i = wstate["issued"]
            s_, n_ = wsched[i]
            P.dma(slots[i % NSLOT][:, 0:n_], s_)
            wstate["issued"] += 1

    def wload(name, src, ncols_total):
        k = wstate["n"]
        wstate["n"] += 1
        widx[id(slots[k % NSLOT])] = k
        if P.dry:
            wsched.append((src, ncols_total))
            return slots[k % NSLOT]
        wstate["cur"] = k
        wpump()
        assert wstate["issued"] > k, ("weight slot not freed in time", name, k)
        return slots[k % NSLOT]

    def wfree(slot):
        if P.dry:
            return
        wstate["freed"].add(widx[id(slot)])
        wpump()

    def dbgout(name, t, shape, dtype=F32, l=0, ti=None):
        if dbg and (not P.dry) and name in dbg and l == 0 and ti == dbg.get("tile", 1) and name not in dbg_outs:
            o = P.dram("dbg_" + name, list(shape), dtype, kind="ExternalOutput")
            P.dma(o, t)
            dbg_outs[name] = o

    def norm_sq(xt_, ss, b):
        P.act(junk, xt_[:, b, :], AF.Square, accum_out=ss[:, b:b + 1])

    def norm_stats(xt_, ss, rs):
        P.record("pool", lambda e: e.memset(ss.ap, 0.0), [], [ss])
        for b in range(NB):
            norm_sq(xt_, ss, b)
        P.act(rs, ss, AF.Ln, scale=1.0 / D, bias=EPS)
        P.act(rs, rs, AF.Exp, scale=-0.5)

    def norm_xn(xt_, rs, b, nbuf=2):
        P.op("dve", "tensor_scalar", out=xnb[b % nbuf], in0=xt_[:, b, :], scalar1=rs[:, b:b + 1], scalar2=None, op0=ALU.mult)

    def norm_block(l, cond, which, xt_, rs, b, ahead=True):
        gi, si = (0, 1) if which == 1 else (2, 3)
        xn = xnb[b % 2] if ahead else xnb[b % 3]
        if ahead and b == 0:
            norm_xn(xt_, rs, 0)
            norm_xn(xt_, rs, 1)
        psb = bank()
        pst = T(psb.ap.bitcast(BF16), *psb.region)
        for kc in range(8):
            P.transpose(pst[:, kc * 128:(kc + 1) * 128], xn[:, kc * 128:(kc + 1) * 128], ident)
        if ahead and b + 2 < NB:
            norm_xn(xt_, rs, b + 2)
        for kc in range(8):
            o = hT[:, kc, b * 128:(b + 1) * 128]
            i = pst[:, kc * 128:(kc + 1) * 128]
            if b % 2 == 0:
                P.op("dve", "tensor_scalar", out=o, in0=i, scalar1=modc[:, l, cond, gi, kc:kc + 1],
                     scalar2=modc[:, l, cond, si, kc:kc + 1], op0=ALU.mult, op1=ALU.add)
            else:
                P.act(o, i, AF.Identity, scale=modc[:, l, cond, gi, kc:kc + 1], bias=modc[:, l, cond, si, kc:kc + 1])

    def norm_to_hT(l, cond, which):
        norm_stats(xt, ssn, rsn)
        for b in range(NB):
            norm_block(l, cond, which, xt, rsn, b)

    def projA(w, c0, ncols, nchunk, mrows=128):
        wv = V(w, w.ap[:, 0:8 * ncols].rearrange("p (k n) -> p k n", n=ncols))
        for ch in range(nchunk):
            ps = bank()
            for kc in range(8):
                P.mm(ps[0:mrows, :], wv[:, kc, c0 + ch * 128: c0 + ch * 128 + mrows], hT[:, kc, :],
                     start=(kc == 0), stop=(kc == 7))
            yield ch, ps

    def projB(w, c0, ncols, n, b):
        wv = V(w, w.ap[:, 0:8 * ncols].rearrange("p (k n) -> p k n", n=ncols))
        ps = bank()
        for kc in range(8):
            P.mm(ps[:, 0:n], hT[:, kc, b * 128:(b + 1) * 128], wv[:, kc, c0:c0 + n], start=(kc == 0), stop=(kc == 7))
        return ps

    def gate_prep(l, b, w_k, need_feat, extra=None):
        t0, t1 = b * 128, (b + 1) * 128
        ps = bank()
        P.mm(ps, lrT[:, t0:t1], wg_bf[:, l, :])
        pk = projB(w_k, 0, 512, 512, b)
        ex = extra() if extra is not None else None
        s = sp[b % 2]
        P.act(e32, ps, AF.Exp, scale=-1.0)
        P.act(s, e32, AF.Ln, bias=1.0)
        pb = bank()
        P.mm(pb[:, 0:256], triL, s[:, 0:256])
        P.mm(pb[:, 256:512], triU, s[:, 256:512])
        if need_feat:
            pf = bank()
            pfv = pv(pf, h=4)
            for h in range(4):
                P.mm(pfv[0:64, h, :], s[:, h * 64:(h + 1) * 64], triL)
                P.mm(pfv[64:128, h, :], s[:, 256 + h * 64:256 + (h + 1) * 64], triU, tile_position=(0, 64))
        else:
            pd = ps
            for h in range(4):
                P.mm(pd[64:128, h:h + 1], s[:, 256 + h * 64:256 + (h + 1) * 64], negcol, tile_position=(0, 64))
        P.act(Ek, pb, AF.Exp, scale=-1.0)
        if need_feat:
            P.act(Epos, pfv, AF.Exp)
            P.act(Eneg, pfv, AF.Exp, scale=-1.0)
            P.act(V(dec, dec.ap[0:64, b, :].unsqueeze(2)), pfv[0:64, :, 127:128], AF.Exp)
            P.act(V(dec, dec.ap[64:128, b, :].unsqueeze(2)), pfv[64:128, :, 0:1], AF.Exp)
        else:
            P.act(dec[64:128, b, :], pd[64:128, 0:4], AF.Exp)
        P.op("dve", "tensor_tensor",
             out=V(kdt[:, b, :].tile, kdt.ap[:, b, :].rearrange("p (h z d) -> p h z d", h=4, z=2)),
             in0=V(pk.tile, pk.ap.rearrange("p (h z d) -> p h z d", h=4, z=2)),
             in1=V(Ek, Ek.ap.rearrange("p (z h d) -> p h z d", z=2, h=4)), op=ALU.mult)
        return ex

    def state_update(b, rows, dst):
        r0, r1 = rows
        pd = bank()
        pdv = pv(pd, h=4)
        for h in range(4):
            P.mm(pdv[:, h, :], kdt[:, b, h * 128:(h + 1) * 128], vtok[:, b, h * 128:(h + 1) * 128])
        P.op("dve", "tensor_tensor", out=S32[r0:r1], in0=S32[r0:r1], in1=pdv[r0:r1], op=ALU.add)
        P.op("dve", "tensor_tensor", out=S32[r0:r1], in0=S32[r0:r1],
             in1=V(dec, dec.ap[r0:r1, b, :].unsqueeze(2).to_broadcast([r1 - r0, 4, 128])), op=ALU.mult)
        if dst is not None:
            P.act(dst, S32[r0:r1], AF.Copy)

    def load_x(l, t, xt_):
        src = xin if l == 0 else xmid_t[t]
        sap = (xin.ap if l == 0 else xmid.ap)[t * 512:(t + 1) * 512, :].rearrange("(b p) d -> p b d", p=128)
        P.dma(xt_, V(src, sap))

    def prepass(l, cond, t, seq_first_blocks, seq_last_blocks, gb0, group, nxt, dcast):
        w5 = wload("wi5", wi_s[l][5], 8 * 544)
        for ch, ps in projA(w5, 512, 544, 1, mrows=32):
            copy(lrT[0:32, :], ps[0:32, :])
        wfree(w5)
        w1 = wload("wi1", wi_s[l][1], 8 * 512)
        w3 = wload("wi3", wi_s[l][3], 8 * 512)
        def chain_step(b):
            gb = gb0 + b
            if gb in seq_last_blocks:
                if group == "ctx":
                    P.record("pool", lambda e: e.memset(S32.ap[64:128], 0.0), [], [S32])
                else:
                    P.dma(S32[64:128], V(sgla, sgla.ap[l, 64:128]))
            sbf = Abf[b % 2]
            sbv = V(sbf, sbf.ap.rearrange("p a b -> p (a b)")[:, 0:512].rearrange("p (h v) -> p h v", h=4))
            P.act(sbv[64:128], S32[64:128], AF.Copy)
            P.dma(V(sb_t[gb], sb_t[gb].ap.rearrange("d (h v) -> d h v", h=4)), sbv[64:128])
            state_update(b, (64, 128), None)
            if gb in seq_first_blocks and group == "ctx":
                si = seq_first_blocks.index(gb)
                P.dma(V(nst, nst.ap[si, l, 1].rearrange("h d v -> d h v")), S32[64:128])

        order = list(reversed(range(NB)))
        for i, b in enumerate(order):
            def v_proj(b=b):
                pvv = projB(w3, 0, 512, 512, b)
                copy(vtok[:, b, :], pvv)
            gate_prep(l, b, w1, False, extra=v_proj)
            if i == 0:
                nxt.load()
            if i == 2:
                nxt.stats()
            if i >= 1:
                chain_step(order[i - 1])
        wfree(w1)
        wfree(w3)
        nxt.finish()
        chain_step(order[-1])

    def mainpass(l, cond, t, seq_first_blocks, seq_last_blocks, gb0, group, last_layer, nxt, dcast):
        nseg, seglen = (2, 256) if group == "ctx" else (8, 64)
        pad = seglen + 30
        w5 = wload("wi5", wi_s[l][5], 8 * 544)
        for ch, ps in projA(w5, 0, 544, 2):
            P.act(u_g[:, ch, :], ps, AF.Gelu_apprx_tanh)
        for ch, ps in projA(w5, 512, 544, 1, mrows=32):
            copy(lrT[0:32, :], ps[0:32, :])
        for b in range(NB):
            ps = projB(w5, 256, 544, 256, b)
            P.act(gv[:, b, :], ps[:, 0:256], AF.Gelu_apprx_tanh)
        wfree(w5)
        if stop == "slab5":
            return
        P.op("pool", "tensor_tensor", out=gsq, in0=gv, in1=gv, op=ALU.mult)
        P.op("dve", "tensor_reduce", out=vst[:, 0, :], in_=gv, axis=AX.X, op=ALU.add)
        P.op("dve", "tensor_reduce", out=vst[:, 1, :], in_=gsq, axis=AX.X, op=ALU.add)
        P.op("dve", "tensor_scalar", out=vst[:, 2, :], in0=vst[:, 0, :], scalar1=1.0 / 256, scalar2=None, op0=ALU.mult)
        P.op("dve", "tensor_tensor", out=vst[:, 3, :], in0=vst[:, 2, :], in1=vst[:, 2, :], op=ALU.mult)
        P.op("dve", "scalar_tensor_tensor", out=vst[:, 4, :], in0=vst[:, 1, :], scalar=1.0 / 256, in1=vst[:, 3, :],
             op0=ALU.mult, op1=ALU.subtract)
        P.act(vst[:, 5, :], vst[:, 4, :], AF.Ln, bias=EPS)
        P.act(vst[:, 5, :], vst[:, 5, :], AF.Exp, scale=-0.5)
        for b in range(NB):
            P.op("dve", "scalar_tensor_tensor", out=vtmp, in0=gv[:, b, :], scalar=vst[:, 2, b:b + 1], in1=cG[:, l, :],
                 op0=ALU.subtract, op1=ALU.mult)
            P.op("dve", "scalar_tensor_tensor", out=vn[:, b, :], in0=vtmp, scalar=vst[:, 5, b:b + 1], in1=cB[:, l, :],
                 op0=ALU.mult, op1=ALU.add)
        if stop == "gmlpstats":
            return
        w1 = wload("wi1", wi_s[l][1], 8 * 512)
        w0 = wload("wi0", wi_s[l][0], 8 * 512)
        w1v = V(w1, w1.ap[:, 0:4096].rearrange("p (k n) -> p k n", n=512))
        w0v = V(w0, w0.ap[:, 0:4096].rearrange("p (k n) -> p k n", n=512))
        for b in range(NB):
            t0, t1 = b * 128, (b + 1) * 128

            def qk_proj(t0=t0, t1=t1):
                res = []
                for wv_ in (w0v, w1v):
                    ps = bank()
                    psv = pv(ps, h=4)
                    for h in range(4):
                        for kc in range(8):
                            P.mm(psv[:, h, :], wv_[:, kc, h * 128:(h + 1) * 128], hT[:, kc, t0:t1],
                                 start=(kc == 0), stop=(kc == 7))
                    res.append(psv)
                return res

            psq, psk = gate_prep(l, b, w1, True, extra=qk_proj)
            P.op("dve", "tensor_tensor", out=qcat[:, :, t0:t1], in0=psq, in1=Epos, op=ALU.mult)
            P.op("dve", "tensor_tensor", out=kcat[:, :, t0:t1], in0=psk, in1=Eneg, op=ALU.mult)
        wfree(w1)
        wfree(w0)
        nxt.load()
        w3 = wload("wi3", wi_s[l][3], 8 * 512)
        for b in range(NB):
            pvv = projB(w3, 0, 512, 512, b)
            copy(vtok[:, b, :], pvv)
        wfree(w3)
        for c in range(2):
            P.op("dve", "tensor_tensor", out=diag[:, c, :, :],
                 in0=V(ident, ident.ap.unsqueeze(1).to_broadcast([128, 31, 128])),
                 in1=V(wdw_t, wdw_t.ap[:, l, c, :].unsqueeze(2).to_broadcast([128, 31, 128])), op=ALU.mult)
        if stop == "glaprep":
            return
        w2 = wload("wi2", wi_s[l][2], 8 * 512)
        for ch, ps in projA(w2, 0, 512, 4):
            th = th32[ch % 2]
            P.act(th, ps, AF.Tanh, scale=0.5)
            P.op("dve", "scalar_tensor_tensor", out=sg[:, ch, :], in0=th, scalar=1.0, in1=ps, op0=ALU.add, op1=ALU.mult)
        wfree(w2)
        if stop == "g":
            return
        w4 = wload("wi4", wi_s[l][4], 8 * 512)
        xcv = V(xcp, xcp.ap[:, :, 0:nseg * pad].rearrange("p c (s w) -> p c s w", w=pad))
        for c in range(2):
            wv4 = V(w4, w4.ap[:, 0:4096].rearrange("p (k n) -> p k n", n=512))
            pa = bank()
            pbk = bank()
            for kc in range(8):
                P.mm(pa, wv4[:, kc, c * 128:(c + 1) * 128], hT[:, kc, :], start=(kc == 0), stop=(kc == 7))
            for kc in range(8):
                P.mm(pbk, wv4[:, kc, 256 + c * 128:256 + (c + 1) * 128], hT[:, kc, :], start=(kc == 0), stop=(kc == 7))
            th = th32[c % 2]
            P.act(th, pbk, AF.Tanh, scale=0.5)
            P.op("dve", "scalar_tensor_tensor", out=xcv[:, c, :, 15:15 + seglen],
                 in0=V(th, th.ap.rearrange("p (s w) -> p s w", w=seglen)), scalar=1.0,
                 in1=V(pa.tile, pa.ap.rearrange("p (s w) -> p s w", w=seglen)), op0=ALU.add, op1=ALU.mult)
        if stop == "conv_glu":
            return
        wfree(w4)
        for b in range(NB):
            P.dma(Scat[64:128, b], V(sb_t[gb0 + b], sb_t[gb0 + b].ap.rearrange("d (h v) -> d h v", h=4)))
        if stop == "gc_load":
            return
        def att_stage(b):
            t0, t1 = b * 128, (b + 1) * 128
            pa2 = bank(2)
            pav = V(pa2.tile, pa2.ap.rearrange("p (a z) -> p a z", a=8))
            for h in range(4):
                for z in range(2):
                    P.mm(pav[:, 4 * z + h, :], kcat[z * 64:(z + 1) * 64, h, t0:t1], qcat[z * 64:(z + 1) * 64, h, t0:t1])
            Ab = Abf[b % 2]
            P.op("dve", "tensor_tensor", out=Ab, in0=pav, in1=mask, op=ALU.mult)

        def norm_tail(b, po):
            t0, t1 = b * 128, (b + 1) * 128
            P.op("dve", "tensor_tensor", out=on32, in0=po, in1=lnvb[b % 2], op=ALU.mult)
            P.op("dve", "scalar_tensor_tensor", out=mixT[:, 0:4, t0:t1],
                 in0=V(on32, on32.ap.rearrange("p (h t) -> p h t", h=4)), scalar=angc[:, l:l + 1],
                 in1=sg[:, :, t0:t1], op0=ALU.mult, op1=ALU.mult)

        att_stage(0)
        pend = None
        for b in range(NB):
            gb = gb0 + b
            t0, t1 = b * 128, (b + 1) * 128
            if gb in seq_first_blocks:
                if group == "ctx":
                    P.record("pool", lambda e: e.memset(S32.ap[0:64], 0.0), [], [S32])
                else:
                    P.dma(S32[0:64], V(sgla, sgla.ap[l, 0:64]))
            if b == 0 or gb in seq_first_blocks:
                P.act(Scat[0:64, b], S32[0:64], AF.Copy)
            state_update(b, (0, 64), Scat[0:64, b + 1] if b + 1 < NB else None)
            if gb in seq_last_blocks and group == "ctx":
                si = seq_last_blocks.index(gb)
                P.dma(V(nst, nst.ap[si, l, 0].rearrange("h d v -> d h v")), S32[0:64])
            if b + 1 < NB:
                att_stage(b + 1)
            if pend is not None:
                norm_tail(*pend)
            Ab = Abf[b % 2]
            po = bank()
            pov = pv(po, h=4)
            for h in range(4):
                P.mm(pov[:, h, :], vtok[:, b, h * 128:(h + 1) * 128], Ab[:, h, :], start=True, stop=False)
                P.mm(pov[:, h, :], vtok[:, b, h * 128:(h + 1) * 128], Ab[:, 4 + h, :], start=False, stop=False)
                P.mm(pov[:, h, :], Scat[:, b, h, :], qcat[:, h, t0:t1], start=False, stop=True)
            P.act(sqb, po, AF.Square)
            pss = bank()
            P.mm(pss, ones_bf, sqb)
            P.act(lnvb[b % 2], pss, AF.Ln, scale=1.0 / 128, bias=EPS)
            P.act(lnvb[b % 2], lnvb[b % 2], AF.Exp, scale=-0.5)
            pend = (b, po)
        norm_tail(*pend)
        if stop == "glacore":
            return
        flat = nseg * pad - 30
        n2 = flat - 512
        for c in range(2):
            py = bank(2)
            for j in range(31):
                P.mm(py[:, 0:512], diag[:, c, j, :], xcp[:, c, j:j + 512], start=(j == 0), stop=(j == 30))
            for j in range(31):
                P.mm(py[:, 512:512 + n2], diag[:, c, j, :], xcp[:, c, 512 + j:512 + j + n2], start=(j == 0), stop=(j == 30))
            pyv = V(py.tile, py.ap[:, 0:nseg * pad].rearrange("p (s w) -> p s w", w=pad))[:, :, 0:seglen]
            P.act(V(y32, y32.ap[:, c, :].rearrange("p (s w) -> p s w", w=seglen)), pyv, AF.Copy)
            P.act(V(ysq, ysq.ap[:, c, :].rearrange("p (s w) -> p s w", w=seglen)), pyv, AF.Square)
        p1 = bank()
        p2 = bank()
        for c in range(2):
            P.mm(p1, ones32, y32[:, c, :], start=(c == 0), stop=(c == 1))
        for c in range(2):
            P.mm(p2, ones32, ysq[:, c, :], start=(c == 0), stop=(c == 1))
        P.op("dve", "tensor_scalar", out=m32, in0=p1, scalar1=1.0 / 256, scalar2=None, op0=ALU.mult)
        P.op("pool", "tensor_tensor", out=msq, in0=m32, in1=m32, op=ALU.mult)
        P.op("dve", "scalar_tensor_tensor", out=var, in0=p2, scalar=1.0 / 256, in1=msq, op0=ALU.mult, op1=ALU.subtract)
        P.act(var, var, AF.Ln, bias=EPS)
        P.act(var, var, AF.Exp, scale=-0.5)
        for c in range(2):
            P.op("pool", "tensor_tensor", out=yc, in0=y32[:, c, :], in1=m32, op=ALU.subtract)
            P.op("dve", "tensor_tensor", out=yc, in0=yc, in1=var, op=ALU.mult)
            P.op("dve", "tensor_scalar", out=z32, in0=yc, scalar1=blg[:, l, c:c + 1], scalar2=blb[:, l, c:c + 1],
                 op0=ALU.mult, op1=ALU.add)
            th = th32[c % 2]
            P.act(th, z32, AF.Tanh, scale=0.5)
            P.op("dve", "scalar_tensor_tensor", out=mixT[:, 4 + c, :], in0=th, scalar=1.0, in1=z32, op0=ALU.add, op1=ALU.mult)
        if stop == "conv":
            return
        for b in range(NB):
            t0, t1 = b * 128, (b + 1) * 128
            pg = bank()
            pgv = pv(pg, a=2)
            for h in range(4):
                o = pgv[(h % 2) * 64:(h % 2) * 64 + 64, h // 2, 0:128]
                tp = {"tile_position": (0, 64)} if h % 2 == 1 else {}
                P.mm(o, vn[:, b, h * 64:(h + 1) * 64], wst_bf[:, l, h, :], start=True, stop=False, **tp)
                P.mm(o, ones_bf[0:1, 0:64], bs_hi[:, l, h * 128:(h + 1) * 128], start=False, stop=False, **tp)
                P.mm(o, ones_bf[0:1, 0:64], bs_lo[:, l, h * 128:(h + 1) * 128], start=False, stop=True, **tp)
            P.op("dve", "scalar_tensor_tensor", out=mixC[:, :, t0:t1], in0=pgv[:, :, 0:128], scalar=2.0,
                 in1=u_g[:, :, t0:t1], op0=ALU.mult, op1=ALU.mult)
        if stop == "gmlp":
            return
        wo = [wload("wo0", wo_s[l][0], 4096), wload("wo1", wo_s[l][1], 4096)]
        P.record("pool", lambda e: e.memset(ssn.ap, 0.0), [], [ssn])
        kk = [0]

        def wout_block(b):
            for nh in range(2):
                ps = bank()
                for kc in range(8):
                    wv_ = V(wo[kc // 4], wo[kc // 4].ap[:, 0:4096].rearrange("p (k n) -> p k n", n=D))
                    mx = mixT[:, kc, b * 128:(b + 1) * 128] if kc < 6 else mixC[:, kc - 6, b * 128:(b + 1) * 128]
                    P.mm(ps, mx, wv_[:, kc % 4, nh * 512:(nh + 1) * 512], start=(kc == 0), stop=(kc == 7))
                tt = tmp32[kk[0] % 2]
                kk[0] += 1
                P.op("dve", "tensor_tensor", out=tt, in0=ps, in1=gate_t[0][:, nh * 512:(nh + 1) * 512], op=ALU.mult)
                P.op("dve", "tensor_tensor", out=xt[:, b, nh * 512:(nh + 1) * 512],
                     in0=xt[:, b, nh * 512:(nh + 1) * 512], in1=tt, op=ALU.add)
            norm_sq(xt, ssn, b)
            P.act(rsn[:, b:b + 1], ssn[:, b:b + 1], AF.Ln, scale=1.0 / D, bias=EPS)
            P.act(rsn[:, b:b + 1], rsn[:, b:b + 1], AF.Exp, scale=-0.5)
            norm_xn(xt, rsn, b, nbuf=3)

        wout_block(0)
        wout_block(1)
        for b in range(NB):
            if b + 2 < NB:
                wout_block(b + 2)
            if b + 2 == NB - 1:
                wfree(wo[0])
                wfree(wo[1])
            norm_block(l, cond, 2, xt, rsn, b, ahead=False)
        if stop == "wout":
            return
        dbgout("x1", xt, [128, 4, 1024], F32, l, t)
        nxt.stats()
        for s in range(8):
            dcast(1 if s % 2 == 0 else 0)
            wf = wload("f1", f1_s[l][s], 4096)
            for ch, ps in projA(wf, 0, 512, 4):
                r = rbf[ch % 2]
                P.act(r, ps, AF.Relu)
                P.op("pool" if ch % 2 == 0 else "dve", "tensor_tensor", out=hid[:, s * 4 + ch, :], in0=r, in1=r, op=ALU.mult)
            wfree(wf)
        dcast(-1)
        blo[0] = 4
        for nh in range(2):
            accs = [P.psum_bank(b) for b in range(4)]
            for ks in range(8):
                if nh == 0 and ks % 2 == 1:
                    nxt.block(ks // 2)
                wf = wload("f2", f2_s[l][nh][ks], 2048)
                wv_ = V(wf, wf.ap[:, 0:2048].rearrange("p (k n) -> p k n", n=512))
                for b in range(NB):
                    for k4 in range(4):
                        kc = ks * 4 + k4
                        P.mm(accs[b], hid[:, kc, b * 128:(b + 1) * 128], wv_[:, k4, :], start=(kc == 0), stop=(kc == 31))
                wfree(wf)
            for b in range(NB):
                tt = tmp32[b % 2]
                P.op("dve", "tensor_tensor", out=tt, in0=accs[b], in1=gate_t[1][:, nh * 512:(nh + 1) * 512], op=ALU.mult)
                P.op("pool", "tensor_tensor", out=xt[:, b, nh * 512:(nh + 1) * 512], in0=xt[:, b, nh * 512:(nh + 1) * 512],
                     in1=tt, op=ALU.add)
        blo[0] = 0
        rr[0] = 4
        if stop == "ffn":
            return
        if not last_layer:
            P.dma(V(xmid_t[t], xmid.ap[t * 512:(t + 1) * 512, :].rearrange("(b p) d -> p b d", p=128)), xt)
        else:
            P.record("pool", lambda e: e.memset(ssn.ap, 0.0), [], [ssn])
            for b in range(NB):
                P.act(junk, xt[:, b, :], AF.Square, accum_out=ssn[:, b:b + 1])
            P.act(rsn, ssn, AF.Ln, scale=1.0 / D, bias=EPS)
            P.act(rsn, rsn, AF.Exp, scale=-0.5)
            for b in range(NB):
                P.op("dve", "scalar_tensor_tensor", out=xt[:, b, :], in0=xt[:, b, :], scalar=rsn[:, b:b + 1], in1=fgt,
                     op0=ALU.mult, op1=ALU.mult)
            P.dma(V(y_t[t], y.ap[t * 512:(t + 1) * 512, :].rearrange("(b p) d -> p b d", p=128)), xt)

    def emit_layers():
        nonlocal xt
        jobs = []
        for l in range(nlayers):
            last = (l == nlayers - 1)
            for group in groups:
                cond = 0 if group == "ctx" else 1
                if group == "ctx":
                    tiles = [0]
                    firsts, lasts = [0, 2], [1, 3]
                else:
                    tiles = list(range(1, 1 + nlat))
                    firsts, lasts = [0], [4 * nlat - 1]
                if do_pre:
                    for ti in reversed(range(len(tiles))):
                        jobs.append(("pre", l, cond, tiles[ti], firsts, lasts, ti * 4, group, last))
                if do_main:
                    for ti in range(len(tiles)):
                        jobs.append(("main", l, cond, tiles[ti], firsts, lasts, ti * 4, group, last))

        class Nxt:
            def __init__(self, j):
                self.j = j
                self.ok = j < len(jobs)
                self.st = 0
                self.nb = 0
                if self.ok:
                    self.l, self.cond, self.t = jobs[j][1:4]
                    self.xt = xtb[j % 2]

            def load(self):
                if self.ok and self.st == 0:
                    load_x(self.l, self.t, self.xt)
                    self.st = 1

            def stats(self):
                self.load()
                if self.ok and self.st == 1:
                    norm_stats(self.xt, ssn2, rsn2)
                    self.st = 2

            def block(self, b):
                self.stats()
                if self.ok and self.nb == b:
                    norm_block(self.l, self.cond, 1, self.xt, rsn2, b)
                    self.nb = b + 1

            def finish(self):
                for b in range(NB):
                    self.block(b)

        deferred = []
        if defer_cast and nlayers > 1:
            for l in range(1, nlayers):
                deferred += layer_cast_units(l)
        main_l0 = [j for j, jb in enumerate(jobs) if jb[0] == "main" and jb[1] == 0]

        P.record("pool", lambda e: e.memset(xcp.ap, 0.0), [], [xcp])
        Nxt(0).finish()
        cur_key = None
        for j, (kind, l, cond, t, firsts, lasts, gb0, group, last) in enumerate(jobs):
            if (l, group) != cur_key:
                cur_key = (l, group)
                P.dma(gate_t[0], gates_s[l][cond][0])
                P.dma(gate_t[1], gates_s[l][cond][1])
                P.record("pool", lambda e: e.memset(xcp.ap, 0.0), [], [xcp])
            if l >= 1:
                assert not deferred
            xt = xtb[j % 2]
            nxt = Nxt(j + 1)
            is_last_l0 = bool(main_l0) and j == main_l0[-1]

            def dcast(n, is_last_l0=is_last_l0, l=l):
                if l != 0:
                    return
                if n < 0:
                    n = len(deferred) if is_last_l0 else 0
                for _ in range(n):
                    if deferred:
                        cast_unit(deferred.pop(0), cst)

            if kind == "pre":
                prepass(l, cond, t, firsts, lasts, gb0, group, nxt, dcast)
            else:
                mainpass(l, cond, t, firsts, lasts, gb0, group, last, nxt, dcast)
            nxt.finish()

    P.dry = True
    emit_layers()
    P.dry = False
    rr[0] = 0
    evc[0] = 0
    wstate["n"] = 0
    wstate["issued"] = 0
    wstate["freed"] = set()
    wstate["cur"] = -1
    emit_layers()
    P.finalize()
    print("ops per engine (n, waits):", P.stats)
    return nc, dbg_outs


_CACHE = {}


def _prep_inputs(inp):
    f = np.float32
    g = {k: np.asarray(v) for k, v in inp.items()}
    w_in = g["w_in"]
    qi = np.arange(0, 256)
    ki = np.arange(256, 512)
    vi = np.arange(512, 1024)
    gi = np.arange(1024, 1536)
    lri = np.arange(1536, 1568)
    gai = np.arange(1568, 1824)
    gbi = np.arange(1824, 2080)
    ui = np.arange(2080, 2336)
    vgi = np.arange(2336, 2592)
    qdup = np.concatenate([np.concatenate([qi[h * 64:(h + 1) * 64]] * 2) for h in range(4)])
    kdup = np.concatenate([np.concatenate([ki[h * 64:(h + 1) * 64]] * 2) for h in range(4)])
    idx = np.concatenate([qdup, kdup, gi, vi, gai, gbi, ui, vgi, lri])
    assert idx.size == NCOL

    def kchunk(w):
        Lw, K, N = w.shape
        return np.ascontiguousarray(w.reshape(Lw, K // 128, 128, N).transpose(0, 2, 1, 3))

    shared = {
        "w_in_r": kchunk(w_in[:, :, idx]),
        "w_out_r": kchunk(g["w_out"]),
        "w_ff1_r": kchunk(g["w_ff1"]),
        "w_ff2_r": kchunk(g["w_ff2"]),
        "w_mod_r": kchunk(g["w_mod"]),
        "b_mod": np.ascontiguousarray(g["b_mod"]),
    }
    wag = g["w_a_gate"]
    bag = g["b_a_gate"]
    wgm = np.zeros((L, 33, 2, 4, 64), f)
    for z in range(2):
        wgm[:, z * 16:(z + 1) * 16, z, :, :] = wag[:, z].reshape(L, 16, 4, 64)
        wgm[:, 32, z, :, :] = bag[:, z].reshape(L, 4, 64)
    shared["wg"] = wgm.reshape(L, 33, 512)
    shared["n1g"] = np.ascontiguousarray(g["norm1_g"].reshape(L, 8, 128).transpose(0, 2, 1))
    shared["n2g"] = np.ascontiguousarray(g["norm2_g"].reshape(L, 8, 128).transpose(0, 2, 1))
    shared["ang"] = np.ascontiguousarray(g["a_norm_g"].T)
    shared["wdw"] = np.ascontiguousarray(g["w_dw"].reshape(L, 31, 2, 128).transpose(0, 3, 2, 1))
    shared["blng"] = np.ascontiguousarray(g["b_ln_g"].reshape(L, 2, 128).transpose(0, 2, 1))
    shared["blnb"] = np.ascontiguousarray(g["b_ln_b"].reshape(L, 2, 128).transpose(0, 2, 1))
    shared["clng"] = np.ascontiguousarray(g["c_ln_g"])
    shared["clnb"] = np.ascontiguousarray(g["c_ln_b"])
    shared["wst"] = np.ascontiguousarray(g["w_s"].transpose(0, 3, 1, 2))
    shared["bs"] = np.ascontiguousarray(g["b_s"].reshape(L, 1, 512))
    shared["fg"] = np.ascontiguousarray(g["final_g"])
    shared = {k: np.ascontiguousarray(v.astype(f)) for k, v in shared.items()}
    maps = []
    for i in range(8):
        m = dict(shared)
        m["xin"] = np.ascontiguousarray(np.concatenate(
            [g["x_prompt"][2 * i], g["x_prompt"][2 * i + 1], g["x_sample"][i]], axis=0).astype(f))
        cv = np.stack([g["c_ctx"], g["c"][i]], axis=-1).astype(f)
        m["cvec"] = np.ascontiguousarray(cv.reshape(8, 128, 2).transpose(1, 0, 2))
        sg_ = g["state_gla"][i]
        m["sgla"] = np.ascontiguousarray(sg_.transpose(0, 1, 3, 2, 4).reshape(L, 128, 4, 128).astype(f))
        maps.append(m)
    return maps


def kernel(**inputs):
    if "nc" not in _CACHE:
        _CACHE["nc"] = build()[0]
    nc = _CACHE["nc"]
    maps = _prep_inputs(inputs)
    res = run_bass_kernel_spmd(nc, maps, core_ids=list(range(8)))
    yp = np.zeros((16, 256, D), np.float32)
    ys = np.zeros((8, 4096, D), np.float32)
    ns = np.zeros((16, L, 2, 4, 64, 128), np.float32)
    for i in range(8):
        r = res.results[i]
        yy = np.asarray(r["y"])
        yp[2 * i] = yy[0:256]
        yp[2 * i + 1] = yy[256:512]
        ys[i] = yy[512:]
        n_ = np.asarray(r["nst"])
        ns[2 * i] = n_[0]
        ns[2 * i + 1] = n_[1]
    return yp, ys, ns
```
